# Optimizing a Trainium2 kernel written in Bass

```python
import jax, jax.numpy as jnp
from jax import lax
import numpy as np

D_MODEL = 1024
BATCH = 16
SEQ = 4096
DEPTH = 4
DEC_BATCH = 8
DEC_SEQ = 8192
PAST_LEN = 128

HEAD_DIM = 64
GRID_W = 64
A_HEADS = 8
A_KV_HEADS = 2
A_ROPE_THETA = 10000.0
A_Q_BLOCK = 128
B_HEADS = 8
B_PATTERNS = ((128, 1), (512, 4), (2048, 16))
B_ROPE_THETA = 500000.0
B_ROPE_DIMS = HEAD_DIM // 4
B_Q_BLOCK = 64
C_HEADS = 12
C_WIN_H = 8
C_WIN_W = 16
D_GROUPS = 4
D_GROUP_DIM = 64

A_WIDTH = A_HEADS * HEAD_DIM
A_KV_WIDTH = A_KV_HEADS * HEAD_DIM
B_WIDTH = B_HEADS * HEAD_DIM
C_WIDTH = C_HEADS * HEAD_DIM
D_WIDTH = D_GROUPS * D_GROUP_DIM
AB_SPLITS = (A_WIDTH, A_KV_WIDTH, A_KV_WIDTH, A_WIDTH, B_WIDTH, B_WIDTH, B_WIDTH, B_WIDTH)
AB_IN = sum(AB_SPLITS)
AB_OUT = A_WIDTH + B_WIDTH
CD_SPLITS = (C_WIDTH, C_WIDTH, C_WIDTH, C_WIDTH, D_WIDTH, D_WIDTH)
CD_IN = sum(CD_SPLITS)
CD_OUT = C_WIDTH + D_WIDTH
N_EVEN = (DEPTH + 1) // 2
N_ODD = DEPTH // 2
EPS = 1e-6
NEG_INF = -1e30

kernel_name = 'hybrid_bidir_encoder_two_groups'


def rmsnorm(x, g):
    xf = x.astype(jnp.float32)
    y = xf * lax.rsqrt(jnp.mean(xf * xf, axis=-1, keepdims=True) + EPS)
    return (y * g.astype(jnp.float32)).astype(x.dtype)


def split_cols(p, sizes):
    idx = [sum(sizes[:i + 1]) for i in range(len(sizes) - 1)]
    return jnp.split(p, idx, axis=-1)


def rope_tables(pos, dim, theta):
    inv = theta ** (-jnp.arange(0, dim, 2, dtype=jnp.float32) / dim)
    ang = pos[:, None] * inv[None, :]
    return jnp.cos(ang), jnp.sin(ang)


def apply_rope(x, cos, sin):
    half = x.shape[-1] // 2
    x1 = x[..., :half].astype(jnp.float32)
    x2 = x[..., half:].astype(jnp.float32)
    c = cos[None, :, None, :]
    s = sin[None, :, None, :]
    return jnp.concatenate([x1 * c - x2 * s, x1 * s + x2 * c], axis=-1).astype(x.dtype)


def axial_rope(x, cos_r, sin_r, cos_c, sin_c):
    half = x.shape[-1] // 2
    return jnp.concatenate([apply_rope(x[..., :half], cos_r, sin_r),
                            apply_rope(x[..., half:], cos_c, sin_c)], axis=-1)


def partial_rope(x, cos, sin):
    return jnp.concatenate([apply_rope(x[..., :B_ROPE_DIMS], cos, sin), x[..., B_ROPE_DIMS:]], axis=-1)


def position_tables(s):
    t = jnp.arange(s)
    row = (t // GRID_W).astype(jnp.float32)
    col = (t % GRID_W).astype(jnp.float32)
    cos_r, sin_r = rope_tables(row, HEAD_DIM // 2, A_ROPE_THETA)
    cos_c, sin_c = rope_tables(col, HEAD_DIM // 2, A_ROPE_THETA)
    cos_b, sin_b = rope_tables(t.astype(jnp.float32), B_ROPE_DIMS, B_ROPE_THETA)
    return (cos_r, sin_r, cos_c, sin_c, cos_b, sin_b)


def adaln(c, ada_w, ada_b):
    mod = jnp.einsum('bd,de->be', jax.nn.silu(c), ada_w) + ada_b
    shift, scale, gate = jnp.split(mod, 3, axis=-1)
    return shift, scale, gate


def dense_gqa_attention(q, k, v):
    b, s, hq, hd = q.shape
    hkv = k.shape[2]
    g = hq // hkv
    nqb = s // A_Q_BLOCK
    qb = q.reshape(b, nqb, A_Q_BLOCK, hkv, g, hd).transpose(1, 0, 2, 3, 4, 5)
    scale = hd ** -0.5

    def block(qi):
        sc = jnp.einsum('bqkgd,bskd->bkgqs', qi, k).astype(jnp.float32) * scale
        p = jax.nn.softmax(sc, axis=-1).astype(v.dtype)
        return jnp.einsum('bkgqs,bskd->bqkgd', p, v)

    o = lax.map(block, qb)
    return o.transpose(1, 0, 2, 3, 4, 5).reshape(b, s, hq, hd)


def sliding_window_attention(q, k, v, radius):
    n, l, h, hd = q.shape
    qbs = B_Q_BLOCK
    nb = -(-l // qbs)
    lp = nb * qbs
    qp = jnp.pad(q, ((0, 0), (0, lp - l), (0, 0), (0, 0)))
    pad_k = ((0, 0), (radius, lp - l + radius), (0, 0), (0, 0))
    kp = jnp.pad(k, pad_k)
    vp = jnp.pad(v, pad_k)
    span = qbs + 2 * radius
    kidx = (jnp.arange(nb) * qbs)[:, None] + jnp.arange(span)[None, :]
    kb = kp[:, kidx]
    vb = vp[:, kidx]
    qb = qp.reshape(n, nb, qbs, h, hd)
    qpos = (jnp.arange(nb) * qbs)[:, None] + jnp.arange(qbs)[None, :]
    kpos = kidx - radius
    valid = ((jnp.abs(qpos[:, :, None] - kpos[:, None, :]) <= radius)
             & (kpos[:, None, :] >= 0) & (kpos[:, None, :] < l))
    sc = jnp.einsum('nbqhd,nbkhd->nbhqk', qb, kb).astype(jnp.float32) * (hd ** -0.5)
    sc = jnp.where(valid[None, :, None], sc, NEG_INF)
    m = jnp.max(sc, axis=-1, keepdims=True)
    e = jnp.exp(sc - m)
    den = jnp.sum(e, axis=-1, keepdims=True)
    o = jnp.einsum('nbhqk,nbkhd->nbqhd', (e / den).astype(v.dtype), vb)
    lse = (m + jnp.log(den))[..., 0]
    o = o.reshape(n, lp, h, hd)[:, :l]
    lse = lse.transpose(0, 1, 3, 2).reshape(n, lp, h)[:, :l]
    return o, lse


def to_residues(t, dil):
    b, s = t.shape[:2]
    rest = t.shape[2:]
    t = t.reshape((b, s // dil, dil) + rest)
    t = jnp.moveaxis(t, 2, 1)
    return t.reshape((b * dil, s // dil) + rest)


def from_residues(t, b, dil):
    rest = t.shape[2:]
    l = t.shape[1]
    t = t.reshape((b, dil, l) + rest)
    t = jnp.moveaxis(t, 1, 2)
    return t.reshape((b, l * dil) + rest)


def dilated_mixture_attention(q, k, v):
    b = q.shape[0]
    outs, lses = [], []
    for window, dil in B_PATTERNS:
        radius = window // (2 * dil)
        o, lse = sliding_window_attention(to_residues(q, dil), to_residues(k, dil), to_residues(v, dil), radius)
        outs.append(from_residues(o, b, dil))
        lses.append(from_residues(lse, b, dil))
    w = jax.nn.softmax(jnp.stack(lses, axis=0), axis=0)
    return jnp.einsum('pbsh,pbshd->bshd', w.astype(q.dtype), jnp.stack(outs, axis=0))


def neighbourhood_attention(q, k, v, rpb):
    b, s, h, hd = q.shape
    rows = s // GRID_W
    kh = min(C_WIN_H, rows)
    kw = C_WIN_W
    kg = k.reshape(b, rows, GRID_W, h, hd)
    vg = v.reshape(b, rows, GRID_W, h, hd)
    qg = q.reshape(b, rows, GRID_W, h, hd).transpose(1, 0, 2, 3, 4)
    cols = jnp.arange(GRID_W)
    col_start = jnp.clip(cols - kw // 2, 0, GRID_W - kw)
    col_mask = (cols[None, :] >= col_start[:, None]) & (cols[None, :] < col_start[:, None] + kw)
    col_idx = jnp.clip(cols[None, :] - cols[:, None] + kw - 1, 0, 2 * kw - 2)
    scale = hd ** -0.5

    def row_block(args):
        r, qr = args
        r0 = jnp.clip(r - kh // 2, 0, rows - kh)
        kr = lax.dynamic_slice_in_dim(kg, r0, kh, axis=1)
        vr = lax.dynamic_slice_in_dim(vg, r0, kh, axis=1)
        row_idx = r0 + jnp.arange(kh) - r + C_WIN_H - 1
        bias = rpb[:, row_idx[None, :, None], col_idx[:, None, :]]
        sc = jnp.einsum('bchd,bjkhd->bhcjk', qr, kr).astype(jnp.float32) * scale + bias[None].astype(jnp.float32)
        sc = jnp.where(col_mask[None, None, :, None, :], sc, NEG_INF)
        p = jax.nn.softmax(sc.reshape(b, h, GRID_W, kh * GRID_W), axis=-1).reshape(sc.shape)
        return jnp.einsum('bhcjk,bjkhd->bchd', p.astype(v.dtype), vr)

    o = lax.map(row_block, (jnp.arange(rows), qg))
    return o.transpose(1, 0, 2, 3, 4).reshape(b, s, h, hd)


def fourier_mixing(u, lin):
    b, s, _ = u.shape
    ug = u.reshape(b, s, D_GROUPS, D_GROUP_DIM).astype(jnp.float32)
    f = jnp.fft.fft2(ug, axes=(1, 3), norm='ortho').real
    return jnp.einsum('bsc,ce->bse', f.reshape(b, s, D_WIDTH).astype(u.dtype), lin)


def even_layer(x, c, pre_g, post_g, ada_w, ada_b, w_in, w_out, qn, kn, tabs):
    b, s, _ = x.shape
    cos_r, sin_r, cos_c, sin_c, cos_b, sin_b = tabs
    shift, scale, gate = adaln(c, ada_w, ada_b)
    h = rmsnorm(x, pre_g) * (1 + scale[:, None, :]) + shift[:, None, :]
    p = jnp.einsum('bsd,de->bse', h, w_in)
    qa, ka, va, ga, qb, kb, vb, gb = split_cols(p, AB_SPLITS)

    def heads(t, n):
        return t.reshape(b, s, n, HEAD_DIM)

    qa = axial_rope(rmsnorm(heads(qa, A_HEADS), qn), cos_r, sin_r, cos_c, sin_c)
    ka = axial_rope(rmsnorm(heads(ka, A_KV_HEADS), kn), cos_r, sin_r, cos_c, sin_c)
    oa = dense_gqa_attention(qa, ka, heads(va, A_KV_HEADS)).reshape(b, s, A_WIDTH)
    qb = partial_rope(heads(qb, B_HEADS), cos_b, sin_b)
    kb = partial_rope(heads(kb, B_HEADS), cos_b, sin_b)
    ob = dilated_mixture_attention(qb, kb, heads(vb, B_HEADS)).reshape(b, s, B_WIDTH)

    mixed = jnp.concatenate([oa * jax.nn.silu(ga), ob * jax.nn.silu(gb)], axis=-1)
    y = rmsnorm(jnp.einsum('bse,ed->bsd', mixed, w_out), post_g)
    return x + gate[:, None, :] * y


def odd_layer(x, c, pre_g, post_g, ada_w, ada_b, w_in, w_out, rpb, lin):
    b, s, _ = x.shape
    shift, scale, gate = adaln(c, ada_w, ada_b)
    h = rmsnorm(x, pre_g) * (1 + scale[:, None, :]) + shift[:, None, :]
    p = jnp.einsum('bsd,de->bse', h, w_in)
    qc, kc, vc, gc, ud, gd = split_cols(p, CD_SPLITS)

    def heads(t):
        return t.reshape(b, s, C_HEADS, HEAD_DIM)

    oc = neighbourhood_attention(heads(qc), heads(kc), heads(vc), rpb).reshape(b, s, C_WIDTH)
    od = fourier_mixing(ud, lin)
    mixed = jnp.concatenate([oc * jax.nn.silu(gc), od * jax.nn.silu(gd)], axis=-1)
    y = rmsnorm(jnp.einsum('bse,ed->bsd', mixed, w_out), post_g)
    return x + gate[:, None, :] * y


def trunk(x, c, pre_g, post_g, ada_w, ada_b, w_in_ab, w_out_ab, qn_a, kn_a, w_in_cd, w_out_cd, rpb_c, lin_d):
    tabs = position_tables(x.shape[1])
    for i in range(DEPTH):
        j = i // 2
        if i % 2 == 0:
            x = even_layer(x, c, pre_g[i], post_g[i], ada_w[i], ada_b[i],
                           w_in_ab[j], w_out_ab[j], qn_a[j], kn_a[j], tabs)
        else:
            x = odd_layer(x, c, pre_g[i], post_g[i], ada_w[i], ada_b[i],
                          w_in_cd[j], w_out_cd[j], rpb_c[j], lin_d[j])
    return x


def setup_inputs(seed: int = 0) -> dict:
    key = jax.random.key(seed)
    ks = jax.random.split(key, 16)
    f32 = jnp.float32
    d = D_MODEL
    return {
        'x_prompt': jax.random.normal(ks[0], (BATCH, SEQ, d), f32),
        'x_sample': jax.random.normal(ks[1], (DEC_BATCH, DEC_SEQ, d), f32),
        'c_prompt': jax.random.normal(ks[2], (BATCH, d), f32),
        'c_sample': jax.random.normal(ks[3], (DEC_BATCH, d), f32),
        'pre_g': 1.0 + 0.1 * jax.random.normal(ks[4], (DEPTH, d), f32),
        'post_g': 1.0 + 0.1 * jax.random.normal(ks[5], (DEPTH, d), f32),
        'ada_w': 0.3 * d ** -0.5 * jax.random.normal(ks[6], (DEPTH, d, 3 * d), f32),
        'ada_b': 0.01 * jax.random.normal(ks[7], (DEPTH, 3 * d), f32),
        'w_in_ab': d ** -0.5 * jax.random.normal(ks[8], (N_EVEN, d, AB_IN), f32),
        'w_out_ab': AB_OUT ** -0.5 * jax.random.normal(ks[9], (N_EVEN, AB_OUT, d), f32),
        'qn_a': 1.0 + 0.1 * jax.random.normal(ks[10], (N_EVEN, HEAD_DIM), f32),
        'kn_a': 1.0 + 0.1 * jax.random.normal(ks[11], (N_EVEN, HEAD_DIM), f32),
        'w_in_cd': d ** -0.5 * jax.random.normal(ks[12], (N_ODD, d, CD_IN), f32),
        'w_out_cd': CD_OUT ** -0.5 * jax.random.normal(ks[13], (N_ODD, CD_OUT, d), f32),
        'rpb_c': 0.1 * jax.random.normal(ks[14], (N_ODD, C_HEADS, 2 * C_WIN_H - 1, 2 * C_WIN_W - 1), f32),
        'lin_d': D_WIDTH ** -0.5 * jax.random.normal(ks[15], (N_ODD, D_WIDTH, D_WIDTH), f32),
    }


def reference(x_prompt, x_sample, c_prompt, c_sample, pre_g, post_g, ada_w, ada_b,
              w_in_ab, w_out_ab, qn_a, kn_a, w_in_cd, w_out_cd, rpb_c, lin_d):
    y_prompt = trunk(x_prompt, c_prompt, pre_g, post_g, ada_w, ada_b, w_in_ab, w_out_ab,
                     qn_a, kn_a, w_in_cd, w_out_cd, rpb_c, lin_d)
    y_sample = trunk(x_sample, c_sample, pre_g, post_g, ada_w, ada_b, w_in_ab, w_out_ab,
                     qn_a, kn_a, w_in_cd, w_out_cd, rpb_c, lin_d)
    return (y_prompt, y_sample)
```

```python
import math
from contextlib import ExitStack

import numpy as np
import ml_dtypes
import concourse.bass as bass
import concourse.mybir as mybir
from concourse.bass_utils import run_bass_kernel_spmd

F32 = mybir.dt.float32
BF16 = mybir.dt.bfloat16
AF = mybir.ActivationFunctionType
ALU = mybir.AluOpType
AX = mybir.AxisListType
NPBF = ml_dtypes.bfloat16

ENGS = ['pe', 'act', 'dve', 'pool', 'sp']
D = 1024
EPS = 1e-6
NEG = -30000.0


class Res:
    __slots__ = ('name', 'writers', 'readers', 'multi', 'sem', 'semval')

    def __init__(self, name, multi=False):
        self.name = name
        self.writers = {}
        self.readers = {}
        self.multi = multi
        self.sem = {}
        self.semval = {}


class Op:
    __slots__ = ('eng', 'fn', 'deps', 'needsig', 'cnt', 'isdma', 'sem', 'semval', 'key', 'idx')


class Prog:
    def __init__(self, nc, stack, same_engine_sync=True):
        self.nc = nc
        self.stack = stack
        self.ops = {e: [] for e in ENGS}
        self.esem = {e: stack.enter_context(nc.semaphore('es_' + e)) for e in ENGS}
        self.same = same_engine_sync
        self.nosame = ('act', 'dve')
        self.nsem = 0
        self.free_sems = {}
        self.phase_res = []
        self.all_sems = []
        self.last = {}
        self.pending = {}
        self.n = 0
        self.nwaits = {}

    def _dma_sem(self, r, eng):
        if eng not in r.sem:
            fl = self.free_sems.setdefault(eng, [])
            if fl:
                r.sem[eng], r.semval[eng] = fl.pop()
            else:
                r.sem[eng] = self.stack.enter_context(self.nc.semaphore('ds%d' % self.nsem))
                r.semval[eng] = 0
                self.nsem += 1
            self.phase_res.append((r, eng))
        return r.sem[eng]

    def op(self, eng, fn, reads=(), writes=(), dma=None, extra=()):
        o = Op()
        o.eng = eng
        o.fn = fn
        o.needsig = False
        o.cnt = 0
        o.isdma = dma is not None
        o.idx = self.n
        self.n += 1
        deps = {}

        def add(d):
            if d is o:
                return
            if (not d.isdma) and d.eng == eng and (eng == 'pe' or (not self.same and eng in self.nosame)):
                return
            k = d.key
            if k not in deps or deps[k].idx < d.idx:
                deps[k] = d

        for d in extra:
            add(d)
        for r in reads:
            if r is None:
                continue
            for d in r.writers.values():
                add(d)
        for w in writes:
            if w is None:
                continue
            for d in w.readers.values():
                add(d)
            if not w.multi and not w.readers:
                for d in w.writers.values():
                    add(d)
        if o.isdma:
            o.sem = self._dma_sem(dma, eng)
            dma.semval[eng] += 16
            o.semval = dma.semval[eng]
            o.key = (eng, id(o.sem))
            self.pending[id(o.sem)] = o
        else:
            o.key = eng
            self.last[eng] = o
        for r in reads:
            if r is None:
                continue
            r.readers[o.key] = o
        for w in writes:
            if w is None:
                continue
            if w.multi:
                if w.readers:
                    w.writers = {o.key: o}
                    w.readers = {}
                else:
                    w.writers[o.key] = o
            else:
                w.writers = {o.key: o}
                w.readers = {}
        o.deps = list(deps.values())
        for d in o.deps:
            d.needsig = True
        self.ops[eng].append(o)
        return o

    def barrier(self, scratch_ap):
        deps = list(self.last.values()) + list(self.pending.values())
        b = self.op('pool', I('memset', scratch_ap, 0.0), extra=deps)
        for en in ('pe', 'act', 'dve', 'sp'):
            self.op(en, None, extra=[b])
        for r, en in self.phase_res:
            self.free_sems.setdefault(en, []).append((r.sem.pop(en), r.semval.pop(en)))
        self.phase_res = []
        self.pending = {}
        self.last = {'pool': b}

    def finalize(self):
        nc = self.nc
        for e in ENGS:
            c = 0
            for o in self.ops[e]:
                if o.isdma or o.fn is None:
                    continue
                if o.needsig:
                    c += 1
                    o.cnt = c
        prog = self
        final_waits = [(o.sem, o.semval) for o in self.pending.values()]

        def emit(engname, eobj):
            waited = {}
            for o in prog.ops[engname]:
                need = {}
                for d in o.deps:
                    if d.isdma:
                        s, v = d.sem, d.semval
                    else:
                        if d.fn is None:
                            continue
                        s, v = prog.esem[d.eng], d.cnt
                    k = id(s)
                    if k not in need or need[k][1] < v:
                        need[k] = (s, v)
                for k, (s, v) in need.items():
                    if waited.get(k, 0) >= v:
                        continue
                    waited[k] = v
                    eobj.wait_ge(s, v)
                    prog.nwaits[engname] = prog.nwaits.get(engname, 0) + 1
                if o.fn is None:
                    continue
                ins = o.fn(eobj)
                if o.isdma:
                    ins.then_inc(o.sem, 16)
                elif o.needsig:
                    ins.then_inc(prog.esem[engname], 1)

        with nc.Block() as block:
            @block.tensor
            def _(e):
                emit('pe', e)

            @block.scalar
            def _(e):
                emit('act', e)

            @block.vector
            def _(e):
                emit('dve', e)

            @block.gpsimd
            def _(e):
                emit('pool', e)
                for s, v in final_waits:
                    e.wait_ge(s, v)

            @block.sync
            def _(e):
                emit('sp', e)


def I(name, *a, **k):
    return lambda e: getattr(e, name)(*a, **k)


def dap(t, off, dims):
    return bass.AP(t, off, [list(d) for d in dims])


def rope_table(smax):
    t = np.arange(smax)
    row = (t // 64).astype(np.float32)
    col = (t % 64).astype(np.float32)
    inv_a = (10000.0 ** (-np.arange(0, 32, 2, dtype=np.float32) / 32)).astype(np.float32)
    inv_b = (500000.0 ** (-np.arange(0, 16, 2, dtype=np.float32) / 16)).astype(np.float32)
    ar = row[:, None] * inv_a[None, :]
    ac = col[:, None] * inv_a[None, :]
    ab = t.astype(np.float32)[:, None] * inv_b[None, :]
    tab = np.concatenate([np.cos(ar), np.cos(ac), np.sin(ar), np.sin(ac), np.cos(ab), np.sin(ab)], axis=1)
    return tab.astype(np.float32)


def dft_tables(S):
    n1 = S // 64
    s1 = np.arange(n1, dtype=np.float64)
    ang1 = 2 * np.pi * np.outer(s1, s1) / n1
    nrm = 1.0 / math.sqrt(S * 64.0)
    c1 = np.cos(ang1) * nrm
    sn1 = np.sin(ang1) * nrm
    d1 = np.concatenate([c1, sn1, -sn1], axis=1).astype(NPBF)
    k1 = np.arange(n1, dtype=np.float64)
    s2 = np.arange(64, dtype=np.float64)
    angt = 2 * np.pi * np.outer(k1, s2) / S
    tw = np.concatenate([np.cos(angt), np.sin(angt)], axis=1).astype(np.float32)
    ang2 = 2 * np.pi * np.outer(s2, s2) / 64.0
    d2 = np.concatenate([np.cos(ang2), np.sin(ang2)], axis=1).astype(NPBF)
    return d1, tw, d2


def const_tables():
    idb = np.eye(128, dtype=np.float32).astype(NPBF)
    onesbd = np.zeros((128, 128), np.float32)
    onesbd[:64, :64] = 1
    onesbd[64:, 64:] = 1
    kk = np.arange(64)[:, None]
    qq = np.arange(192)[None, :]
    mb = np.where((kk <= qq) & (qq <= kk + 128), 0.0, NEG).astype(np.float32)
    maskb = np.concatenate([mb, mb], axis=0).astype(NPBF)
    cols = np.arange(64)
    cstart = np.clip(cols - 8, 0, 48)
    kc = np.arange(64)[:, None]
    qc = np.arange(64)[None, :]
    mc = np.where((kc >= cstart[None, :]) & (kc < cstart[None, :] + 16), 0.0, NEG).astype(np.float32)
    maskc = np.concatenate([mc, mc], axis=0)
    c = np.arange(64, dtype=np.float64)
    ang = 2 * np.pi * np.outer(c, c) / 64.0
    cs = np.zeros((128, 256), np.float64)
    for g in range(2):
        cs[64 * g:64 * g + 64, 64 * g:64 * g + 64] = np.cos(ang)
        cs[64 * g:64 * g + 64, 128 + 64 * g:128 + 64 * g + 64] = -np.sin(ang)
    return idb, onesbd.astype(NPBF), maskb, maskc, cs.astype(NPBF)


class Builder:
    POOL_ACC = (0, 1, 2, 3)
    POOL_ST = (4, 5, 6, 7)

    def __init__(self, seq_lens, depth, n_even, n_odd, same_sync=True):
        self.seq_lens = list(seq_lens)
        self.depth = depth
        self.n_even = n_even
        self.n_odd = n_odd
        self.nseq = len(seq_lens)
        self.ntok = sum(seq_lens)
        self.smax = max(seq_lens)
        self.svals = sorted(set(seq_lens))
        self.row0 = [sum(seq_lens[:i]) for i in range(self.nseq)]
        self.same_sync = same_sync

    def areset(self):
        self.aoff = 0

    def af(self, n):
        o = self.aoff
        self.aoff += n
        assert self.aoff <= self.NA, ("arena overflow", self.aoff, self.NA)
        return self.AR[:, o:o + n]

    def ab(self, n):
        assert n % 2 == 0
        o = self.aoff
        self.aoff += n // 2
        assert self.aoff <= self.NA, ("arena overflow", self.aoff, self.NA)
        return self.AR[:, o:o + n // 2].bitcast(BF16)

    def bank(self, pool=None):
        if pool is None:
            i = self.bank_i
            self.bank_i = (self.bank_i + 1) % 8
        else:
            k = self.pool_i.get(pool, 0)
            self.pool_i[pool] = k + 1
            i = pool[k % len(pool)]
        return self.PS[i], self.rPS[i]

    def R(self, name, multi=False):
        return Res(name, multi)

    def bar(self):
        self.P.barrier(self.BSCR[:, 0:1])

    def build(self):
        nc = bass.Bass("TRN2", target_bir_lowering=False)
        self.nc = nc
        ns, nt, sm = self.nseq, self.ntok, self.smax
        dt = nc.dram_tensor
        X = dict(kind="ExternalInput")
        self.XIN = dt("xin", [nt, D], F32, **X)
        self.CTd = dt("ct", [128, 8 * ns], F32, **X)
        self.PREG = dt("preg", [self.depth, D], F32, **X)
        self.POSTG = dt("postg", [self.depth, D], F32, **X)
        self.ADAW = dt("adaw", [self.depth, D, 3 * D], F32, **X)
        self.ADAB = dt("adab", [self.depth, 3 * D], F32, **X)
        self.WAB = dt("wab", [max(self.n_even, 1), D, 3328], F32, **X)
        self.WOAB = dt("woab", [max(self.n_even, 1), D, D], F32, **X)
        self.QKN = dt("qkn", [max(self.n_even, 1), 128], F32, **X)
        self.WCD = dt("wcd", [max(self.n_odd, 1), D, 3584], F32, **X)
        self.WOCD = dt("wocd", [max(self.n_odd, 1), D, D], F32, **X)
        self.RPBX = dt("rpbx", [max(self.n_odd, 1), 6, 128, 960], F32, **X)
        self.LIND = dt("lind", [max(self.n_odd, 1), 256, 256], F32, **X)
        self.ROPE = dt("rope", [sm, 80], F32, **X)
        self.IDBd = dt("idb", [128, 128], BF16, **X)
        self.ONESd = dt("onesbd", [128, 128], BF16, **X)
        self.SELd = dt("sel", [ns, ns * 128], F32, **X)
        self.MASKBd = dt("maskb", [128, 192], BF16, **X)
        self.MASKCd = dt("maskc", [128, 64], F32, **X)
        self.CS64d = dt("cs64", [128, 256], BF16, **X)
        self.DFT1d, self.TWd, self.DFT2d = {}, {}, {}
        for S in self.svals:
            n1 = S // 64
            self.DFT1d[S] = dt("dft1_%d" % S, [n1, 3 * n1], BF16, **X)
            self.TWd[S] = dt("tw_%d" % S, [n1, 128], F32, **X)
            self.DFT2d[S] = dt("dft2_%d" % S, [64, 128], BF16, **X)
        self.YOUT = dt("yout", [nt, D], F32, kind="ExternalOutput")
        self.XS = dt("xs", [nt, D], F32)
        self.QA_T = dt("qa_t", [4, 128, sm], BF16)
        self.KA_T = dt("ka_t", [128, sm], BF16)
        self.VAUGd = dt("vaug", [sm, 384], BF16)
        self.GATE_T = dt("gate_t", [8, 128, sm], BF16)
        self.QB_T = dt("qb_t", [4, 128, sm], BF16)
        self.KB_T = dt("kb_t", [4, 128, sm], BF16)
        self.VBd = dt("vb", [sm, 512], BF16)
        self.MIX_T = dt("mix_t", [8, 128, sm], BF16)
        self.QC_T = dt("qc_t", [6, 128, sm], BF16)
        self.KC_T = dt("kc_t", [6, 128, sm], BF16)
        self.VCd = dt("vc", [sm, 768], BF16)
        self.YDd = dt("yd", [sm, 512], BF16)
        self.GDd = dt("gd", [128, 64, 512], BF16)
        self.MODD = dt("modd", [ns, 3 * D], F32)

        with ExitStack() as st:
            self.P = Prog(nc, st, self.same_sync)
            sbt = lambda n, s, d: st.enter_context(nc.sbuf_tensor(n, s, d))
            self.NA = 40960
            self.AR = sbt("arena", [128, self.NA], F32)
            self.IDB = sbt("idb_s", [128, 128], BF16)
            self.ONESBD = sbt("ones_s", [128, 128], BF16)
            self.MASKB = sbt("maskb_s", [128, 192], BF16)
            self.MASKC = sbt("maskc_s", [128, 64], F32)
            self.CS64 = sbt("cs64_s", [128, 256], BF16)
            self.SEL = sbt("sel_s", [ns, ns * 128], F32)
            self.CT = sbt("ct_s", [128, 8 * ns], F32)
            self.QKNB = sbt("qkn_s", [128, 128], F32)
            self.ABC = sbt("abc", [128, 3 * D], F32)
            self.EPST = sbt("epst", [128, 1], F32)
            self.BSCR = sbt("bscr", [128, 2], F32)
            self.NEGH = sbt("negh", [128, 16], F32)
            self.JUNK = sbt("junk", [128, 1024], BF16)
            self.DFT1, self.TW, self.DFT2 = {}, {}, {}
            for S in self.svals:
                n1 = S // 64
                self.DFT1[S] = sbt("dft1s_%d" % S, [n1, 3 * n1], BF16)
                self.TW[S] = sbt("tws_%d" % S, [n1, 128], F32)
                self.DFT2[S] = sbt("dft2s_%d" % S, [64, 128], BF16)
            self.PS2 = [st.enter_context(nc.psum_tensor("ps%d" % i, [128, 1024], F32)) for i in range(4)]
            self.PS = [self.PS2[i // 2][:, (i % 2) * 512:(i % 2 + 1) * 512] for i in range(8)]
            self.st2_i = 0
            self.rPS = [Res("ps%d" % i) for i in range(8)]
            self.bank_i = 0
            self.pool_i = {}
            self.rD = {n: Res(n, multi=True) for n in
                       ["modd", "xs", "yout", "qa_t", "ka_t", "vaug", "gate_t", "qb_t", "kb_t", "vb", "mix_t",
                        "qc_t", "kc_t", "vc", "yd", "gd"]}
            self.rC = Res("consts")
            self.rMOD = Res("modrow")
            self.rABC = Res("abc")
            self.rQKN = Res("qkn")

            self.setup()
            prev = (self.XIN, None)
            for l in range(self.depth):
                src, rsrc = prev
                if (self.depth - 1 - l) % 2 == 0:
                    dst, rdst = self.YOUT, self.rD["yout"]
                else:
                    dst, rdst = self.XS, self.rD["xs"]
                prev = (dst, rdst)
                self.adaln(l)
                for si in range(self.nseq):
                    S = self.seq_lens[si]
                    self.modprep(l, si)
                    if l % 2 == 0:
                        j = l // 2
                        self.load_w(self.WAB, j, 3328)
                        self.proj(l, si, src, rsrc, even=True, j=j)
                        self.bar()
                        self.mixer_a(si)
                        self.bar()
                        self.mixer_b(si)
                        self.bar()
                        self.outproj(self.WOAB, j, si, src, rsrc, dst, rdst)
                        self.bar()
                    else:
                        j = l // 2
                        self.load_w(self.WCD, j, 3584)
                        self.proj(l, si, src, rsrc, even=False, j=j)
                        self.bar()
                        self.mixer_c(j, si)
                        self.bar()
                        self.mixer_d(j, si)
                        self.bar()
                        self.outproj(self.WOCD, j, si, src, rsrc, dst, rdst)
                        self.bar()
            self.P.finalize()
        return nc

    def setup(self):
        P = self.P
        rc = self.rC
        ld = lambda t, d: P.op('sp', I('dma_start', out=t[:], in_=d.ap()), writes=[rc], dma=Res("c"))
        ld(self.IDB, self.IDBd)
        ld(self.ONESBD, self.ONESd)
        ld(self.MASKB, self.MASKBd)
        ld(self.MASKC, self.MASKCd)
        ld(self.CS64, self.CS64d)
        ld(self.SEL, self.SELd)
        ld(self.CT, self.CTd)
        for S in self.svals:
            ld(self.DFT1[S], self.DFT1d[S])
            ld(self.TW[S], self.TWd[S])
            ld(self.DFT2[S], self.DFT2d[S])
        P.op('pool', I('memset', self.EPST[:], EPS), writes=[rc])
        P.op('pool', I('memset', self.BSCR[:], 0.0), writes=[rc])
        P.op('pool', I('memset', self.NEGH[:], -0.5), writes=[rc])
        self.bar()
        P.op('act', I('activation', out=self.CT[:], in_=self.CT[:], func=AF.Silu), writes=[rc])
        self.bar()

    def adaln(self, l):
        P = self.P
        ns = self.nseq
        self.areset()
        MR = self.af(3072)
        rMR = Res("mr")
        wts = [self.af(3072) for _ in range(8)]
        rw = [Res("adaw%d" % k) for k in range(8)]
        bb = self.af(3072)
        rb = Res("adab")
        for k in range(8):
            P.op('sp', I('dma_start', out=wts[k], in_=dap(self.ADAW, (l * D + 128 * k) * 3 * D, [[3 * D, 128], [1, 3 * D]])),
                 writes=[rw[k]], dma=rw[k])
        P.op('sp', I('dma_start', out=bb[0:ns, :], in_=dap(self.ADAB, l * 3 * D, [[0, ns], [1, 3 * D]])),
             writes=[rb], dma=rb)
        for half in range(2):
            banks = [self.bank() for _ in range(3)]
            for k in range(8):
                for b in range(3):
                    c0 = half * 1536 + b * 512
                    ps, rps = banks[b]
                    P.op('pe', I('matmul', ps[0:ns, :], self.CT[:, k * ns:(k + 1) * ns], wts[k][:, c0:c0 + 512],
                                                                     start=(k == 0), stop=(k == 7)),
                         reads=[rw[k], self.rC], writes=[rps])
            for b in range(3):
                c0 = half * 1536 + b * 512
                ps, rps = banks[b]
                P.op('dve', I('tensor_tensor', out=MR[0:ns, c0:c0 + 512], in0=ps[0:ns, :], in1=bb[0:ns, c0:c0 + 512], op=ALU.add),
                     reads=[rps, rb], writes=[rMR])
        P.op('pool', I('dma_start', out=self.MODD.ap(), in_=MR[0:ns, :]), reads=[rMR], writes=[self.rD["modd"]], dma=rMR)
        self.bar()

    def modprep(self, l, si):
        P = self.P
        ns = self.nseq
        self.areset()
        pg = self.af(1024)
        qg = self.af(1024)
        rpg, rqg = Res("pg"), Res("qg")
        P.op('sp', I('dma_start', out=pg, in_=dap(self.PREG, l * D, [[0, 128], [1, D]])), writes=[rpg], dma=rpg)
        P.op('sp', I('dma_start', out=qg, in_=dap(self.POSTG, l * D, [[0, 128], [1, D]])), writes=[rqg], dma=rqg)
        MR = self.af(3072)
        rMR = Res("mr")
        P.op('sp', I('dma_start', out=MR[0:ns, :], in_=self.MODD.ap()), reads=[self.rD["modd"]], writes=[rMR], dma=rMR)
        sel = self.SEL[:, si * 128:(si + 1) * 128]
        for part in range(3):
            for h in range(2):
                ps, rps = self.bank()
                c0 = part * 1024 + h * 512
                P.op('pe', I('matmul', ps[:, :], sel, MR[0:ns, c0:c0 + 512], start=True, stop=True),
                     reads=[rMR, self.rC], writes=[rps])
                if part == 0:
                    P.op('dve', I('tensor_copy', out=self.ABC[:, 1024 + h * 512:1024 + (h + 1) * 512], in_=ps[:, :]),
                         reads=[rps], writes=[self.rABC])
                elif part == 1:
                    P.op('dve', I('scalar_tensor_tensor', out=self.ABC[:, h * 512:(h + 1) * 512], in0=ps[:, :], scalar=1.0,
                                                                             in1=pg[:, h * 512:(h + 1) * 512], op0=ALU.add, op1=ALU.mult),
                         reads=[rps, rpg], writes=[self.rABC])
                else:
                    P.op('dve', I('tensor_tensor', out=self.ABC[:, 2048 + h * 512:2048 + (h + 1) * 512], in0=ps[:, :],
                                                                      in1=qg[:, h * 512:(h + 1) * 512], op=ALU.mult),
                         reads=[rps, rqg], writes=[self.rABC])
        self.bar()

    def load_w(self, WD, j, ncol):
        P = self.P
        self.areset()
        self.WB = self.ab(8 * ncol)
        self.wncol = ncol
        self.rWB = Res("wb")
        mark = self.aoff
        st = [self.af(ncol) for _ in range(2)]
        rst = [Res("wst0"), Res("wst1")]
        engs = ['pool', 'act', 'dve']
        for k in range(8):
            b = k % 2
            P.op('sp', I('dma_start', out=st[b], in_=dap(WD, (j * D + 128 * k) * ncol, [[ncol, 128], [1, ncol]])),
                 writes=[rst[b]], dma=rst[b])
            en = engs[k % 3]
            dst = self.WB[:, k * ncol:(k + 1) * ncol]
            if en == 'act':
                P.op('act', I('copy', out=dst, in_=st[b]), reads=[rst[b]], writes=[self.rWB])
            else:
                P.op(en, I('tensor_copy', out=dst, in_=st[b]), reads=[rst[b]], writes=[self.rWB])
        self.bar()
        self.aoff = mark

    def norm_tile(self, xt, rxt, hb, rhb, hf, rhf, ss, rss):
        P = self.P
        P.op('act', I('activation', out=self.JUNK[:], in_=xt, func=AF.Square, accum_out=ss[:, 0:1]), reads=[rxt], writes=[rss])
        P.op('act', I('activation', out=ss[:, 1:2], in_=ss[:, 0:1], func=AF.Sqrt, bias=self.EPST[:, 0:1], scale=1.0 / D),
             reads=[rss, self.rC], writes=[rss])
        P.op('dve', I('reciprocal', out=ss[:, 2:3], in_=ss[:, 1:2]), reads=[rss], writes=[rss])
        P.op('dve', I('scalar_tensor_tensor', out=hf, in0=xt, scalar=ss[:, 2:3], in1=self.ABC[:, 0:1024], op0=ALU.mult, op1=ALU.mult),
             reads=[rxt, rss, self.rABC], writes=[rhf])
        P.op('pool', I('tensor_tensor', out=hb, in0=hf, in1=self.ABC[:, 1024:2048], op=ALU.add), reads=[rhf, self.rABC], writes=[rhb])

    def proj(self, l, si, src, rsrc, even, j):
        P = self.P
        S = self.seq_lens[si]
        row0 = self.row0[si]
        ncol = self.wncol
        WB = self.WB
        ntile = S // 128
        ngrp = S // 512
        ntm = 2304 if even else 768
        nfm = (ncol - ntm) // 128
        XT = [self.af(1024) for _ in range(2)]
        HF = [self.af(1024) for _ in range(2)]
        HB = [self.ab(1024) for _ in range(2)]
        RT = [self.af(80) for _ in range(2)]
        SS = [self.af(4) for _ in range(2)]
        HTg = [self.ab(8 * 512) for _ in range(2)]
        FMst = [self.ab(nfm * 512) for _ in range(2)]
        rXT = [Res("xt%d" % i) for i in range(2)]
        rHF = [Res("hf%d" % i) for i in range(2)]
        rHB = [Res("hb%d" % i) for i in range(2)]
        rRT = [Res("rt%d" % i) for i in range(2)]
        rSS = [Res("ss%d" % i) for i in range(2)]
        rHT = [[Res("ht%d_%d" % (i, s)) for s in range(4)] for i in range(2)]
        rFM = [Res("fm%d" % i) for i in range(2)]
        rS2 = [Res("s2_%d" % i) for i in range(2)]
        rS3 = [Res("s3_%d" % i) for i in range(2)]
        rS4 = [Res("s4_%d" % i) for i in range(2)]
        if even:
            QSQ = self.af(640)
            QN = [self.af(640) for _ in range(2)]
            TMP = [self.af(320) for _ in range(4)]
            QST = self.af(32)
            QATOK = [self.ab(640) for _ in range(2)]
            VAST = [self.ab(384) for _ in range(2)]
            QBTOK = [self.ab(1024) for _ in range(2)]
            VBST = [self.ab(512) for _ in range(2)]
            TB = [self.af(64) for _ in range(4)]
            QATst = [self.ab(5 * 512) for _ in range(2)]
            QBTst = [self.ab(8 * 512) for _ in range(2)]
            rQSQ, rQST = Res("qsq"), Res("qst")
            rQN = [Res("qn%d" % i) for i in range(2)]
            rTMP = [Res("tmp%d" % i) for i in range(4)]
            rTB = [Res("tb%d" % i) for i in range(4)]
            rQATOK = [Res("qatok%d" % i) for i in range(2)]
            rVAST = [Res("vast%d" % i) for i in range(2)]
            rQBTOK = [[Res("qbtok%d_%d" % (i, w)) for w in range(2)] for i in range(2)]
            rVBST = [Res("vbst%d" % i) for i in range(2)]
            rQATst = [Res("qatst%d" % i) for i in range(2)]
            rQBTst = [Res("qbtst%d" % i) for i in range(2)]
            P.op('sp', I('dma_start', out=self.QKNB[:], in_=dap(self.QKN, j * 128, [[0, 128], [1, 128]])), writes=[self.rQKN], dma=self.rQKN)
            for b in range(2):
                P.op('pool', I('memset', VAST[b], 1.0), writes=[rVAST[b]])
        else:
            VCST = [self.ab(768) for _ in range(2)]
            UDT = [self.ab(2 * 512) for _ in range(2)]
            YST = [self.ab(512) for _ in range(2)]
            rVCST = [Res("vcst%d" % i) for i in range(2)]
            rUDT = [Res("udt%d" % i) for i in range(2)]
            rYST = [Res("yst%d" % i) for i in range(2)]
        wview = lambda k, c0, n: WB[:, k * ncol + c0:k * ncol + c0 + n]

        def s1(it):
            b = it % 2
            r = row0 + 128 * it
            P.op('sp', I('dma_start', out=XT[b], in_=dap(src, r * D, [[D, 128], [1, D]])), reads=[rsrc], writes=[rXT[b]], dma=rXT[b])
            if even:
                P.op('sp', I('dma_start', out=RT[b], in_=dap(self.ROPE, 128 * it * 80, [[80, 128], [1, 80]])), writes=[rRT[b]], dma=rRT[b])
            self.norm_tile(XT[b], rXT[b], HB[b], rHB[b], HF[b], rHF[b], SS[b], rSS[b])

        def s2(it):
            b = it % 2
            gb = (it // 4) % 2
            sub = it % 4
            ps, rps = self.bank()
            pb = ps[:, :].bitcast(BF16)
            for k in range(8):
                P.op('pe', I('transpose', out=pb[:, 128 * k:128 * k + 128], in_=HB[b][:, 128 * k:128 * k + 128], identity=self.IDB[:]),
                     reads=[rHB[b], self.rC], writes=[rps])
            dst = HTg[gb].rearrange("p (k t) -> p k t", k=8)[:, :, sub * 128:(sub + 1) * 128]
            P.op('act', I('copy', out=dst, in_=pb[:, 0:1024].rearrange("p (k t) -> p k t", k=8)), reads=[rps], writes=[rHT[gb][sub]])

        def tm_block(it, c0, n):
            gb = (it // 4) % 2
            sub = it % 4
            ps, rps = self.bank()
            for k in range(8):
                P.op('pe', I('matmul', ps[:, 0:n], HTg[gb][:, k * 512 + sub * 128:k * 512 + sub * 128 + 128], wview(k, c0, n),
                                                   start=(k == 0), stop=(k == 7)),
                     reads=[rHT[gb][sub], self.rWB], writes=[rps])
            return ps, rps

        def qk_norm_rope(it, ps, rps, c0, nh, gain0, dstap, rdst):
            b = it % 2
            n = 64 * nh
            xin = ps[:, c0:c0 + n]
            P.op('act', I('activation', out=QSQ[:, 0:n], in_=xin, func=AF.Square), reads=[rps], writes=[rQSQ])
            P.op('dve', I('tensor_reduce', out=QST[:, 0:nh], in_=QSQ[:, 0:n].rearrange("p (h d) -> p h d", h=nh), axis=AX.X, op=ALU.add),
                 reads=[rQSQ], writes=[rQST])
            P.op('dve', I('tensor_scalar', out=QST[:, 8:8 + nh], in0=QST[:, 0:nh], scalar1=1.0 / 64, scalar2=EPS, op0=ALU.mult, op1=ALU.add),
                 reads=[rQST], writes=[rQST])
            P.op('pool', I('tensor_tensor', out=QST[:, 16:16 + nh], in0=QST[:, 8:8 + nh], in1=self.NEGH[:, 0:nh], op=ALU.pow),
                 reads=[rQST, self.rC], writes=[rQST])
            qn = QN[b][:, 0:n]
            P.op('dve', I('tensor_tensor', out=qn.rearrange("p (h d) -> p h d", h=nh), in0=xin.rearrange("p (h d) -> p h d", h=nh),
                                                  in1=QST[:, 16:16 + nh].unsqueeze(2).broadcast_to([128, nh, 64]), op=ALU.mult),
                 reads=[rps, rQST], writes=[rQN[b]])
            P.op('pool', I('tensor_tensor', out=qn.rearrange("p (h d) -> p h d", h=nh), in0=qn.rearrange("p (h d) -> p h d", h=nh),
                                                   in1=self.QKNB[:, gain0:gain0 + 64].unsqueeze(1).broadcast_to([128, nh, 64]), op=ALU.mult),
                 reads=[rQN[b], self.rQKN], writes=[rQN[b]])
            v = qn.rearrange("p (h a x f) -> p h a x f", h=nh, a=2, x=2)
            o = dstap.rearrange("p (h a x f) -> p h a x f", h=nh, a=2, x=2)
            cosv = RT[b][:, 0:32].rearrange("p (a f) -> p a f", a=2).unsqueeze(1).broadcast_to([128, nh, 2, 16])
            sinv = RT[b][:, 32:64].rearrange("p (a f) -> p a f", a=2).unsqueeze(1).broadcast_to([128, nh, 2, 16])
            x1 = v[:, :, :, 0, :]
            x2 = v[:, :, :, 1, :]
            m = nh * 32
            t = [TMP[i][:, 0:m].rearrange("p (h a f) -> p h a f", h=nh, a=2) for i in range(4)]
            P.op('dve', I('tensor_tensor', out=t[0], in0=x1, in1=cosv, op=ALU.mult), reads=[rQN[b], rRT[b]], writes=[rTMP[0]])
            P.op('dve', I('tensor_tensor', out=t[1], in0=x2, in1=sinv, op=ALU.mult), reads=[rQN[b], rRT[b]], writes=[rTMP[1]])
            P.op('dve', I('tensor_tensor', out=o[:, :, :, 0, :], in0=t[0], in1=t[1], op=ALU.subtract), reads=[rTMP[0], rTMP[1]], writes=[rdst])
            P.op('pool', I('tensor_tensor', out=t[2], in0=x1, in1=sinv, op=ALU.mult), reads=[rQN[b], rRT[b]], writes=[rTMP[2]])
            P.op('pool', I('tensor_tensor', out=t[3], in0=x2, in1=cosv, op=ALU.mult), reads=[rQN[b], rRT[b]], writes=[rTMP[3]])
            P.op('pool', I('tensor_tensor', out=o[:, :, :, 1, :], in0=t[2], in1=t[3], op=ALU.add), reads=[rTMP[2], rTMP[3]], writes=[rdst])

        def partial_rope(it, ps, rps, dst, rdst):
            b = it % 2
            pv = ps[:, 0:512].rearrange("p (h d) -> p h d", h=8)
            dv = dst.rearrange("p (h d) -> p h d", h=8)
            x1 = pv[:, :, 0:8]
            x2 = pv[:, :, 8:16]
            cosv = RT[b][:, 64:72].unsqueeze(1).broadcast_to([128, 8, 8])
            sinv = RT[b][:, 72:80].unsqueeze(1).broadcast_to([128, 8, 8])
            t = [TB[i][:, 0:64].rearrange("p (h f) -> p h f", h=8) for i in range(4)]
            P.op('dve', I('tensor_tensor', out=t[0], in0=x1, in1=cosv, op=ALU.mult), reads=[rps, rRT[b], rdst], writes=[rTB[0]])
            P.op('dve', I('tensor_tensor', out=t[1], in0=x2, in1=sinv, op=ALU.mult), reads=[rps, rRT[b]], writes=[rTB[1]])
            P.op('dve', I('tensor_tensor', out=dv[:, :, 0:8], in0=t[0], in1=t[1], op=ALU.subtract), reads=[rTB[0], rTB[1]], writes=[rdst])
            P.op('dve', I('tensor_tensor', out=t[2], in0=x1, in1=sinv, op=ALU.mult), reads=[rps, rRT[b]], writes=[rTB[2]])
            P.op('dve', I('tensor_tensor', out=t[3], in0=x2, in1=cosv, op=ALU.mult), reads=[rps, rRT[b]], writes=[rTB[3]])
            P.op('dve', I('tensor_tensor', out=dv[:, :, 8:16], in0=t[2], in1=t[3], op=ALU.add), reads=[rTB[2], rTB[3]], writes=[rdst])

        def s3_even(it):
            b = it % 2
            gb = (it // 4) % 2
            sub = it % 4
            r = 128 * it
            ps, rps = tm_block(it, 0, 512)
            qk_norm_rope(it, ps, rps, 0, 8, 0, QATOK[b][:, 0:512], rQATOK[b])
            ps, rps = tm_block(it, 512, 256)
            qk_norm_rope(it, ps, rps, 0, 2, 64, QATOK[b][:, 512:640], rQATOK[b])
            P.op('dve', I('tensor_copy', out=VAST[b].rearrange("p (k c) -> p k c", k=2)[:, :, 64:128],
                                                in_=ps[:, 128:256].rearrange("p (k c) -> p k c", k=2)), reads=[rps, rQSQ], writes=[rVAST[b]])
            P.op('pool', I('dma_start', out=dap(self.VAUGd, r * 384, [[384, 128], [1, 384]]), in_=VAST[b]),
                 reads=[rVAST[b]], writes=[self.rD["vaug"]], dma=rVAST[b])
            for which in range(2):
                ps, rps = tm_block(it, 768 + 512 * which, 512)
                dstb = QBTOK[b][:, 512 * which:512 * which + 512]
                P.op('act', I('copy', out=dstb, in_=ps[:, 0:512]), reads=[rps], writes=[rQBTOK[b][which]])
                partial_rope(it, ps, rps, dstb, rQBTOK[b][which])
            ps, rps = tm_block(it, 1792, 512)
            P.op('act', I('copy', out=VBST[b], in_=ps[:, 0:512]), reads=[rps], writes=[rVBST[b]])
            P.op('pool', I('dma_start', out=dap(self.VBd, r * 512, [[512, 128], [1, 512]]), in_=VBST[b]),
                 reads=[rVBST[b]], writes=[self.rD["vb"]], dma=rVBST[b])

        def s3b_even(it):
            b = it % 2
            gb = (it // 4) % 2
            sub = it % 4
            ps2, rps2 = self.bank()
            pb = ps2[:, :].bitcast(BF16)
            for c in range(5):
                P.op('pe', I('transpose', out=pb[:, 128 * c:128 * c + 128], in_=QATOK[b][:, 128 * c:128 * c + 128], identity=self.IDB[:]),
                     reads=[rQATOK[b], self.rC], writes=[rps2])
            dst = QATst[gb].rearrange("p (c t) -> p c t", c=5)[:, :, sub * 128:(sub + 1) * 128]
            P.op('act', I('copy', out=dst, in_=pb[:, 0:640].rearrange("p (c t) -> p c t", c=5)), reads=[rps2], writes=[rQATst[gb]])
            ps3, rps3 = self.bank()
            pb3 = ps3[:, :].bitcast(BF16)
            for c in range(8):
                P.op('pe', I('transpose', out=pb3[:, 128 * c:128 * c + 128], in_=QBTOK[b][:, 128 * c:128 * c + 128], identity=self.IDB[:]),
                     reads=rQBTOK[b] + [self.rC], writes=[rps3])
            dst3 = QBTst[gb].rearrange("p (c t) -> p c t", c=8)[:, :, sub * 128:(sub + 1) * 128]
            P.op('act', I('copy', out=dst3, in_=pb3[:, 0:1024].rearrange("p (c t) -> p c t", c=8)), reads=[rps3], writes=[rQBTst[gb]])
        def s3_odd(it):
            b = it % 2
            r = 128 * it
            ps, rps = tm_block(it, 0, 512)
            P.op('act', I('copy', out=VCST[b][:, 0:512], in_=ps[:, 0:512]), reads=[rps], writes=[rVCST[b]])
            ps, rps = tm_block(it, 512, 256)
            P.op('act', I('copy', out=VCST[b][:, 512:768], in_=ps[:, 0:256]), reads=[rps], writes=[rVCST[b]])
            P.op('pool', I('dma_start', out=dap(self.VCd, r * 768, [[768, 128], [1, 768]]), in_=VCST[b]),
                 reads=[rVCST[b]], writes=[self.rD["vc"]], dma=rVCST[b])

        def s4(g):
            gb = g % 2
            sm = self.smax
            t0 = 512 * g
            for c in range(nfm):
                ps, rps = self.bank()
                for k in range(8):
                    P.op('pe', I('matmul', ps[:, :], wview(k, ntm + 128 * c, 128), HTg[gb][:, k * 512:(k + 1) * 512],
                                                                   start=(k == 0), stop=(k == 7)),
                         reads=rHT[gb] + [self.rWB], writes=[rps])
                dst = FMst[gb][:, c * 512:(c + 1) * 512]
                if even:
                    P.op('act', I('activation', out=dst, in_=ps[:, :], func=AF.Silu), reads=[rps], writes=[rFM[gb]])
                else:
                    if c < 6:
                        P.op('act', I('mul', dst, ps[:, :], 0.125), reads=[rps], writes=[rFM[gb]])
                    elif c < 12:
                        P.op('dve', I('tensor_copy', out=dst, in_=ps[:, :]), reads=[rps], writes=[rFM[gb]])
                    elif c < 20:
                        P.op('act', I('activation', out=dst, in_=ps[:, :], func=AF.Silu), reads=[rps], writes=[rFM[gb]])
                    else:
                        P.op('dve', I('tensor_copy', out=UDT[gb][:, (c - 20) * 512:(c - 19) * 512], in_=ps[:, :]), reads=[rps], writes=[rUDT[gb]])
            fm3 = lambda c0, n: FMst[gb][:, c0 * 512:(c0 + n) * 512].rearrange("p (c t) -> p c t", c=n)
            dd = lambda T, c0, n: dap(T, c0 * 128 * sm + t0, [[sm, 128], [128 * sm, n], [1, 512]])
            if even:
                P.op('pool', I('dma_start', out=dd(self.GATE_T, 0, 8), in_=fm3(0, 8)), reads=[rFM[gb]], writes=[self.rD["gate_t"]], dma=rFM[gb])
                q3 = QATst[gb][:, 0:4 * 512].rearrange("p (c t) -> p c t", c=4)
                P.op('pool', I('dma_start', out=dd(self.QA_T, 0, 4), in_=q3), reads=[rQATst[gb]], writes=[self.rD["qa_t"]], dma=rQATst[gb])
                P.op('pool', I('dma_start', out=dap(self.KA_T, t0, [[sm, 128], [1, 512]]), in_=QATst[gb][:, 4 * 512:5 * 512]),
                     reads=[rQATst[gb]], writes=[self.rD["ka_t"]], dma=rS2[gb])
                qb3 = QBTst[gb][:, 0:4 * 512].rearrange("p (c t) -> p c t", c=4)
                kb3 = QBTst[gb][:, 4 * 512:8 * 512].rearrange("p (c t) -> p c t", c=4)
                P.op('pool', I('dma_start', out=dd(self.QB_T, 0, 4), in_=qb3), reads=[rQBTst[gb]], writes=[self.rD["qb_t"]], dma=rQBTst[gb])
                P.op('pool', I('dma_start', out=dd(self.KB_T, 0, 4), in_=kb3), reads=[rQBTst[gb]], writes=[self.rD["kb_t"]], dma=rS3[gb])
            else:
                P.op('pool', I('dma_start', out=dd(self.QC_T, 0, 6), in_=fm3(0, 6)), reads=[rFM[gb]], writes=[self.rD["qc_t"]], dma=rFM[gb])
                P.op('pool', I('dma_start', out=dd(self.KC_T, 0, 6), in_=fm3(6, 6)), reads=[rFM[gb]], writes=[self.rD["kc_t"]], dma=rS2[gb])
                P.op('pool', I('dma_start', out=dd(self.GATE_T, 0, 8), in_=fm3(12, 8)), reads=[rFM[gb]], writes=[self.rD["gate_t"]], dma=rS3[gb])
                for sub in range(4):
                    yb = (4 * g + sub) % 2
                    ps, rps = self.bank()
                    for jc in range(2):
                        P.op('pe', I('matmul', ps[:, 256 * jc:256 * jc + 256],
                                                                             UDT[gb][:, jc * 512 + sub * 128:jc * 512 + sub * 128 + 128],
                                                                             self.CS64[:, :], start=True, stop=True),
                             reads=[rUDT[gb], self.rC], writes=[rps])
                    src_v = ps[:, :].rearrange("p (j r m) -> p r j m", j=2, r=2)
                    dst_v = YST[yb].rearrange("p (r j m) -> p r j m", r=2, j=2)
                    P.op('act', I('copy', out=dst_v, in_=src_v), reads=[rps], writes=[rYST[yb]])
                    r = t0 + 128 * sub
                    P.op('pool', I('dma_start', out=dap(self.YDd, r * 512, [[512, 128], [1, 512]]), in_=YST[yb]),
                         reads=[rYST[yb]], writes=[self.rD["yd"]], dma=rYST[yb])

        s3 = s3_even if even else s3_odd
        s1(0)
        for it in range(ntile):
            if it + 1 < ntile:
                s1(it + 1)
            s2(it)
            s3(it)
            if even:
                if it % 4 > 0:
                    s3b_even(it - 1)
                if it % 4 == 3:
                    s3b_even(it)
            if it % 4 == 3:
                s4(it // 4)

    def finalize_pair(self, numps, rnum, denps, rden, n, gate, rgate, rd, rrd, ot, rot):
        P = self.P
        P.op('dve', I('reciprocal', out=rd[:, 0:n], in_=denps), reads=[rden], writes=[rrd])
        P.op('dve', I('tensor_tensor', out=rd[:, 0:n], in0=numps, in1=rd[:, 0:n], op=ALU.mult), reads=[rnum, rrd], writes=[rrd])
        P.op('pool', I('tensor_tensor', out=ot[:, 0:n], in0=rd[:, 0:n], in1=gate[:, 0:n], op=ALU.mult), reads=[rrd, rgate], writes=[rot])

    def mixer_a(self, si):
        P = self.P
        S = self.seq_lens[si]
        sm = self.smax
        nch = S // 128
        nqb = S // 512
        self.areset()
        KA2 = [self.ab(S) for _ in range(2)]
        rKA2 = [Res("ka2_%d" % i) for i in range(2)]
        VA = self.ab(nch * 384)
        rVA = Res("va")
        for kv in range(2):
            for hh in range(2):
                rr = Res("ka2l")
                P.op('sp', I('dma_start', out=KA2[kv][64 * hh:64 * hh + 64, :], in_=dap(self.KA_T, 64 * kv * sm, [[sm, 64], [1, S]])),
                     reads=[self.rD["ka_t"]], writes=[rKA2[kv]], dma=rr)
        npiece = (nch + 7) // 8
        rVAp = [Res("va%d" % i) for i in range(npiece)]
        for pi in range(npiece):
            c0 = 8 * pi
            cn = min(8, nch - c0)
            P.op('sp', I('dma_start', out=VA[:, c0 * 384:(c0 + cn) * 384].rearrange("p (c x) -> p c x", c=cn),
                         in_=dap(self.VAUGd, c0 * 128 * 384, [[384, 128], [128 * 384, cn], [1, 384]])),
                 reads=[self.rD["vaug"]], writes=[rVAp[pi]], dma=rVAp[pi])
        NQ = 3
        QT = [self.ab(512) for _ in range(NQ)]
        GT = [self.ab(512) for _ in range(NQ)]
        rQT = [Res("qt%d" % i) for i in range(NQ)]
        rGT = [Res("gt%d" % i) for i in range(NQ)]
        NE = 3
        ET = [self.ab(1024) for _ in range(NE)]
        rET = [Res("et%d" % i) for i in range(NE)]
        RD = [self.af(512) for _ in range(2)]
        rRD = [Res("rd%d" % i) for i in range(2)]
        OT = [self.ab(512) for _ in range(2)]
        rOT = [Res("ot%d" % i) for i in range(2)]
        items = [(qb, c) for qb in range(nqb) for c in range(4)]

        def load(ix):
            qb, c = items[ix]
            s = ix % NQ
            P.op('sp', I('dma_start', out=QT[s], in_=dap(self.QA_T, c * 128 * sm + 512 * qb, [[sm, 128], [1, 512]])),
                 reads=[self.rD["qa_t"]], writes=[rQT[s]], dma=rQT[s])
            P.op('sp', I('dma_start', out=GT[s], in_=dap(self.GATE_T, c * 128 * sm + 512 * qb, [[sm, 128], [1, 512]])),
                 reads=[self.rD["gate_t"]], writes=[rGT[s]], dma=rGT[s])

        ecount = 0
        load(0)
        for ix, (qb, c) in enumerate(items):
            if ix + 1 < len(items):
                load(ix + 1)
            s = ix % NQ
            kv = c // 2
            accs = [self.bank(self.POOL_ACC), self.bank(self.POOL_ACC)]
            pend = []

            def pv(jj, es):
                for e_ in range(2):
                    aps, raps = accs[e_]
                    lo = 64 if e_ == 0 else 0
                    vap = VA[:, jj * 384 + kv * 192 + lo:jj * 384 + kv * 192 + lo + 128]
                    P.op('pe', I('matmul', aps[:, :], vap, ET[es][:, 512 * e_:512 * e_ + 512], start=(jj == 0), stop=(jj == nch - 1)),
                         reads=[rVAp[jj // 8], rET[es]], writes=[raps])

            for jj in range(nch):
                if len(pend) == 2:
                    pv(*pend.pop(0))
                k2 = self.st2_i % 2
                self.st2_i += 1
                st2 = self.PS2[2 + k2]
                rst = [self.rPS[4 + 2 * k2], self.rPS[5 + 2 * k2]]
                for e_ in range(2):
                    P.op('pe', I('matmul', st2[:, 512 * e_:512 * e_ + 512], KA2[kv][64 * e_:64 * e_ + 64, 128 * jj:128 * jj + 128],
                                 QT[s][64 * e_:64 * e_ + 64, :], start=True, stop=True),
                         reads=[rKA2[kv], rQT[s]], writes=[rst[e_]])
                es = ecount % NE
                ecount += 1
                P.op('act', I('activation', out=ET[es], in_=st2[:, 0:1024], func=AF.Exp, scale=0.125), reads=rst, writes=[rET[es]])
                pend.append((jj, es))
            while pend:
                pv(*pend.pop(0))
            fb = ix % 2
            a0, ra0 = accs[0]
            a1, ra1 = accs[1]
            P.op('dve', I('reciprocal', out=RD[fb][0:64, :], in_=a0[64:128, :]), reads=[ra0], writes=[rRD[fb]])
            P.op('dve', I('reciprocal', out=RD[fb][64:128, :], in_=a1[0:64, :]), reads=[ra1], writes=[rRD[fb]])
            P.op('dve', I('tensor_tensor', out=RD[fb][0:64, :], in0=a0[0:64, :], in1=RD[fb][0:64, :], op=ALU.mult), reads=[ra0, rRD[fb]], writes=[rRD[fb]])
            P.op('dve', I('tensor_tensor', out=RD[fb][64:128, :], in0=a1[64:128, :], in1=RD[fb][64:128, :], op=ALU.mult), reads=[ra1, rRD[fb]], writes=[rRD[fb]])
            P.op('pool', I('tensor_tensor', out=OT[fb], in0=RD[fb], in1=GT[s], op=ALU.mult), reads=[rRD[fb], rGT[s]], writes=[rOT[fb]])
            P.op('pool', I('dma_start', out=dap(self.MIX_T, c * 128 * sm + 512 * qb, [[sm, 128], [1, 512]]), in_=OT[fb]),
                 reads=[rOT[fb]], writes=[self.rD["mix_t"]], dma=rOT[fb])

    def acc_alloc(self, n):
        P = self.P
        numps, rnum = self.bank(self.POOL_ACC)
        denps, rden = self.bank(self.POOL_ACC)
        P.op('dve', I('memset', numps[:, 0:n], 0.0), writes=[rnum])
        P.op('dve', I('memset', denps[:, 0:n], 0.0), writes=[rden])
        return numps, rnum, denps, rden

    def pair_attn_block(self, steps, kbd, rkbd, vbd, rvbd, qrhs_fn, rq, bias_fn, rbias, scale, numps, rnum, denps, rden, ET, rET, ecount):
        P = self.P
        NE = len(ET)
        pend = []
        groups = []
        cur, tot = [], 0
        for (jj, c0, n, ex) in steps:
            if cur and tot + n > 512:
                groups.append(cur)
                cur, tot = [], 0
            cur.append((jj, c0, n, ex, tot))
            tot += n
        if cur:
            groups.append(cur)

        def qk(grp):
            ps, rps = self.bank(self.POOL_ST)
            tot = 0
            for (jj, c0, n, ex, off) in grp:
                P.op('pe', I('matmul', ps[:, off:off + n], kbd[:, 128 * jj:128 * jj + 128], qrhs_fn(c0, n, ex), start=True, stop=False),
                     reads=[rkbd, rq], writes=[rps])
                P.op('pe', I('matmul', ps[:, off:off + n], self.IDB[:, :], bias_fn(c0, n, ex), start=False, stop=True),
                     reads=[rbias, self.rC], writes=[rps])
                tot = off + n
            es = ecount[0] % NE
            ecount[0] += 1
            P.op('act', I('activation', out=ET[es][:, 0:tot], in_=ps[:, 0:tot], func=AF.Exp, scale=scale), reads=[rps], writes=[rET[es]])
            return es

        def pv(grp, es):
            for (jj, c0, n, ex, off) in grp:
                P.op('pe', I('matmul', numps[:, c0:c0 + n], vbd[:, 128 * jj:128 * jj + 128], ET[es][:, off:off + n], start=False, stop=False, skip_group_check=True),
                     reads=list(rvbd) + [rET[es]], writes=[rnum])
                P.op('pe', I('matmul', denps[:, c0:c0 + n], self.ONESBD[:, :], ET[es][:, off:off + n], start=False, stop=False, skip_group_check=True),
                     reads=[self.rC, rET[es]], writes=[rden])

        for grp in groups:
            if len(pend) == 2:
                pv(*pend.pop(0))
            es = qk(grp)
            pend.append((grp, es))
        while pend:
            pv(*pend.pop(0))

    def mixer_b(self, si):
        P = self.P
        S = self.seq_lens[si]
        sm = self.smax
        self.areset()
        QT = self.ab(S)
        KT = self.ab(S)
        ACCN = self.af(S)
        ACCD = self.af(S)
        rQT, rKT, rACCN, rACCD = Res("bq"), Res("bk"), Res("accn"), Res("accd")
        NR = 3
        KBD = [self.ab(10 * 128) for _ in range(NR)]
        VBD = [self.ab(10 * 128) for _ in range(NR)]
        rKBD = [Res("kbd%d" % i) for i in range(NR)]
        rVBD = [[Res("vbd%d_%d" % (i, h)) for h in range(2)] for i in range(NR)]
        NE = 4
        ET = [self.ab(512) for _ in range(NE)]
        rET = [Res("bet%d" % i) for i in range(NE)]
        GT = [self.ab(512) for _ in range(2)]
        rGT = [Res("bgt%d" % i) for i in range(2)]
        OT = [self.ab(512) for _ in range(2)]
        rOT = [Res("bot%d" % i) for i in range(2)]
        rVL = [[Res("vl%d_%d" % (i, h)) for h in range(2)] for i in range(NR)]
        for i in range(NR):
            P.op('pool', I('memset', KBD[i], 0.0), writes=[rKBD[i]])
            P.op('pool', I('memset', VBD[i], 0.0), writes=rVBD[i])
        ecount = [0]
        bcount = [0]
        for c in range(4):
            P.op('sp', I('dma_start', out=QT, in_=dap(self.QB_T, c * 128 * sm, [[sm, 128], [1, S]])), reads=[self.rD["qb_t"]], writes=[rQT], dma=rQT)
            P.op('sp', I('dma_start', out=KT, in_=dap(self.KB_T, c * 128 * sm, [[sm, 128], [1, S]])), reads=[self.rD["kb_t"]], writes=[rKT], dma=rKT)
            P.op('pool', I('memset', ACCN, 0.0), writes=[rACCN])
            P.op('pool', I('memset', ACCD, 0.0), writes=[rACCD])
            blocks = []
            for dil in (1, 4, 16):
                L = S // dil
                QBW = min(512, L)
                for rho in range(dil):
                    for q0 in range(0, L, QBW):
                        blocks.append((dil, rho, q0, QBW, L))

            accd_ = {}

            def prolog(bi):
                dil, rho, q0, QBW, L = blocks[bi]
                slot = bi % NR
                accd_[bi] = self.acc_alloc(QBW)
                j0 = max(0, q0 // 64 - 1)
                j1 = min(L // 64, (q0 + QBW) // 64 + 1)
                nk = j1 - j0
                for hh in range(2):
                    srcap = KT[64 * hh:64 * hh + 64, :]
                    srcv = bass.AP(srcap.tensor, srcap.offset + rho + dil * 64 * j0, [list(srcap.ap[0]), [64 * dil, nk], [dil, 64]])
                    dstv = KBD[slot][64 * hh:64 * hh + 64, 0:nk * 128].rearrange("p (j x) -> p j x", j=nk)[:, :, 64 * hh:64 * hh + 64]
                    P.op('pool', I('tensor_copy', out=dstv, in_=srcv), reads=[rKT], writes=[rKBD[slot]])
                    vsrc = dap(self.VBd, (rho + dil * 64 * j0) * 512 + (2 * c + hh) * 64, [[dil * 512, 64], [64 * dil * 512, nk], [1, 64]])
                    vdst = VBD[slot][64 * hh:64 * hh + 64, 0:nk * 128].rearrange("p (j x) -> p j x", j=nk)[:, :, 64 * hh:64 * hh + 64]
                    P.op('sp', I('dma_start', out=vdst, in_=vsrc), reads=[self.rD["vb"]], writes=[rVBD[slot][hh]], dma=rVBD[slot][hh])

            def body(bi):
                dil, rho, q0, QBW, L = blocks[bi]
                slot = bi % NR
                j0 = max(0, q0 // 64 - 1)
                j1 = min(L // 64, (q0 + QBW) // 64 + 1)
                steps = []
                for j in range(j0, j1):
                    qa = max(q0, 64 * j - 64)
                    qe = min(q0 + QBW, 64 * j + 128, L)
                    if qe <= qa:
                        continue
                    steps.append((j - j0, qa - q0, qe - qa, (qa, qa - (64 * j - 64))))
                numps, rnum, denps, rden = accd_.pop(bi)

                def qrhs(c0, n, ex):
                    return bass.AP(QT.tensor, QT.offset + rho + dil * ex[0], [list(QT.ap[0]), [dil, n]])

                def biasf(c0, n, ex):
                    return self.MASKB[:, ex[1]:ex[1] + n]

                self.pair_attn_block(steps, KBD[slot], rKBD[slot], VBD[slot], rVBD[slot], qrhs, rQT, biasf, self.rC, 0.125,
                                     numps[:, 0:QBW], rnum, denps[:, 0:QBW], rden, ET, rET, ecount)
                accn = bass.AP(ACCN.tensor, ACCN.offset + rho + dil * q0, [list(ACCN.ap[0]), [dil, QBW]])
                accd = bass.AP(ACCD.tensor, ACCD.offset + rho + dil * q0, [list(ACCD.ap[0]), [dil, QBW]])
                P.op('dve', I('tensor_tensor', out=accn, in0=numps[:, 0:QBW], in1=accn, op=ALU.add), reads=[rnum, rACCN], writes=[rACCN])
                P.op('dve', I('tensor_tensor', out=accd, in0=denps[:, 0:QBW], in1=accd, op=ALU.add), reads=[rden, rACCD], writes=[rACCD])

            prolog(0)
            for bi in range(len(blocks)):
                if bi + 1 < len(blocks):
                    prolog(bi + 1)
                body(bi)
            for qb in range(S // 512):
                fb = bcount[0] % 2
                bcount[0] += 1
                P.op('sp', I('dma_start', out=GT[fb], in_=dap(self.GATE_T, (4 + c) * 128 * sm + 512 * qb, [[sm, 128], [1, 512]])),
                     reads=[self.rD["gate_t"]], writes=[rGT[fb]], dma=rGT[fb])
                sl = slice(512 * qb, 512 * qb + 512)
                P.op('dve', I('reciprocal', out=ACCD[:, sl], in_=ACCD[:, sl]), reads=[rACCD], writes=[rACCD])
                P.op('dve', I('tensor_tensor', out=ACCN[:, sl], in0=ACCN[:, sl], in1=ACCD[:, sl], op=ALU.mult), reads=[rACCD, rACCN], writes=[rACCN])
                P.op('pool', I('tensor_tensor', out=OT[fb], in0=ACCN[:, sl], in1=GT[fb], op=ALU.mult), reads=[rACCN, rGT[fb]], writes=[rOT[fb]])
                P.op('pool', I('dma_start', out=dap(self.MIX_T, (4 + c) * 128 * sm + 512 * qb, [[sm, 128], [1, 512]]), in_=OT[fb]),
                     reads=[rOT[fb]], writes=[self.rD["mix_t"]], dma=rOT[fb])

    def mixer_c(self, j, si):
        P = self.P
        S = self.seq_lens[si]
        sm = self.smax
        rows = S // 64
        self.areset()
        QT = self.ab(S)
        KT = self.ab(S)
        rQT, rKT = Res("cq"), Res("ck")
        BTF = self.af(960)
        BT = self.ab(960)
        rBTF, rBT = Res("btf"), Res("bt")
        NR = 3
        KBD = [self.ab(16 * 128) for _ in range(NR)]
        VBD = [self.ab(16 * 128) for _ in range(NR)]
        rKBD = [Res("ckbd%d" % i) for i in range(NR)]
        rVBD = [[Res("cvbd%d_%d" % (i, h)) for h in range(2)] for i in range(NR)]
        NE = 4
        ET = [self.ab(512) for _ in range(NE)]
        rET = [Res("cet%d" % i) for i in range(NE)]
        GT = [self.ab(512) for _ in range(2)]
        rGT = [Res("cgt%d" % i) for i in range(2)]
        RD = [self.af(512) for _ in range(2)]
        rRD = [Res("crd%d" % i) for i in range(2)]
        OT = [self.ab(512) for _ in range(2)]
        rOT = [Res("cot%d" % i) for i in range(2)]
        rVL = [[Res("cvl%d_%d" % (i, h)) for h in range(2)] for i in range(NR)]
        for i in range(NR):
            P.op('pool', I('memset', KBD[i], 0.0), writes=[rKBD[i]])
            P.op('pool', I('memset', VBD[i], 0.0), writes=rVBD[i])
        r0f = lambda r: min(max(r - 4, 0), rows - 8)
        nblk = rows // 8
        ecount = [0]
        fcount = [0]
        for c in range(6):
            P.op('sp', I('dma_start', out=QT, in_=dap(self.QC_T, c * 128 * sm, [[sm, 128], [1, S]])), reads=[self.rD["qc_t"]], writes=[rQT], dma=rQT)
            P.op('sp', I('dma_start', out=KT, in_=dap(self.KC_T, c * 128 * sm, [[sm, 128], [1, S]])), reads=[self.rD["kc_t"]], writes=[rKT], dma=rKT)
            P.op('sp', I('dma_start', out=BTF, in_=dap(self.RPBX, (j * 6 + c) * 128 * 960, [[960, 128], [1, 960]])), writes=[rBTF], dma=rBTF)
            P.op('dve', I('tensor_tensor', out=BT.rearrange("p (a q) -> p a q", a=15), in0=BTF.rearrange("p (a q) -> p a q", a=15),
                                                  in1=self.MASKC[:, :].unsqueeze(1).broadcast_to([128, 15, 64]), op=ALU.add),
                 reads=[rBTF, self.rC], writes=[rBT])

            def krange(b):
                R0 = 8 * b
                return r0f(R0), r0f(R0 + 7) + 8

            accd_ = {}

            def prolog(b):
                slot = b % NR
                accd_[b] = self.acc_alloc(512)
                k0, k1 = krange(b)
                nk = k1 - k0
                for hh in range(2):
                    srcv = KT[64 * hh:64 * hh + 64, 64 * k0:64 * k1].rearrange("p (j x) -> p j x", j=nk)
                    dstv = KBD[slot][64 * hh:64 * hh + 64, 0:nk * 128].rearrange("p (j x) -> p j x", j=nk)[:, :, 64 * hh:64 * hh + 64]
                    P.op('pool', I('tensor_copy', out=dstv, in_=srcv), reads=[rKT], writes=[rKBD[slot]])
                    vsrc = dap(self.VCd, 64 * k0 * 768 + (2 * c + hh) * 64, [[768, 64], [64 * 768, nk], [1, 64]])
                    vdst = VBD[slot][64 * hh:64 * hh + 64, 0:nk * 128].rearrange("p (j x) -> p j x", j=nk)[:, :, 64 * hh:64 * hh + 64]
                    P.op('sp', I('dma_start', out=vdst, in_=vsrc), reads=[self.rD["vc"]], writes=[rVBD[slot][hh]], dma=rVBD[slot][hh])

            def body(b):
                slot = b % NR
                R0 = 8 * b
                k0, k1 = krange(b)
                steps = []
                for kr in range(k0, k1):
                    valid = [r for r in range(R0, R0 + 8) if r0f(r) <= kr <= r0f(r) + 7]
                    if not valid:
                        continue
                    ra, rb = valid[0], valid[-1] + 1
                    assert valid == list(range(ra, rb))
                    e0 = ra - kr + 7
                    assert 0 <= e0 and e0 + (rb - ra) <= 15
                    steps.append((kr - k0, 64 * (ra - R0), 64 * (rb - ra), (64 * ra, e0)))
                numps, rnum, denps, rden = accd_.pop(b)

                def qrhs(c0, n, ex):
                    return QT[:, ex[0]:ex[0] + n]

                def biasf(c0, n, ex):
                    return BT[:, 64 * ex[1]:64 * ex[1] + n]

                self.pair_attn_block(steps, KBD[slot], rKBD[slot], VBD[slot], rVBD[slot], qrhs, rQT, biasf, rBT, 1.0,
                                     numps[:, :], rnum, denps[:, :], rden, ET, rET, ecount)
                fb = fcount[0] % 2
                fcount[0] += 1
                P.op('sp', I('dma_start', out=GT[fb], in_=dap(self.GATE_T, c * 128 * sm + 512 * b, [[sm, 128], [1, 512]])),
                     reads=[self.rD["gate_t"]], writes=[rGT[fb]], dma=rGT[fb])
                self.finalize_pair(numps[:, :], rnum, denps[:, :], rden, 512, GT[fb], rGT[fb], RD[fb], rRD[fb], OT[fb], rOT[fb])
                P.op('pool', I('dma_start', out=dap(self.MIX_T, c * 128 * sm + 512 * b, [[sm, 128], [1, 512]]), in_=OT[fb]),
                     reads=[rOT[fb]], writes=[self.rD["mix_t"]], dma=rOT[fb])

            prolog(0)
            for b in range(nblk):
                if b + 1 < nblk:
                    prolog(b + 1)
                body(b)

    def mixer_d(self, j, si):
        P = self.P
        S = self.seq_lens[si]
        sm = self.smax
        n1 = S // 64
        d1, tw, d2 = self.DFT1[S], self.TW[S], self.DFT2[S]
        C1 = d1[:, 0:n1]
        S1 = d1[:, n1:2 * n1]
        NS1 = d1[:, 2 * n1:3 * n1]
        self.areset()
        YB = [self.ab(8 * 512) for _ in range(2)]
        GP = [self.ab(8 * 512) for _ in range(2)]
        T1 = [self.af(256) for _ in range(2)]
        rYB = [Res("yb%d" % i) for i in range(2)]
        rGP = [Res("gp%d" % i) for i in range(2)]
        rT1 = [Res("t1%d" % i) for i in range(2)]
        for blk in range(8):
            b = blk % 2
            P.op('sp', I('dma_start', out=YB[b][0:n1, :].rearrange("p (s c) -> p s c", s=8),
                                                           in_=dap(self.YDd, blk * 8 * 512, [[64 * 512, n1], [512, 8], [1, 512]])),
                 reads=[self.rD["yd"]], writes=[rYB[b]], dma=rYB[b])
            yv = YB[b][0:n1, :].rearrange("p (s c) -> p s c", s=8)
            for pr in range(4):
                gr, rgr = self.bank()
                gi, rgi = self.bank()
                yr = yv[:, 2 * pr:2 * pr + 2, 0:256]
                yi = yv[:, 2 * pr:2 * pr + 2, 256:512]
                P.op('pe', I('matmul', gr[0:n1, :], C1, yr, start=True, stop=False), reads=[rYB[b], self.rC], writes=[rgr])
                P.op('pe', I('matmul', gr[0:n1, :], S1, yi, start=False, stop=True), reads=[rYB[b], self.rC], writes=[rgr])
                P.op('pe', I('matmul', gi[0:n1, :], C1, yi, start=True, stop=False), reads=[rYB[b], self.rC], writes=[rgi])
                P.op('pe', I('matmul', gi[0:n1, :], NS1, yr, start=False, stop=True), reads=[rYB[b], self.rC], writes=[rgi])
                for u in range(2):
                    s2 = blk * 8 + 2 * pr + u
                    tc_ = tw[:, s2:s2 + 1]
                    ts_ = tw[:, 64 + s2:64 + s2 + 1]
                    grs = gr[0:n1, 256 * u:256 * u + 256]
                    gis = gi[0:n1, 256 * u:256 * u + 256]
                    o0 = (2 * pr + u) * 512
                    t1a = T1[0][0:n1, :]
                    t1b = T1[1][0:n1, :]
                    P.op('dve', I('tensor_scalar', out=t1a, in0=gis, scalar1=ts_, scalar2=None, op0=ALU.mult),
                         reads=[rgi, self.rC], writes=[rT1[0]])
                    P.op('dve', I('scalar_tensor_tensor', out=GP[b][0:n1, o0:o0 + 256], in0=grs, scalar=tc_, in1=t1a,
                                                                                                       op0=ALU.mult, op1=ALU.add),
                         reads=[rgr, rT1[0], self.rC], writes=[rGP[b]])
                    P.op('dve', I('tensor_scalar', out=t1b, in0=grs, scalar1=ts_, scalar2=None, op0=ALU.mult),
                         reads=[rgr, self.rC], writes=[rT1[1]])
                    P.op('dve', I('scalar_tensor_tensor', out=GP[b][0:n1, o0 + 256:o0 + 512], in0=gis, scalar=tc_, in1=t1b,
                                                                                                       op0=ALU.mult, op1=ALU.subtract),
                         reads=[rgi, rT1[1], self.rC], writes=[rGP[b]])
            P.op('pool', I('dma_start', out=dap(self.GDd, blk * 8 * 512, [[64 * 512, n1], [1, 8 * 512]]), in_=GP[b][0:n1, :]),
                 reads=[rGP[b]], writes=[self.rD["gd"]], dma=rGP[b])
        self.bar()
        self.areset()
        FT = self.ab(2 * S)
        rFT = Res("ft")
        GB = [self.ab(8 * 512) for _ in range(2)]
        rGB = [Res("gb%d" % i) for i in range(2)]
        C2 = d2[:, 0:64]
        S2_ = d2[:, 64:128]
        for kb in range(n1 // 8):
            b = kb % 2
            P.op('sp', I('dma_start', out=GB[b][0:64, :].rearrange("p (k c) -> p k c", k=8),
                                                         in_=dap(self.GDd, kb * 8 * 64 * 512, [[512, 64], [64 * 512, 8], [1, 512]])),
                 reads=[self.rD["gd"]], writes=[rGB[b]], dma=rGB[b])
            for fc in range(2):
                ps, rps = self.bank()
                for ki in range(8):
                    gr_ = GB[b][0:64, ki * 512 + fc * 128:ki * 512 + fc * 128 + 128]
                    gi_ = GB[b][0:64, ki * 512 + 256 + fc * 128:ki * 512 + 256 + fc * 128 + 128]
                    P.op('pe', I('matmul', ps[:, 64 * ki:64 * ki + 64], gr_, C2, start=True, stop=False),
                         reads=[rGB[b], self.rC], writes=[rps])
                    P.op('pe', I('matmul', ps[:, 64 * ki:64 * ki + 64], gi_, S2_, start=False, stop=True),
                         reads=[rGB[b], self.rC], writes=[rps])
                dst = bass.AP(FT.tensor, FT.offset + fc * S + kb * 8, [list(FT.ap[0]), [1, 8], [n1, 64]])
                P.op('dve', I('tensor_copy', out=dst, in_=ps[:, :].rearrange("p (k q) -> p k q", k=8)), reads=[rps], writes=[rFT])
        LF = self.af(512)
        LB = self.ab(512)
        rLF, rLB = Res("lf"), Res("lb")
        P.op('sp', I('dma_start', out=LF.rearrange("p (c x) -> p c x", c=2), in_=dap(self.LIND, j * 256 * 256, [[256, 128], [128 * 256, 2], [1, 256]])),
             writes=[rLF], dma=rLF)
        P.op('dve', I('tensor_copy', out=LB, in_=LF), reads=[rLF], writes=[rLB])
        GT = [self.ab(512) for _ in range(2)]
        rGT = [Res("dgt%d" % i) for i in range(2)]
        OT = [self.ab(512) for _ in range(2)]
        rOT = [Res("dot%d" % i) for i in range(2)]
        cnt = 0
        for ec in range(2):
            for kb in range(S // 512):
                fb = cnt % 2
                cnt += 1
                P.op('sp', I('dma_start', out=GT[fb], in_=dap(self.GATE_T, (6 + ec) * 128 * sm + 512 * kb, [[sm, 128], [1, 512]])),
                     reads=[self.rD["gate_t"]], writes=[rGT[fb]], dma=rGT[fb])
                ps, rps = self.bank()
                for fc in range(2):
                    P.op('pe', I('matmul', ps[:, :], LB[:, fc * 256 + ec * 128:fc * 256 + ec * 128 + 128],
                                                                             FT[:, fc * S + 512 * kb:fc * S + 512 * kb + 512], start=(fc == 0), stop=(fc == 1)),
                         reads=[rLB, rFT], writes=[rps])
                P.op('dve', I('tensor_tensor', out=OT[fb], in0=ps[:, :], in1=GT[fb], op=ALU.mult), reads=[rps, rGT[fb]], writes=[rOT[fb]])
                P.op('pool', I('dma_start', out=dap(self.MIX_T, (6 + ec) * 128 * sm + 512 * kb, [[sm, 128], [1, 512]]), in_=OT[fb]),
                     reads=[rOT[fb]], writes=[self.rD["mix_t"]], dma=rOT[fb])

    def outproj(self, WOD, j, si, src, rsrc, dst, rdst):
        P = self.P
        S = self.seq_lens[si]
        sm = self.smax
        row0 = self.row0[si]
        self.areset()
        WO = self.ab(8 * 1024)
        rWO = Res("wo")
        st = [self.af(1024) for _ in range(2)]
        rst = [Res("wos0"), Res("wos1")]
        for k in range(8):
            b = k % 2
            P.op('sp', I('dma_start', out=st[b], in_=dap(WOD, (j * D + 128 * k) * D, [[D, 128], [1, D]])), writes=[rst[b]], dma=rst[b])
            P.op('pool' if k % 2 else 'dve', I('tensor_copy', out=WO[:, k * 1024:(k + 1) * 1024], in_=st[b]), reads=[rst[b]], writes=[rWO])
        MT = [self.ab(8 * 512) for _ in range(2)]
        rMT = [Res("mt%d" % i) for i in range(2)]
        XT = [self.af(1024) for _ in range(2)]
        rXT = [Res("oxt%d" % i) for i in range(2)]
        YT = [self.af(1024) for _ in range(2)]
        rYT = [Res("oyt%d" % i) for i in range(2)]
        SS = [self.af(8) for _ in range(2)]
        rSS = [Res("oss%d" % i) for i in range(2)]
        ntile = S // 128

        def loadg(g):
            gb = g % 2
            P.op('sp', I('dma_start', out=MT[gb].rearrange("p (c t) -> p c t", c=8), in_=dap(self.MIX_T, 512 * g, [[sm, 128], [128 * sm, 8], [1, 512]])),
                 reads=[self.rD["mix_t"]], writes=[rMT[gb]], dma=rMT[gb])

        loadg(0)
        for it in range(ntile):
            g, sub = it // 4, it % 4
            gb = g % 2
            b = it % 2
            if sub == 0 and g + 1 < S // 512:
                loadg(g + 1)
            r = row0 + 128 * it
            P.op('sp', I('dma_start', out=XT[b], in_=dap(src, r * D, [[D, 128], [1, D]])), reads=[rsrc], writes=[rXT[b]], dma=rXT[b])
            banks = [self.bank(), self.bank()]
            for nb in range(2):
                ps, rps = banks[nb]
                for c in range(8):
                    P.op('pe', I('matmul', ps[:, :], MT[gb][:, c * 512 + sub * 128:c * 512 + sub * 128 + 128],
                                                                                     WO[:, c * 1024 + nb * 512:c * 1024 + nb * 512 + 512],
                                                                                     start=(c == 0), stop=(c == 7)),
                         reads=[rMT[gb], rWO], writes=[rps])
                P.op('act', I('activation', out=self.JUNK[:, 0:512], in_=ps[:, :], func=AF.Square, accum_out=SS[b][:, nb:nb + 1]),
                     reads=[rps], writes=[rSS[b]])
            P.op('dve', I('tensor_tensor', out=SS[b][:, 2:3], in0=SS[b][:, 0:1], in1=SS[b][:, 1:2], op=ALU.add), reads=[rSS[b]], writes=[rSS[b]])
            P.op('act', I('activation', out=SS[b][:, 3:4], in_=SS[b][:, 2:3], func=AF.Sqrt, bias=self.EPST[:, 0:1], scale=1.0 / D),
                 reads=[rSS[b], self.rC], writes=[rSS[b]])
            P.op('dve', I('reciprocal', out=SS[b][:, 4:5], in_=SS[b][:, 3:4]), reads=[rSS[b]], writes=[rSS[b]])
            for nb in range(2):
                ps, rps = banks[nb]
                sl = slice(nb * 512, nb * 512 + 512)
                P.op('dve', I('scalar_tensor_tensor', out=YT[b][:, sl], in0=ps[:, :], scalar=SS[b][:, 4:5],
                                                                                       in1=self.ABC[:, 2048 + nb * 512:2048 + nb * 512 + 512],
                                                                                       op0=ALU.mult, op1=ALU.mult),
                     reads=[rps, rSS[b], self.rABC], writes=[rYT[b]])
            P.op('pool', I('tensor_tensor', out=YT[b], in0=YT[b], in1=XT[b], op=ALU.add), reads=[rYT[b], rXT[b]], writes=[rYT[b]])
            P.op('pool', I('dma_start', out=dap(dst, r * D, [[D, 128], [1, D]]), in_=YT[b]), reads=[rYT[b]], writes=[rdst], dma=rYT[b])


_W_EVEN_PERM = None


def _even_perm():
    idx = list(range(0, 768)) + list(range(1280, 2816)) + list(range(768, 1280)) + list(range(2816, 3328))
    return np.array(idx)


def _odd_perm():
    idx = list(range(1536, 2304)) + list(range(0, 1536)) + list(range(2304, 3072)) + list(range(3328, 3584)) + list(range(3072, 3328))
    return np.array(idx)


def _rpb_expand(rpb):
    n_odd = rpb.shape[0]
    kc = np.arange(64)[:, None]
    qc = np.arange(64)[None, :]
    cidx = np.clip(kc - qc + 15, 0, 30)
    out = np.zeros((n_odd, 6, 128, 15, 64), np.float32)
    for e in range(15):
        dr = 14 - e
        g = rpb[:, :, dr, :][:, :, cidx]
        g = g.reshape(n_odd, 6, 2, 64, 64).reshape(n_odd, 6, 128, 64)
        out[:, :, :, e, :] = g
    return np.ascontiguousarray(out.reshape(n_odd, 6, 128, 960))


def make_shared_inputs(seq_lens, pre_g, post_g, ada_w, ada_b, w_in_ab, w_out_ab, qn_a, kn_a, w_in_cd, w_out_cd, rpb_c, lin_d):
    f = lambda a: np.ascontiguousarray(np.asarray(a, dtype=np.float32))
    smax = max(seq_lens)
    ns = len(seq_lens)
    idb, onesbd, maskb, maskc, cs64 = const_tables()
    sel = np.zeros((ns, ns * 128), np.float32)
    for i in range(ns):
        sel[i, i * 128:(i + 1) * 128] = 1.0
    m = {
        "preg": f(pre_g), "postg": f(post_g), "adaw": f(ada_w), "adab": f(ada_b),
        "wab": f(np.asarray(w_in_ab)[:, :, _even_perm()]), "woab": f(w_out_ab),
        "qkn": f(np.concatenate([np.asarray(qn_a), np.asarray(kn_a)], axis=1)),
        "wcd": f(np.asarray(w_in_cd)[:, :, _odd_perm()]), "wocd": f(w_out_cd),
        "rpbx": _rpb_expand(np.asarray(rpb_c, dtype=np.float32)), "lind": f(lin_d),
        "rope": rope_table(smax), "idb": idb, "onesbd": onesbd, "sel": sel, "maskb": maskb, "maskc": maskc, "cs64": cs64,
    }
    for S in sorted(set(seq_lens)):
        d1, tw, d2 = dft_tables(S)
        m["dft1_%d" % S] = d1
        m["tw_%d" % S] = tw
        m["dft2_%d" % S] = d2
    return m


def core_inputs(shared, xs, cs):
    ns = len(xs)
    m = dict(shared)
    m["xin"] = np.ascontiguousarray(np.concatenate(xs, axis=0).astype(np.float32))
    c = np.stack(cs, axis=0).astype(np.float32)
    ct = c.reshape(ns, 8, 128).transpose(2, 1, 0).reshape(128, 8 * ns)
    m["ct"] = np.ascontiguousarray(ct)
    return m


_NC_CACHE = {}


def kernel(x_prompt, x_sample, c_prompt, c_sample, pre_g, post_g, ada_w, ada_b,
           w_in_ab, w_out_ab, qn_a, kn_a, w_in_cd, w_out_cd, rpb_c, lin_d):
    x_prompt = np.asarray(x_prompt)
    x_sample = np.asarray(x_sample)
    c_prompt = np.asarray(c_prompt)
    c_sample = np.asarray(c_sample)
    ncores = 8
    sp, ss = x_prompt.shape[1], x_sample.shape[1]
    seq_lens = [sp, sp, ss]
    depth = np.asarray(pre_g).shape[0]
    key = (tuple(seq_lens), depth)
    if key not in _NC_CACHE:
        _NC_CACHE[key] = Builder(seq_lens, depth, (depth + 1) // 2, depth // 2).build()
    nc = _NC_CACHE[key]
    shared = make_shared_inputs(seq_lens, pre_g, post_g, ada_w, ada_b, w_in_ab, w_out_ab, qn_a, kn_a, w_in_cd, w_out_cd, rpb_c, lin_d)
    in_maps = []
    for c in range(ncores):
        in_maps.append(core_inputs(shared, [x_prompt[2 * c], x_prompt[2 * c + 1], x_sample[c]],
                                   [c_prompt[2 * c], c_prompt[2 * c + 1], c_sample[c]]))
    res = run_bass_kernel_spmd(nc, in_maps, core_ids=list(range(ncores)))
    y_prompt = np.empty(x_prompt.shape, np.float32)
    y_sample = np.empty(x_sample.shape, np.float32)
    for c in range(ncores):
        y = np.asarray(res.results[c]["yout"])
        y_prompt[2 * c] = y[0:sp]
        y_prompt[2 * c + 1] = y[sp:2 * sp]
        y_sample[c] = y[2 * sp:2 * sp + ss]
    return (y_prompt, y_sample)
```

```python
import math
from contextlib import ExitStack

import numpy as np
import ml_dtypes
import concourse.bass as bass
import concourse.mybir as mybir
from concourse.bass_utils import run_bass_kernel_spmd

F32 = mybir.dt.float32
BF16 = mybir.dt.bfloat16
AF = mybir.ActivationFunctionType
ALU = mybir.AluOpType
AX = mybir.AxisListType
NPBF = ml_dtypes.bfloat16

ENGS = ['pe', 'act', 'dve', 'pool', 'sp']
D = 1024
EPS = 1e-6
NEG = -30000.0


class Res:
    __slots__ = ('name', 'writers', 'readers', 'multi', 'sem', 'semval')

    def __init__(self, name, multi=False):
        self.name = name
        self.writers = {}
        self.readers = {}
        self.multi = multi
        self.sem = {}
        self.semval = {}


class Op:
    __slots__ = ('eng', 'fn', 'deps', 'needsig', 'cnt', 'isdma', 'sem', 'semval', 'key', 'idx')


class Prog:
    def __init__(self, nc, stack, same_engine_sync=True):
        self.nc = nc
        self.stack = stack
        self.ops = {e: [] for e in ENGS}
        self.esem = {e: stack.enter_context(nc.semaphore('es_' + e)) for e in ENGS}
        self.same = same_engine_sync
        self.nosame = ('act', 'dve')
        self.nsem = 0
        self.free_sems = {}
        self.phase_res = []
        self.all_sems = []
        self.last = {}
        self.pending = {}
        self.n = 0
        self.nwaits = {}

    def _dma_sem(self, r, eng):
        if eng not in r.sem:
            fl = self.free_sems.setdefault(eng, [])
            if fl:
                r.sem[eng], r.semval[eng] = fl.pop()
            else:
                r.sem[eng] = self.stack.enter_context(self.nc.semaphore('ds%d' % self.nsem))
                r.semval[eng] = 0
                self.nsem += 1
            self.phase_res.append((r, eng))
        return r.sem[eng]

    def op(self, eng, fn, reads=(), writes=(), dma=None, extra=()):
        o = Op()
        o.eng = eng
        o.fn = fn
        o.needsig = False
        o.cnt = 0
        o.isdma = dma is not None
        o.idx = self.n
        self.n += 1
        deps = {}

        def add(d):
            if d is o:
                return
            if (not d.isdma) and d.eng == eng and (eng == 'pe' or (not self.same and eng in self.nosame)):
                return
            k = d.key
            if k not in deps or deps[k].idx < d.idx:
                deps[k] = d

        for d in extra:
            add(d)
        for r in reads:
            if r is None:
                continue
            for d in r.writers.values():
                add(d)
        for w in writes:
            if w is None:
                continue
            for d in w.readers.values():
                add(d)
            if not w.multi and not w.readers:
                for d in w.writers.values():
                    add(d)
        if o.isdma:
            o.sem = self._dma_sem(dma, eng)
            dma.semval[eng] += 16
            o.semval = dma.semval[eng]
            o.key = (eng, id(o.sem))
            self.pending[id(o.sem)] = o
        else:
            o.key = eng
            self.last[eng] = o
        for r in reads:
            if r is None:
                continue
            r.readers[o.key] = o
        for w in writes:
            if w is None:
                continue
            if w.multi:
                if w.readers:
                    w.writers = {o.key: o}
                    w.readers = {}
                else:
                    w.writers[o.key] = o
            else:
                w.writers = {o.key: o}
                w.readers = {}
        o.deps = list(deps.values())
        for d in o.deps:
            d.needsig = True
        self.ops[eng].append(o)
        return o

    def barrier(self, scratch_ap):
        deps = list(self.last.values()) + list(self.pending.values())
        b = self.op('pool', I('memset', scratch_ap, 0.0), extra=deps)
        for en in ('pe', 'act', 'dve', 'sp'):
            self.op(en, None, extra=[b])
        for r, en in self.phase_res:
            self.free_sems.setdefault(en, []).append((r.sem.pop(en), r.semval.pop(en)))
        self.phase_res = []
        self.pending = {}
        self.last = {'pool': b}

    def finalize(self):
        nc = self.nc
        for e in ENGS:
            c = 0
            for o in self.ops[e]:
                if o.isdma or o.fn is None:
                    continue
                if o.needsig:
                    c += 1
                    o.cnt = c
        prog = self
        final_waits = [(o.sem, o.semval) for o in self.pending.values()]

        def emit(engname, eobj):
            waited = {}
            for o in prog.ops[engname]:
                need = {}
                for d in o.deps:
                    if d.isdma:
                        s, v = d.sem, d.semval
                    else:
                        if d.fn is None:
                            continue
                        s, v = prog.esem[d.eng], d.cnt
                    k = id(s)
                    if k not in need or need[k][1] < v:
                        need[k] = (s, v)
                for k, (s, v) in need.items():
                    if waited.get(k, 0) >= v:
                        continue
                    waited[k] = v
                    eobj.wait_ge(s, v)
                    prog.nwaits[engname] = prog.nwaits.get(engname, 0) + 1
                if o.fn is None:
                    continue
                ins = o.fn(eobj)
                if o.isdma:
                    ins.then_inc(o.sem, 16)
                elif o.needsig:
                    ins.then_inc(prog.esem[engname], 1)

        with nc.Block() as block:
            @block.tensor
            def _(e):
                emit('pe', e)

            @block.scalar
            def _(e):
                emit('act', e)

            @block.vector
            def _(e):
                emit('dve', e)

            @block.gpsimd
            def _(e):
                emit('pool', e)
                for s, v in final_waits:
                    e.wait_ge(s, v)

            @block.sync
            def _(e):
                emit('sp', e)


def I(name, *a, **k):
    return lambda e: getattr(e, name)(*a, **k)


def dap(t, off, dims):
    return bass.AP(t, off, [list(d) for d in dims])


def rope_table(smax):
    t = np.arange(smax)
    row = (t // 64).astype(np.float32)
    col = (t % 64).astype(np.float32)
    inv_a = (10000.0 ** (-np.arange(0, 32, 2, dtype=np.float32) / 32)).astype(np.float32)
    inv_b = (500000.0 ** (-np.arange(0, 16, 2, dtype=np.float32) / 16)).astype(np.float32)
    ar = row[:, None] * inv_a[None, :]
    ac = col[:, None] * inv_a[None, :]
    ab = t.astype(np.float32)[:, None] * inv_b[None, :]
    tab = np.concatenate([np.cos(ar), np.cos(ac), np.sin(ar), np.sin(ac), np.cos(ab), np.sin(ab)], axis=1)
    return tab.astype(np.float32)


def dft_tables(S):
    n1 = S // 64
    s1 = np.arange(n1, dtype=np.float64)
    ang1 = 2 * np.pi * np.outer(s1, s1) / n1
    nrm = 1.0 / math.sqrt(S * 64.0)
    c1 = np.cos(ang1) * nrm
    sn1 = np.sin(ang1) * nrm
    d1 = np.concatenate([c1, sn1, -sn1], axis=1).astype(NPBF)
    k1 = np.arange(n1, dtype=np.float64)
    s2 = np.arange(64, dtype=np.float64)
    angt = 2 * np.pi * np.outer(k1, s2) / S
    tw = np.concatenate([np.cos(angt), np.sin(angt)], axis=1).astype(np.float32)
    ang2 = 2 * np.pi * np.outer(s2, s2) / 64.0
    d2 = np.concatenate([np.cos(ang2), np.sin(ang2)], axis=1).astype(NPBF)
    return d1, tw, d2


def const_tables():
    idb = np.eye(128, dtype=np.float32).astype(NPBF)
    onesbd = np.zeros((128, 128), np.float32)
    onesbd[:64, :64] = 1
    onesbd[64:, 64:] = 1
    kk = np.arange(64)[:, None]
    qq = np.arange(192)[None, :]
    mb = np.where((kk <= qq) & (qq <= kk + 128), 0.0, NEG).astype(np.float32)
    maskb = np.concatenate([mb, mb], axis=0).astype(NPBF)
    cols = np.arange(64)
    cstart = np.clip(cols - 8, 0, 48)
    kc = np.arange(64)[:, None]
    qc = np.arange(64)[None, :]
    mc = np.where((kc >= cstart[None, :]) & (kc < cstart[None, :] + 16), 0.0, NEG).astype(np.float32)
    maskc = np.concatenate([mc, mc], axis=0)
    c = np.arange(64, dtype=np.float64)
    ang = 2 * np.pi * np.outer(c, c) / 64.0
    cs = np.zeros((128, 256), np.float64)
    for g in range(2):
        cs[64 * g:64 * g + 64, 64 * g:64 * g + 64] = np.cos(ang)
        cs[64 * g:64 * g + 64, 128 + 64 * g:128 + 64 * g + 64] = -np.sin(ang)
    return idb, onesbd.astype(NPBF), maskb, maskc, cs.astype(NPBF)


class Builder:
    POOL_ACC = (0, 1, 2, 3)
    POOL_ST = (4, 5, 6, 7)

    def __init__(self, seq_lens, depth, n_even, n_odd, same_sync=True):
        self.seq_lens = list(seq_lens)
        self.depth = depth
        self.n_even = n_even
        self.n_odd = n_odd
        self.nseq = len(seq_lens)
        self.ntok = sum(seq_lens)
        self.smax = max(seq_lens)
        self.svals = sorted(set(seq_lens))
        self.row0 = [sum(seq_lens[:i]) for i in range(self.nseq)]
        self.same_sync = same_sync

    def areset(self):
        self.aoff = 0

    def af(self, n):
        o = self.aoff
        self.aoff += n
        assert self.aoff <= self.NA, ("arena overflow", self.aoff, self.NA)
        return self.AR[:, o:o + n]

    def ab(self, n):
        assert n % 2 == 0
        o = self.aoff
        self.aoff += n // 2
        assert self.aoff <= self.NA, ("arena overflow", self.aoff, self.NA)
        return self.AR[:, o:o + n // 2].bitcast(BF16)

    def bank(self, pool=None):
        if pool is None:
            i = self.bank_i
            self.bank_i = (self.bank_i + 1) % 8
        else:
            k = self.pool_i.get(pool, 0)
            self.pool_i[pool] = k + 1
            i = pool[k % len(pool)]
        return self.PS[i], self.rPS[i]

    def R(self, name, multi=False):
        return Res(name, multi)

    def bar(self):
        self.P.barrier(self.BSCR[:, 0:1])

    def build(self):
        nc = bass.Bass("TRN2", target_bir_lowering=False)
        self.nc = nc
        ns, nt, sm = self.nseq, self.ntok, self.smax
        dt = nc.dram_tensor
        X = dict(kind="ExternalInput")
        self.XIN = dt("xin", [nt, D], F32, **X)
        self.CTd = dt("ct", [128, 8 * ns], F32, **X)
        self.PREG = dt("preg", [self.depth, D], F32, **X)
        self.POSTG = dt("postg", [self.depth, D], F32, **X)
        self.ADAW = dt("adaw", [self.depth, D, 3 * D], F32, **X)
        self.ADAB = dt("adab", [self.depth, 3 * D], F32, **X)
        self.WAB = dt("wab", [max(self.n_even, 1), D, 3328], F32, **X)
        self.WOAB = dt("woab", [max(self.n_even, 1), D, D], F32, **X)
        self.QKN = dt("qkn", [max(self.n_even, 1), 128], F32, **X)
        self.WCD = dt("wcd", [max(self.n_odd, 1), D, 3584], F32, **X)
        self.WOCD = dt("wocd", [max(self.n_odd, 1), D, D], F32, **X)
        self.RPBX = dt("rpbx", [max(self.n_odd, 1), 6, 128, 960], F32, **X)
        self.LIND = dt("lind", [max(self.n_odd, 1), 256, 256], F32, **X)
        self.ROPE = dt("rope", [sm, 80], F32, **X)
        self.IDBd = dt("idb", [128, 128], BF16, **X)
        self.ONESd = dt("onesbd", [128, 128], BF16, **X)
        self.SELd = dt("sel", [ns, ns * 128], F32, **X)
        self.MASKBd = dt("maskb", [128, 192], BF16, **X)
        self.MASKCd = dt("maskc", [128, 64], F32, **X)
        self.CS64d = dt("cs64", [128, 256], BF16, **X)
        self.DFT1d, self.TWd, self.DFT2d = {}, {}, {}
        for S in self.svals:
            n1 = S // 64
            self.DFT1d[S] = dt("dft1_%d" % S, [n1, 3 * n1], BF16, **X)
            self.TWd[S] = dt("tw_%d" % S, [n1, 128], F32, **X)
            self.DFT2d[S] = dt("dft2_%d" % S, [64, 128], BF16, **X)
        self.YOUT = dt("yout", [nt, D], F32, kind="ExternalOutput")
        self.XS = dt("xs", [nt, D], F32)
        self.QA_T = dt("qa_t", [4, 128, sm], BF16)
        self.KA_T = dt("ka_t", [128, sm], BF16)
        self.VAUGd = dt("vaug", [sm, 384], BF16)
        self.GATE_T = dt("gate_t", [8, 128, sm], BF16)
        self.QB_T = dt("qb_t", [4, 128, sm], BF16)
        self.KB_T = dt("kb_t", [4, 128, sm], BF16)
        self.VBd = dt("vb", [sm, 512], BF16)
        self.MIX_T = dt("mix_t", [8, 128, sm], BF16)
        self.QC_T = dt("qc_t", [6, 128, sm], BF16)
        self.KC_T = dt("kc_t", [6, 128, sm], BF16)
        self.VCd = dt("vc", [sm, 768], BF16)
        self.YDd = dt("yd", [sm, 512], BF16)
        self.GDd = dt("gd", [128, 64, 512], BF16)
        self.MODD = dt("modd", [ns, 3 * D], F32)

        with ExitStack() as st:
            self.P = Prog(nc, st, self.same_sync)
            sbt = lambda n, s, d: st.enter_context(nc.sbuf_tensor(n, s, d))
            self.NA = 40960
            self.AR = sbt("arena", [128, self.NA], F32)
            self.IDB = sbt("idb_s", [128, 128], BF16)
            self.ONESBD = sbt("ones_s", [128, 128], BF16)
            self.MASKB = sbt("maskb_s", [128, 192], BF16)
            self.MASKC = sbt("maskc_s", [128, 64], F32)
            self.CS64 = sbt("cs64_s", [128, 256], BF16)
            self.SEL = sbt("sel_s", [ns, ns * 128], F32)
            self.CT = sbt("ct_s", [128, 8 * ns], F32)
            self.QKNB = sbt("qkn_s", [128, 128], F32)
            self.ABC = sbt("abc", [128, 3 * D], F32)
            self.EPST = sbt("epst", [128, 1], F32)
            self.BSCR = sbt("bscr", [128, 2], F32)
            self.NEGH = sbt("negh", [128, 16], F32)
            self.JUNK = sbt("junk", [128, 1024], BF16)
            self.DFT1, self.TW, self.DFT2 = {}, {}, {}
            for S in self.svals:
                n1 = S // 64
                self.DFT1[S] = sbt("dft1s_%d" % S, [n1, 3 * n1], BF16)
                self.TW[S] = sbt("tws_%d" % S, [n1, 128], F32)
                self.DFT2[S] = sbt("dft2s_%d" % S, [64, 128], BF16)
            self.PS2 = [st.enter_context(nc.psum_tensor("ps%d" % i, [128, 1024], F32)) for i in range(4)]
            self.PS = [self.PS2[i // 2][:, (i % 2) * 512:(i % 2 + 1) * 512] for i in range(8)]
            self.st2_i = 0
            self.rPS = [Res("ps%d" % i) for i in range(8)]
            self.bank_i = 0
            self.pool_i = {}
            self.rD = {n: Res(n, multi=True) for n in
                       ["modd", "xs", "yout", "qa_t", "ka_t", "vaug", "gate_t", "qb_t", "kb_t", "vb", "mix_t",
                        "qc_t", "kc_t", "vc", "yd", "gd"]}
            self.rC = Res("consts")
            self.rMOD = Res("modrow")
            self.rABC = Res("abc")
            self.rQKN = Res("qkn")

            self.setup()
            prev = (self.XIN, None)
            for l in range(self.depth):
                src, rsrc = prev
                if (self.depth - 1 - l) % 2 == 0:
                    dst, rdst = self.YOUT, self.rD["yout"]
                else:
                    dst, rdst = self.XS, self.rD["xs"]
                prev = (dst, rdst)
                self.adaln(l)
                for si in range(self.nseq):
                    S = self.seq_lens[si]
                    self.modprep(l, si)
                    if l % 2 == 0:
                        j = l // 2
                        self.load_w(self.WAB, j, 3328)
                        self.proj(l, si, src, rsrc, even=True, j=j)
                        self.bar()
                        self.mixer_a(si)
                        self.bar()
                        self.mixer_b(si)
                        self.bar()
                        self.outproj(self.WOAB, j, si, src, rsrc, dst, rdst)
                        self.bar()
                    else:
                        j = l // 2
                        self.load_w(self.WCD, j, 3584)
                        self.proj(l, si, src, rsrc, even=False, j=j)
                        self.bar()
                        self.mixer_c(j, si)
                        self.bar()
                        self.mixer_d(j, si)
                        self.bar()
                        self.outproj(self.WOCD, j, si, src, rsrc, dst, rdst)
                        self.bar()
            self.P.finalize()
        return nc

    def setup(self):
        P = self.P
        rc = self.rC
        ld = lambda t, d: P.op('sp', I('dma_start', out=t[:], in_=d.ap()), writes=[rc], dma=Res("c"))
        ld(self.IDB, self.IDBd)
        ld(self.ONESBD, self.ONESd)
        ld(self.MASKB, self.MASKBd)
        ld(self.MASKC, self.MASKCd)
        ld(self.CS64, self.CS64d)
        ld(self.SEL, self.SELd)
        ld(self.CT, self.CTd)
        for S in self.svals:
            ld(self.DFT1[S], self.DFT1d[S])
            ld(self.TW[S], self.TWd[S])
            ld(self.DFT2[S], self.DFT2d[S])
        P.op('pool', I('memset', self.EPST[:], EPS), writes=[rc])
        P.op('pool', I('memset', self.BSCR[:], 0.0), writes=[rc])
        P.op('pool', I('memset', self.NEGH[:], -0.5), writes=[rc])
        self.bar()
        P.op('act', I('activation', out=self.CT[:], in_=self.CT[:], func=AF.Silu), writes=[rc])
        self.bar()

    def adaln(self, l):
        P = self.P
        ns = self.nseq
        self.areset()
        MR = self.af(3072)
        rMR = Res("mr")
        wts = [self.af(3072) for _ in range(8)]
        rw = [Res("adaw%d" % k) for k in range(8)]
        bb = self.af(3072)
        rb = Res("adab")
        for k in range(8):
            P.op('sp', I('dma_start', out=wts[k], in_=dap(self.ADAW, (l * D + 128 * k) * 3 * D, [[3 * D, 128], [1, 3 * D]])),
                 writes=[rw[k]], dma=rw[k])
        P.op('sp', I('dma_start', out=bb[0:ns, :], in_=dap(self.ADAB, l * 3 * D, [[0, ns], [1, 3 * D]])),
             writes=[rb], dma=rb)
        for half in range(2):
            banks = [self.bank() for _ in range(3)]
            for k in range(8):
                for b in range(3):
                    c0 = half * 1536 + b * 512
                    ps, rps = banks[b]
                    P.op('pe', I('matmul', ps[0:ns, :], self.CT[:, k * ns:(k + 1) * ns], wts[k][:, c0:c0 + 512],
                                                                     start=(k == 0), stop=(k == 7)),
                         reads=[rw[k], self.rC], writes=[rps])
            for b in range(3):
                c0 = half * 1536 + b * 512
                ps, rps = banks[b]
                P.op('dve', I('tensor_tensor', out=MR[0:ns, c0:c0 + 512], in0=ps[0:ns, :], in1=bb[0:ns, c0:c0 + 512], op=ALU.add),
                     reads=[rps, rb], writes=[rMR])
        P.op('pool', I('dma_start', out=self.MODD.ap(), in_=MR[0:ns, :]), reads=[rMR], writes=[self.rD["modd"]], dma=rMR)
        self.bar()

    def modprep(self, l, si):
        P = self.P
        ns = self.nseq
        self.areset()
        pg = self.af(1024)
        qg = self.af(1024)
        rpg, rqg = Res("pg"), Res("qg")
        P.op('sp', I('dma_start', out=pg, in_=dap(self.PREG, l * D, [[0, 128], [1, D]])), writes=[rpg], dma=rpg)
        P.op('sp', I('dma_start', out=qg, in_=dap(self.POSTG, l * D, [[0, 128], [1, D]])), writes=[rqg], dma=rqg)
        MR = self.af(3072)
        rMR = Res("mr")
        P.op('sp', I('dma_start', out=MR[0:ns, :], in_=self.MODD.ap()), reads=[self.rD["modd"]], writes=[rMR], dma=rMR)
        sel = self.SEL[:, si * 128:(si + 1) * 128]
        for part in range(3):
            for h in range(2):
                ps, rps = self.bank()
                c0 = part * 1024 + h * 512
                P.op('pe', I('matmul', ps[:, :], sel, MR[0:ns, c0:c0 + 512], start=True, stop=True),
                     reads=[rMR, self.rC], writes=[rps])
                if part == 0:
                    P.op('dve', I('tensor_copy', out=self.ABC[:, 1024 + h * 512:1024 + (h + 1) * 512], in_=ps[:, :]),
                         reads=[rps], writes=[self.rABC])
                elif part == 1:
                    P.op('dve', I('scalar_tensor_tensor', out=self.ABC[:, h * 512:(h + 1) * 512], in0=ps[:, :], scalar=1.0,
                                                                             in1=pg[:, h * 512:(h + 1) * 512], op0=ALU.add, op1=ALU.mult),
                         reads=[rps, rpg], writes=[self.rABC])
                else:
                    P.op('dve', I('tensor_tensor', out=self.ABC[:, 2048 + h * 512:2048 + (h + 1) * 512], in0=ps[:, :],
                                                                      in1=qg[:, h * 512:(h + 1) * 512], op=ALU.mult),
                         reads=[rps, rqg], writes=[self.rABC])
        self.bar()

    def load_w(self, WD, j, ncol):
        P = self.P
        self.areset()
        self.WB = self.ab(8 * ncol)
        self.wncol = ncol
        self.rWB = Res("wb")
        mark = self.aoff
        st = [self.af(ncol) for _ in range(2)]
        rst = [Res("wst0"), Res("wst1")]
        engs = ['pool', 'act', 'dve']
        for k in range(8):
            b = k % 2
            P.op('sp', I('dma_start', out=st[b], in_=dap(WD, (j * D + 128 * k) * ncol, [[ncol, 128], [1, ncol]])),
                 writes=[rst[b]], dma=rst[b])
            en = engs[k % 3]
            dst = self.WB[:, k * ncol:(k + 1) * ncol]
            if en == 'act':
                P.op('act', I('copy', out=dst, in_=st[b]), reads=[rst[b]], writes=[self.rWB])
            else:
                P.op(en, I('tensor_copy', out=dst, in_=st[b]), reads=[rst[b]], writes=[self.rWB])
        self.bar()
        self.aoff = mark

    def norm_tile(self, xt, rxt, hb, rhb, hf, rhf, ss, rss):
        P = self.P
        P.op('act', I('activation', out=self.JUNK[:], in_=xt, func=AF.Square, accum_out=ss[:, 0:1]), reads=[rxt], writes=[rss])
        P.op('act', I('activation', out=ss[:, 1:2], in_=ss[:, 0:1], func=AF.Sqrt, bias=self.EPST[:, 0:1], scale=1.0 / D),
             reads=[rss, self.rC], writes=[rss])
        P.op('dve', I('reciprocal', out=ss[:, 2:3], in_=ss[:, 1:2]), reads=[rss], writes=[rss])
        P.op('dve', I('scalar_tensor_tensor', out=hf, in0=xt, scalar=ss[:, 2:3], in1=self.ABC[:, 0:1024], op0=ALU.mult, op1=ALU.mult),
             reads=[rxt, rss, self.rABC], writes=[rhf])
        P.op('pool', I('tensor_tensor', out=hb, in0=hf, in1=self.ABC[:, 1024:2048], op=ALU.add), reads=[rhf, self.rABC], writes=[rhb])

    def proj(self, l, si, src, rsrc, even, j):
        P = self.P
        S = self.seq_lens[si]
        row0 = self.row0[si]
        ncol = self.wncol
        WB = self.WB
        ntile = S // 128
        ngrp = S // 512
        ntm = 2304 if even else 768
        nfm = (ncol - ntm) // 128
        XT = [self.af(1024) for _ in range(2)]
        HF = [self.af(1024) for _ in range(2)]
        HB = [self.ab(1024) for _ in range(2)]
        RT = [self.af(80) for _ in range(2)]
        SS = [self.af(4) for _ in range(2)]
        HTg = [self.ab(8 * 512) for _ in range(2)]
        FMst = [self.ab(nfm * 512) for _ in range(2)]
        rXT = [Res("xt%d" % i) for i in range(2)]
        rHF = [Res("hf%d" % i) for i in range(2)]
        rHB = [Res("hb%d" % i) for i in range(2)]
        rRT = [Res("rt%d" % i) for i in range(2)]
        rSS = [Res("ss%d" % i) for i in range(2)]
        rHT = [[Res("ht%d_%d" % (i, s)) for s in range(4)] for i in range(2)]
        rFM = [Res("fm%d" % i) for i in range(2)]
        rS2 = [Res("s2_%d" % i) for i in range(2)]
        rS3 = [Res("s3_%d" % i) for i in range(2)]
        rS4 = [Res("s4_%d" % i) for i in range(2)]
        if even:
            QSQ = self.af(640)
            QN = [self.af(640) for _ in range(2)]
            TMP = [self.af(320) for _ in range(4)]
            QST = self.af(32)
            QATOK = [self.ab(640) for _ in range(2)]
            VAST = [self.ab(384) for _ in range(2)]
            QBTOK = [self.ab(1024) for _ in range(2)]
            VBST = [self.ab(512) for _ in range(2)]
            TB = [self.af(64) for _ in range(4)]
            QATst = [self.ab(5 * 512) for _ in range(2)]
            QBTst = [self.ab(8 * 512) for _ in range(2)]
            rQSQ, rQST = Res("qsq"), Res("qst")
            rQN = [Res("qn%d" % i) for i in range(2)]
            rTMP = [Res("tmp%d" % i) for i in range(4)]
            rTB = [Res("tb%d" % i) for i in range(4)]
            rQATOK = [Res("qatok%d" % i) for i in range(2)]
            rVAST = [Res("vast%d" % i) for i in range(2)]
            rQBTOK = [[Res("qbtok%d_%d" % (i, w)) for w in range(2)] for i in range(2)]
            rVBST = [Res("vbst%d" % i) for i in range(2)]
            rQATst = [Res("qatst%d" % i) for i in range(2)]
            rQBTst = [Res("qbtst%d" % i) for i in range(2)]
            P.op('sp', I('dma_start', out=self.QKNB[:], in_=dap(self.QKN, j * 128, [[0, 128], [1, 128]])), writes=[self.rQKN], dma=self.rQKN)
            for b in range(2):
                P.op('pool', I('memset', VAST[b], 1.0), writes=[rVAST[b]])
        else:
            VCST = [self.ab(768) for _ in range(2)]
            UDT = [self.ab(2 * 512) for _ in range(2)]
            YST = [self.ab(512) for _ in range(2)]
            rVCST = [Res("vcst%d" % i) for i in range(2)]
            rUDT = [Res("udt%d" % i) for i in range(2)]
            rYST = [Res("yst%d" % i) for i in range(2)]
        wview = lambda k, c0, n: WB[:, k * ncol + c0:k * ncol + c0 + n]

        def s1(it):
            b = it % 2
            r = row0 + 128 * it
            P.op('sp', I('dma_start', out=XT[b], in_=dap(src, r * D, [[D, 128], [1, D]])), reads=[rsrc], writes=[rXT[b]], dma=rXT[b])
            if even:
                P.op('sp', I('dma_start', out=RT[b], in_=dap(self.ROPE, 128 * it * 80, [[80, 128], [1, 80]])), writes=[rRT[b]], dma=rRT[b])
            self.norm_tile(XT[b], rXT[b], HB[b], rHB[b], HF[b], rHF[b], SS[b], rSS[b])

        def s2(it):
            b = it % 2
            gb = (it // 4) % 2
            sub = it % 4
            ps, rps = self.bank()
            pb = ps[:, :].bitcast(BF16)
            for k in range(8):
                P.op('pe', I('transpose', out=pb[:, 128 * k:128 * k + 128], in_=HB[b][:, 128 * k:128 * k + 128], identity=self.IDB[:]),
                     reads=[rHB[b], self.rC], writes=[rps])
            dst = HTg[gb].rearrange("p (k t) -> p k t", k=8)[:, :, sub * 128:(sub + 1) * 128]
            P.op('act', I('copy', out=dst, in_=pb[:, 0:1024].rearrange("p (k t) -> p k t", k=8)), reads=[rps], writes=[rHT[gb][sub]])

        def tm_block(it, c0, n):
            gb = (it // 4) % 2
            sub = it % 4
            ps, rps = self.bank()
            for k in range(8):
                P.op('pe', I('matmul', ps[:, 0:n], HTg[gb][:, k * 512 + sub * 128:k * 512 + sub * 128 + 128], wview(k, c0, n),
                                                   start=(k == 0), stop=(k == 7)),
                     reads=[rHT[gb][sub], self.rWB], writes=[rps])
            return ps, rps

        def qk_norm_rope(it, ps, rps, c0, nh, gain0, dstap, rdst):
            b = it % 2
            n = 64 * nh
            xin = ps[:, c0:c0 + n]
            P.op('act', I('activation', out=QSQ[:, 0:n], in_=xin, func=AF.Square), reads=[rps], writes=[rQSQ])
            P.op('dve', I('tensor_reduce', out=QST[:, 0:nh], in_=QSQ[:, 0:n].rearrange("p (h d) -> p h d", h=nh), axis=AX.X, op=ALU.add),
                 reads=[rQSQ], writes=[rQST])
            P.op('dve', I('tensor_scalar', out=QST[:, 8:8 + nh], in0=QST[:, 0:nh], scalar1=1.0 / 64, scalar2=EPS, op0=ALU.mult, op1=ALU.add),
                 reads=[rQST], writes=[rQST])
            P.op('pool', I('tensor_tensor', out=QST[:, 16:16 + nh], in0=QST[:, 8:8 + nh], in1=self.NEGH[:, 0:nh], op=ALU.pow),
                 reads=[rQST, self.rC], writes=[rQST])
            qn = QN[b][:, 0:n]
            P.op('dve', I('tensor_tensor', out=qn.rearrange("p (h d) -> p h d", h=nh), in0=xin.rearrange("p (h d) -> p h d", h=nh),
                                                  in1=QST[:, 16:16 + nh].unsqueeze(2).broadcast_to([128, nh, 64]), op=ALU.mult),
                 reads=[rps, rQST], writes=[rQN[b]])
            P.op('dve', I('tensor_tensor', out=qn.rearrange("p (h d) -> p h d", h=nh), in0=qn.rearrange("p (h d) -> p h d", h=nh),
                                                   in1=self.QKNB[:, gain0:gain0 + 64].unsqueeze(1).broadcast_to([128, nh, 64]), op=ALU.mult),
                 reads=[rQN[b], self.rQKN], writes=[rQN[b]])
            v = qn.rearrange("p (h a x f) -> p h a x f", h=nh, a=2, x=2)
            o = dstap.rearrange("p (h a x f) -> p h a x f", h=nh, a=2, x=2)
            cosv = RT[b][:, 0:32].rearrange("p (a f) -> p a f", a=2).unsqueeze(1).broadcast_to([128, nh, 2, 16])
            sinv = RT[b][:, 32:64].rearrange("p (a f) -> p a f", a=2).unsqueeze(1).broadcast_to([128, nh, 2, 16])
            x1 = v[:, :, :, 0, :]
            x2 = v[:, :, :, 1, :]
            m = nh * 32
            t = [TMP[i][:, 0:m].rearrange("p (h a f) -> p h a f", h=nh, a=2) for i in range(4)]
            P.op('dve', I('tensor_tensor', out=t[0], in0=x1, in1=cosv, op=ALU.mult), reads=[rQN[b], rRT[b]], writes=[rTMP[0]])
            P.op('dve', I('tensor_tensor', out=t[1], in0=x2, in1=sinv, op=ALU.mult), reads=[rQN[b], rRT[b]], writes=[rTMP[1]])
            P.op('dve', I('tensor_tensor', out=o[:, :, :, 0, :], in0=t[0], in1=t[1], op=ALU.subtract), reads=[rTMP[0], rTMP[1]], writes=[rdst])
            P.op('pool', I('tensor_tensor', out=t[2], in0=x1, in1=sinv, op=ALU.mult), reads=[rQN[b], rRT[b]], writes=[rTMP[2]])
            P.op('pool', I('tensor_tensor', out=t[3], in0=x2, in1=cosv, op=ALU.mult), reads=[rQN[b], rRT[b]], writes=[rTMP[3]])
            P.op('pool', I('tensor_tensor', out=o[:, :, :, 1, :], in0=t[2], in1=t[3], op=ALU.add), reads=[rTMP[2], rTMP[3]], writes=[rdst])

        def partial_rope(it, ps, rps, dst, rdst):
            b = it % 2
            pv = ps[:, 0:512].rearrange("p (h d) -> p h d", h=8)
            dv = dst.rearrange("p (h d) -> p h d", h=8)
            x1 = pv[:, :, 0:8]
            x2 = pv[:, :, 8:16]
            cosv = RT[b][:, 64:72].unsqueeze(1).broadcast_to([128, 8, 8])
            sinv = RT[b][:, 72:80].unsqueeze(1).broadcast_to([128, 8, 8])
            t = [TB[i][:, 0:64].rearrange("p (h f) -> p h f", h=8) for i in range(4)]
            P.op('dve', I('tensor_tensor', out=t[0], in0=x1, in1=cosv, op=ALU.mult), reads=[rps, rRT[b], rdst], writes=[rTB[0]])
            P.op('dve', I('tensor_tensor', out=t[1], in0=x2, in1=sinv, op=ALU.mult), reads=[rps, rRT[b]], writes=[rTB[1]])
            P.op('dve', I('tensor_tensor', out=dv[:, :, 0:8], in0=t[0], in1=t[1], op=ALU.subtract), reads=[rTB[0], rTB[1]], writes=[rdst])
            P.op('dve', I('tensor_tensor', out=t[2], in0=x1, in1=sinv, op=ALU.mult), reads=[rps, rRT[b]], writes=[rTB[2]])
            P.op('dve', I('tensor_tensor', out=t[3], in0=x2, in1=cosv, op=ALU.mult), reads=[rps, rRT[b]], writes=[rTB[3]])
            P.op('dve', I('tensor_tensor', out=dv[:, :, 8:16], in0=t[2], in1=t[3], op=ALU.add), reads=[rTB[2], rTB[3]], writes=[rdst])

        def s3_even(it):
            b = it % 2
            gb = (it // 4) % 2
            sub = it % 4
            r = 128 * it
            ps, rps = tm_block(it, 0, 512)
            qk_norm_rope(it, ps, rps, 0, 8, 0, QATOK[b][:, 0:512], rQATOK[b])
            ps, rps = tm_block(it, 512, 256)
            qk_norm_rope(it, ps, rps, 0, 2, 64, QATOK[b][:, 512:640], rQATOK[b])
            P.op('dve', I('tensor_copy', out=VAST[b].rearrange("p (k c) -> p k c", k=2)[:, :, 64:128],
                                                in_=ps[:, 128:256].rearrange("p (k c) -> p k c", k=2)), reads=[rps, rQSQ], writes=[rVAST[b]])
            P.op('pool', I('dma_start', out=dap(self.VAUGd, r * 384, [[384, 128], [1, 384]]), in_=VAST[b]),
                 reads=[rVAST[b]], writes=[self.rD["vaug"]], dma=rVAST[b])
            for which in range(2):
                ps, rps = tm_block(it, 768 + 512 * which, 512)
                dstb = QBTOK[b][:, 512 * which:512 * which + 512]
                P.op('act', I('copy', out=dstb, in_=ps[:, 0:512]), reads=[rps], writes=[rQBTOK[b][which]])
                partial_rope(it, ps, rps, dstb, rQBTOK[b][which])
            ps, rps = tm_block(it, 1792, 512)
            P.op('act', I('copy', out=VBST[b], in_=ps[:, 0:512]), reads=[rps], writes=[rVBST[b]])
            P.op('pool', I('dma_start', out=dap(self.VBd, r * 512, [[512, 128], [1, 512]]), in_=VBST[b]),
                 reads=[rVBST[b]], writes=[self.rD["vb"]], dma=rVBST[b])

        def s3b_even(it):
            b = it % 2
            gb = (it // 4) % 2
            sub = it % 4
            ps2, rps2 = self.bank()
            pb = ps2[:, :].bitcast(BF16)
            for c in range(5):
                P.op('pe', I('transpose', out=pb[:, 128 * c:128 * c + 128], in_=QATOK[b][:, 128 * c:128 * c + 128], identity=self.IDB[:]),
                     reads=[rQATOK[b], self.rC], writes=[rps2])
            dst = QATst[gb].rearrange("p (c t) -> p c t", c=5)[:, :, sub * 128:(sub + 1) * 128]
            P.op('act', I('copy', out=dst, in_=pb[:, 0:640].rearrange("p (c t) -> p c t", c=5)), reads=[rps2], writes=[rQATst[gb]])
            ps3, rps3 = self.bank()
            pb3 = ps3[:, :].bitcast(BF16)
            for c in range(8):
                P.op('pe', I('transpose', out=pb3[:, 128 * c:128 * c + 128], in_=QBTOK[b][:, 128 * c:128 * c + 128], identity=self.IDB[:]),
                     reads=rQBTOK[b] + [self.rC], writes=[rps3])
            dst3 = QBTst[gb].rearrange("p (c t) -> p c t", c=8)[:, :, sub * 128:(sub + 1) * 128]
            P.op('act', I('copy', out=dst3, in_=pb3[:, 0:1024].rearrange("p (c t) -> p c t", c=8)), reads=[rps3], writes=[rQBTst[gb]])
        def s3_odd(it):
            b = it % 2
            r = 128 * it
            ps, rps = tm_block(it, 0, 512)
            P.op('act', I('copy', out=VCST[b][:, 0:512], in_=ps[:, 0:512]), reads=[rps], writes=[rVCST[b]])
            ps, rps = tm_block(it, 512, 256)
            P.op('act', I('copy', out=VCST[b][:, 512:768], in_=ps[:, 0:256]), reads=[rps], writes=[rVCST[b]])
            P.op('pool', I('dma_start', out=dap(self.VCd, r * 768, [[768, 128], [1, 768]]), in_=VCST[b]),
                 reads=[rVCST[b]], writes=[self.rD["vc"]], dma=rVCST[b])

        def s4(g):
            gb = g % 2
            sm = self.smax
            t0 = 512 * g
            for c in range(nfm):
                ps, rps = self.bank()
                for k in range(8):
                    P.op('pe', I('matmul', ps[:, :], wview(k, ntm + 128 * c, 128), HTg[gb][:, k * 512:(k + 1) * 512],
                                                                   start=(k == 0), stop=(k == 7)),
                         reads=rHT[gb] + [self.rWB], writes=[rps])
                dst = FMst[gb][:, c * 512:(c + 1) * 512]
                if even:
                    P.op('act', I('activation', out=dst, in_=ps[:, :], func=AF.Silu), reads=[rps], writes=[rFM[gb]])
                else:
                    if c < 6:
                        P.op('act', I('mul', dst, ps[:, :], 0.125), reads=[rps], writes=[rFM[gb]])
                    elif c < 12:
                        P.op('dve', I('tensor_copy', out=dst, in_=ps[:, :]), reads=[rps], writes=[rFM[gb]])
                    elif c < 20:
                        P.op('act', I('activation', out=dst, in_=ps[:, :], func=AF.Silu), reads=[rps], writes=[rFM[gb]])
                    else:
                        P.op('dve', I('tensor_copy', out=UDT[gb][:, (c - 20) * 512:(c - 19) * 512], in_=ps[:, :]), reads=[rps], writes=[rUDT[gb]])
            fm3 = lambda c0, n: FMst[gb][:, c0 * 512:(c0 + n) * 512].rearrange("p (c t) -> p c t", c=n)
            dd = lambda T, c0, n: dap(T, c0 * 128 * sm + t0, [[sm, 128], [128 * sm, n], [1, 512]])
            if even:
                P.op('pool', I('dma_start', out=dd(self.GATE_T, 0, 8), in_=fm3(0, 8)), reads=[rFM[gb]], writes=[self.rD["gate_t"]], dma=rFM[gb])
                q3 = QATst[gb][:, 0:4 * 512].rearrange("p (c t) -> p c t", c=4)
                P.op('pool', I('dma_start', out=dd(self.QA_T, 0, 4), in_=q3), reads=[rQATst[gb]], writes=[self.rD["qa_t"]], dma=rQATst[gb])
                P.op('pool', I('dma_start', out=dap(self.KA_T, t0, [[sm, 128], [1, 512]]), in_=QATst[gb][:, 4 * 512:5 * 512]),
                     reads=[rQATst[gb]], writes=[self.rD["ka_t"]], dma=rS2[gb])
                qb3 = QBTst[gb][:, 0:4 * 512].rearrange("p (c t) -> p c t", c=4)
                kb3 = QBTst[gb][:, 4 * 512:8 * 512].rearrange("p (c t) -> p c t", c=4)
                P.op('pool', I('dma_start', out=dd(self.QB_T, 0, 4), in_=qb3), reads=[rQBTst[gb]], writes=[self.rD["qb_t"]], dma=rQBTst[gb])
                P.op('pool', I('dma_start', out=dd(self.KB_T, 0, 4), in_=kb3), reads=[rQBTst[gb]], writes=[self.rD["kb_t"]], dma=rS3[gb])
            else:
                P.op('pool', I('dma_start', out=dd(self.QC_T, 0, 6), in_=fm3(0, 6)), reads=[rFM[gb]], writes=[self.rD["qc_t"]], dma=rFM[gb])
                P.op('pool', I('dma_start', out=dd(self.KC_T, 0, 6), in_=fm3(6, 6)), reads=[rFM[gb]], writes=[self.rD["kc_t"]], dma=rS2[gb])
                P.op('pool', I('dma_start', out=dd(self.GATE_T, 0, 8), in_=fm3(12, 8)), reads=[rFM[gb]], writes=[self.rD["gate_t"]], dma=rS3[gb])
                for sub in range(4):
                    yb = (4 * g + sub) % 2
                    ps, rps = self.bank()
                    for jc in range(2):
                        P.op('pe', I('matmul', ps[:, 256 * jc:256 * jc + 256],
                                                                             UDT[gb][:, jc * 512 + sub * 128:jc * 512 + sub * 128 + 128],
                                                                             self.CS64[:, :], start=True, stop=True),
                             reads=[rUDT[gb], self.rC], writes=[rps])
                    src_v = ps[:, :].rearrange("p (j r m) -> p r j m", j=2, r=2)
                    dst_v = YST[yb].rearrange("p (r j m) -> p r j m", r=2, j=2)
                    P.op('act', I('copy', out=dst_v, in_=src_v), reads=[rps], writes=[rYST[yb]])
                    r = t0 + 128 * sub
                    P.op('pool', I('dma_start', out=dap(self.YDd, r * 512, [[512, 128], [1, 512]]), in_=YST[yb]),
                         reads=[rYST[yb]], writes=[self.rD["yd"]], dma=rYST[yb])

        s3 = s3_even if even else s3_odd
        s1(0)
        for it in range(ntile):
            if it + 1 < ntile:
                s1(it + 1)
            s2(it)
            s3(it)
            if even:
                if it % 4 > 0:
                    s3b_even(it - 1)
                if it % 4 == 3:
                    s3b_even(it)
            if it % 4 == 3:
                s4(it // 4)

    def finalize_pair(self, numps, rnum, denps, rden, n, gate, rgate, rd, rrd, ot, rot):
        P = self.P
        P.op('dve', I('reciprocal', out=rd[:, 0:n], in_=denps), reads=[rden], writes=[rrd])
        P.op('dve', I('tensor_tensor', out=rd[:, 0:n], in0=numps, in1=rd[:, 0:n], op=ALU.mult), reads=[rnum, rrd], writes=[rrd])
        P.op('pool', I('tensor_tensor', out=ot[:, 0:n], in0=rd[:, 0:n], in1=gate[:, 0:n], op=ALU.mult), reads=[rrd, rgate], writes=[rot])

    def mixer_a(self, si):
        P = self.P
        S = self.seq_lens[si]
        sm = self.smax
        nch = S // 128
        nqb = S // 512
        self.areset()
        KA2 = [self.ab(S) for _ in range(2)]
        rKA2 = [Res("ka2_%d" % i) for i in range(2)]
        VA = self.ab(nch * 384)
        rVA = Res("va")
        for kv in range(2):
            for hh in range(2):
                rr = Res("ka2l")
                P.op('sp', I('dma_start', out=KA2[kv][64 * hh:64 * hh + 64, :], in_=dap(self.KA_T, 64 * kv * sm, [[sm, 64], [1, S]])),
                     reads=[self.rD["ka_t"]], writes=[rKA2[kv]], dma=rr)
        npiece = (nch + 7) // 8
        rVAp = [Res("va%d" % i) for i in range(npiece)]
        for pi in range(npiece):
            c0 = 8 * pi
            cn = min(8, nch - c0)
            P.op('sp', I('dma_start', out=VA[:, c0 * 384:(c0 + cn) * 384].rearrange("p (c x) -> p c x", c=cn),
                         in_=dap(self.VAUGd, c0 * 128 * 384, [[384, 128], [128 * 384, cn], [1, 384]])),
                 reads=[self.rD["vaug"]], writes=[rVAp[pi]], dma=rVAp[pi])
        NQ = 3
        QT = [self.ab(512) for _ in range(NQ)]
        GT = [self.ab(512) for _ in range(NQ)]
        rQT = [Res("qt%d" % i) for i in range(NQ)]
        rGT = [Res("gt%d" % i) for i in range(NQ)]
        NE = 3
        ET = [self.ab(1024) for _ in range(NE)]
        rET = [Res("et%d" % i) for i in range(NE)]
        RD = [self.af(512) for _ in range(2)]
        rRD = [Res("rd%d" % i) for i in range(2)]
        OT = [self.ab(512) for _ in range(2)]
        rOT = [Res("ot%d" % i) for i in range(2)]
        items = [(qb, c) for qb in range(nqb) for c in range(4)]

        def load(ix):
            qb, c = items[ix]
            s = ix % NQ
            P.op('sp', I('dma_start', out=QT[s], in_=dap(self.QA_T, c * 128 * sm + 512 * qb, [[sm, 128], [1, 512]])),
                 reads=[self.rD["qa_t"]], writes=[rQT[s]], dma=rQT[s])
            P.op('sp', I('dma_start', out=GT[s], in_=dap(self.GATE_T, c * 128 * sm + 512 * qb, [[sm, 128], [1, 512]])),
                 reads=[self.rD["gate_t"]], writes=[rGT[s]], dma=rGT[s])

        ecount = 0
        load(0)
        for ix, (qb, c) in enumerate(items):
            if ix + 1 < len(items):
                load(ix + 1)
            s = ix % NQ
            kv = c // 2
            accs = [self.bank(self.POOL_ACC), self.bank(self.POOL_ACC)]
            pend = []

            def pv(jj, es):
                for e_ in range(2):
                    aps, raps = accs[e_]
                    lo = 64 if e_ == 0 else 0
                    vap = VA[:, jj * 384 + kv * 192 + lo:jj * 384 + kv * 192 + lo + 128]
                    P.op('pe', I('matmul', aps[:, :], vap, ET[es][:, 512 * e_:512 * e_ + 512], start=(jj == 0), stop=(jj == nch - 1)),
                         reads=[rVAp[jj // 8], rET[es]], writes=[raps])

            for jj in range(nch):
                if len(pend) == 2:
                    pv(*pend.pop(0))
                k2 = self.st2_i % 2
                self.st2_i += 1
                st2 = self.PS2[2 + k2]
                rst = [self.rPS[4 + 2 * k2], self.rPS[5 + 2 * k2]]
                for e_ in range(2):
                    P.op('pe', I('matmul', st2[:, 512 * e_:512 * e_ + 512], KA2[kv][64 * e_:64 * e_ + 64, 128 * jj:128 * jj + 128],
                                 QT[s][64 * e_:64 * e_ + 64, :], start=True, stop=True),
                         reads=[rKA2[kv], rQT[s]], writes=[rst[e_]])
                es = ecount % NE
                ecount += 1
                P.op('act', I('activation', out=ET[es], in_=st2[:, 0:1024], func=AF.Exp, scale=0.125), reads=rst, writes=[rET[es]])
                pend.append((jj, es))
            while pend:
                pv(*pend.pop(0))
            fb = ix % 2
            a0, ra0 = accs[0]
            a1, ra1 = accs[1]
            P.op('dve', I('reciprocal', out=RD[fb][0:64, :], in_=a0[64:128, :]), reads=[ra0], writes=[rRD[fb]])
            P.op('dve', I('reciprocal', out=RD[fb][64:128, :], in_=a1[0:64, :]), reads=[ra1], writes=[rRD[fb]])
            P.op('dve', I('tensor_tensor', out=RD[fb][0:64, :], in0=a0[0:64, :], in1=RD[fb][0:64, :], op=ALU.mult), reads=[ra0, rRD[fb]], writes=[rRD[fb]])
            P.op('dve', I('tensor_tensor', out=RD[fb][64:128, :], in0=a1[64:128, :], in1=RD[fb][64:128, :], op=ALU.mult), reads=[ra1, rRD[fb]], writes=[rRD[fb]])
            P.op('pool', I('tensor_tensor', out=OT[fb], in0=RD[fb], in1=GT[s], op=ALU.mult), reads=[rRD[fb], rGT[s]], writes=[rOT[fb]])
            P.op('pool', I('dma_start', out=dap(self.MIX_T, c * 128 * sm + 512 * qb, [[sm, 128], [1, 512]]), in_=OT[fb]),
                 reads=[rOT[fb]], writes=[self.rD["mix_t"]], dma=rOT[fb])

    def acc_alloc(self, n):
        P = self.P
        numps, rnum = self.bank(self.POOL_ACC)
        denps, rden = self.bank(self.POOL_ACC)
        P.op('dve', I('memset', numps[:, 0:n], 0.0), writes=[rnum])
        P.op('dve', I('memset', denps[:, 0:n], 0.0), writes=[rden])
        return numps, rnum, denps, rden

    def pair_attn_block(self, steps, kbd, rkbd, vbd, rvbd, qrhs_fn, rq, bias_fn, rbias, scale, numps, rnum, denps, rden, ET, rET, ecount):
        P = self.P
        NE = len(ET)
        pend = []
        groups = []
        cur, tot = [], 0
        for (jj, c0, n, ex) in steps:
            if cur and tot + n > 512:
                groups.append(cur)
                cur, tot = [], 0
            cur.append((jj, c0, n, ex, tot))
            tot += n
        if cur:
            groups.append(cur)

        def qk(grp):
            ps, rps = self.bank(self.POOL_ST)
            tot = 0
            for (jj, c0, n, ex, off) in grp:
                P.op('pe', I('matmul', ps[:, off:off + n], kbd[:, 128 * jj:128 * jj + 128], qrhs_fn(c0, n, ex), start=True, stop=False),
                     reads=list(rkbd) + [rq], writes=[rps])
                P.op('pe', I('matmul', ps[:, off:off + n], self.IDB[:, :], bias_fn(c0, n, ex), start=False, stop=True),
                     reads=[rbias, self.rC], writes=[rps])
                tot = off + n
            es = ecount[0] % NE
            ecount[0] += 1
            P.op('act', I('activation', out=ET[es][:, 0:tot], in_=ps[:, 0:tot], func=AF.Exp, scale=scale), reads=[rps], writes=[rET[es]])
            return es

        def pv(grp, es):
            for (jj, c0, n, ex, off) in grp:
                P.op('pe', I('matmul', numps[:, c0:c0 + n], vbd[:, 128 * jj:128 * jj + 128], ET[es][:, off:off + n], start=False, stop=False, skip_group_check=True),
                     reads=list(rvbd) + [rET[es]], writes=[rnum])
                P.op('pe', I('matmul', denps[:, c0:c0 + n], self.ONESBD[:, :], ET[es][:, off:off + n], start=False, stop=False, skip_group_check=True),
                     reads=[self.rC, rET[es]], writes=[rden])

        for grp in groups:
            if len(pend) == 2:
                pv(*pend.pop(0))
            es = qk(grp)
            pend.append((grp, es))
        while pend:
            pv(*pend.pop(0))

    def mixer_b(self, si):
        P = self.P
        S = self.seq_lens[si]
        sm = self.smax
        self.areset()
        QT = self.ab(S)
        KT = self.ab(S)
        ACCN = self.af(S)
        ACCD = self.af(S)
        rQT, rKT, rACCN, rACCD = Res("bq"), Res("bk"), Res("accn"), Res("accd")
        NR = 3
        KBD = [self.ab(10 * 128) for _ in range(NR)]
        VBD = [self.ab(10 * 128) for _ in range(NR)]
        rKBD = [Res("kbd%d" % i) for i in range(NR)]
        rKBD2 = [Res("kbd2_%d" % i) for i in range(NR)]
        rVBD = [[Res("vbd%d_%d" % (i, h)) for h in range(2)] for i in range(NR)]
        NE = 4
        ET = [self.ab(512) for _ in range(NE)]
        rET = [Res("bet%d" % i) for i in range(NE)]
        GT = [self.ab(512) for _ in range(2)]
        rGT = [Res("bgt%d" % i) for i in range(2)]
        OT = [self.ab(512) for _ in range(2)]
        rOT = [Res("bot%d" % i) for i in range(2)]
        rVL = [[Res("vl%d_%d" % (i, h)) for h in range(2)] for i in range(NR)]
        for i in range(NR):
            P.op('pool', I('memset', KBD[i], 0.0), writes=[rKBD[i], rKBD2[i]])
            P.op('pool', I('memset', VBD[i], 0.0), writes=rVBD[i])
        ecount = [0]
        bcount = [0]
        for c in range(4):
            P.op('sp', I('dma_start', out=QT, in_=dap(self.QB_T, c * 128 * sm, [[sm, 128], [1, S]])), reads=[self.rD["qb_t"]], writes=[rQT], dma=rQT)
            P.op('sp', I('dma_start', out=KT, in_=dap(self.KB_T, c * 128 * sm, [[sm, 128], [1, S]])), reads=[self.rD["kb_t"]], writes=[rKT], dma=rKT)
            P.op('pool', I('memset', ACCN, 0.0), writes=[rACCN])
            P.op('pool', I('memset', ACCD, 0.0), writes=[rACCD])
            blocks = []
            for dil in (1, 4, 16):
                L = S // dil
                QBW = min(512, L)
                for rho in range(dil):
                    for q0 in range(0, L, QBW):
                        blocks.append((dil, rho, q0, QBW, L))

            accd_ = {}

            def prolog(bi):
                dil, rho, q0, QBW, L = blocks[bi]
                slot = bi % NR
                accd_[bi] = self.acc_alloc(QBW)
                j0 = max(0, q0 // 64 - 1)
                j1 = min(L // 64, (q0 + QBW) // 64 + 1)
                nk = j1 - j0
                for hh in range(2):
                    srcap = KT[64 * hh:64 * hh + 64, :]
                    srcv = bass.AP(srcap.tensor, srcap.offset + rho + dil * 64 * j0, [list(srcap.ap[0]), [64 * dil, nk], [dil, 64]])
                    dstv = KBD[slot][64 * hh:64 * hh + 64, 0:nk * 128].rearrange("p (j x) -> p j x", j=nk)[:, :, 64 * hh:64 * hh + 64]
                    if hh == 0:
                        P.op('pool', I('tensor_copy', out=dstv, in_=srcv), reads=[rKT], writes=[rKBD[slot]])
                    else:
                        P.op('act', I('copy', out=dstv, in_=srcv), reads=[rKT], writes=[rKBD2[slot]])
                    vsrc = dap(self.VBd, (rho + dil * 64 * j0) * 512 + (2 * c + hh) * 64, [[dil * 512, 64], [64 * dil * 512, nk], [1, 64]])
                    vdst = VBD[slot][64 * hh:64 * hh + 64, 0:nk * 128].rearrange("p (j x) -> p j x", j=nk)[:, :, 64 * hh:64 * hh + 64]
                    P.op('sp', I('dma_start', out=vdst, in_=vsrc), reads=[self.rD["vb"]], writes=[rVBD[slot][hh]], dma=rVBD[slot][hh])

            def body(bi):
                dil, rho, q0, QBW, L = blocks[bi]
                slot = bi % NR
                j0 = max(0, q0 // 64 - 1)
                j1 = min(L // 64, (q0 + QBW) // 64 + 1)
                steps = []
                for j in range(j0, j1):
                    qa = max(q0, 64 * j - 64)
                    qe = min(q0 + QBW, 64 * j + 128, L)
                    if qe <= qa:
                        continue
                    steps.append((j - j0, qa - q0, qe - qa, (qa, qa - (64 * j - 64))))
                numps, rnum, denps, rden = accd_.pop(bi)

                def qrhs(c0, n, ex):
                    return bass.AP(QT.tensor, QT.offset + rho + dil * ex[0], [list(QT.ap[0]), [dil, n]])

                def biasf(c0, n, ex):
                    return self.MASKB[:, ex[1]:ex[1] + n]

                self.pair_attn_block(steps, KBD[slot], [rKBD[slot], rKBD2[slot]], VBD[slot], rVBD[slot], qrhs, rQT, biasf, self.rC, 0.125,
                                     numps[:, 0:QBW], rnum, denps[:, 0:QBW], rden, ET, rET, ecount)
                accn = bass.AP(ACCN.tensor, ACCN.offset + rho + dil * q0, [list(ACCN.ap[0]), [dil, QBW]])
                accd = bass.AP(ACCD.tensor, ACCD.offset + rho + dil * q0, [list(ACCD.ap[0]), [dil, QBW]])
                P.op('dve', I('tensor_tensor', out=accn, in0=numps[:, 0:QBW], in1=accn, op=ALU.add), reads=[rnum, rACCN], writes=[rACCN])
                P.op('dve', I('tensor_tensor', out=accd, in0=denps[:, 0:QBW], in1=accd, op=ALU.add), reads=[rden, rACCD], writes=[rACCD])

            prolog(0)
            for bi in range(len(blocks)):
                if bi + 1 < len(blocks):
                    prolog(bi + 1)
                body(bi)
            for qb in range(S // 512):
                fb = bcount[0] % 2
                bcount[0] += 1
                P.op('sp', I('dma_start', out=GT[fb], in_=dap(self.GATE_T, (4 + c) * 128 * sm + 512 * qb, [[sm, 128], [1, 512]])),
                     reads=[self.rD["gate_t"]], writes=[rGT[fb]], dma=rGT[fb])
                sl = slice(512 * qb, 512 * qb + 512)
                P.op('dve', I('reciprocal', out=ACCD[:, sl], in_=ACCD[:, sl]), reads=[rACCD], writes=[rACCD])
                P.op('dve', I('tensor_tensor', out=ACCN[:, sl], in0=ACCN[:, sl], in1=ACCD[:, sl], op=ALU.mult), reads=[rACCD, rACCN], writes=[rACCN])
                P.op('pool', I('tensor_tensor', out=OT[fb], in0=ACCN[:, sl], in1=GT[fb], op=ALU.mult), reads=[rACCN, rGT[fb]], writes=[rOT[fb]])
                P.op('pool', I('dma_start', out=dap(self.MIX_T, (4 + c) * 128 * sm + 512 * qb, [[sm, 128], [1, 512]]), in_=OT[fb]),
                     reads=[rOT[fb]], writes=[self.rD["mix_t"]], dma=rOT[fb])

    def mixer_c(self, j, si):
        P = self.P
        S = self.seq_lens[si]
        sm = self.smax
        rows = S // 64
        self.areset()
        QT = self.ab(S)
        KT = self.ab(S)
        rQT, rKT = Res("cq"), Res("ck")
        BTF = self.af(960)
        BT = self.ab(960)
        rBTF, rBT = Res("btf"), Res("bt")
        NR = 3
        KBD = [self.ab(16 * 128) for _ in range(NR)]
        VBD = [self.ab(16 * 128) for _ in range(NR)]
        rKBD = [Res("ckbd%d" % i) for i in range(NR)]
        rVBD = [[Res("cvbd%d_%d" % (i, h)) for h in range(2)] for i in range(NR)]
        NE = 4
        ET = [self.ab(512) for _ in range(NE)]
        rET = [Res("cet%d" % i) for i in range(NE)]
        GT = [self.ab(512) for _ in range(2)]
        rGT = [Res("cgt%d" % i) for i in range(2)]
        RD = [self.af(512) for _ in range(2)]
        rRD = [Res("crd%d" % i) for i in range(2)]
        OT = [self.ab(512) for _ in range(2)]
        rOT = [Res("cot%d" % i) for i in range(2)]
        rVL = [[Res("cvl%d_%d" % (i, h)) for h in range(2)] for i in range(NR)]
        for i in range(NR):
            P.op('pool', I('memset', KBD[i], 0.0), writes=[rKBD[i]])
            P.op('pool', I('memset', VBD[i], 0.0), writes=rVBD[i])
        r0f = lambda r: min(max(r - 4, 0), rows - 8)
        nblk = rows // 8
        ecount = [0]
        fcount = [0]
        for c in range(6):
            P.op('sp', I('dma_start', out=QT, in_=dap(self.QC_T, c * 128 * sm, [[sm, 128], [1, S]])), reads=[self.rD["qc_t"]], writes=[rQT], dma=rQT)
            P.op('sp', I('dma_start', out=KT, in_=dap(self.KC_T, c * 128 * sm, [[sm, 128], [1, S]])), reads=[self.rD["kc_t"]], writes=[rKT], dma=rKT)
            P.op('sp', I('dma_start', out=BTF, in_=dap(self.RPBX, (j * 6 + c) * 128 * 960, [[960, 128], [1, 960]])), writes=[rBTF], dma=rBTF)
            P.op('dve', I('tensor_tensor', out=BT.rearrange("p (a q) -> p a q", a=15), in0=BTF.rearrange("p (a q) -> p a q", a=15),
                                                  in1=self.MASKC[:, :].unsqueeze(1).broadcast_to([128, 15, 64]), op=ALU.add),
                 reads=[rBTF, self.rC], writes=[rBT])

            def krange(b):
                R0 = 8 * b
                return r0f(R0), r0f(R0 + 7) + 8

            accd_ = {}

            def prolog(b):
                slot = b % NR
                accd_[b] = self.acc_alloc(512)
                k0, k1 = krange(b)
                nk = k1 - k0
                for hh in range(2):
                    srcv = KT[64 * hh:64 * hh + 64, 64 * k0:64 * k1].rearrange("p (j x) -> p j x", j=nk)
                    dstv = KBD[slot][64 * hh:64 * hh + 64, 0:nk * 128].rearrange("p (j x) -> p j x", j=nk)[:, :, 64 * hh:64 * hh + 64]
                    P.op('pool', I('tensor_copy', out=dstv, in_=srcv), reads=[rKT], writes=[rKBD[slot]])
                    vsrc = dap(self.VCd, 64 * k0 * 768 + (2 * c + hh) * 64, [[768, 64], [64 * 768, nk], [1, 64]])
                    vdst = VBD[slot][64 * hh:64 * hh + 64, 0:nk * 128].rearrange("p (j x) -> p j x", j=nk)[:, :, 64 * hh:64 * hh + 64]
                    P.op('sp', I('dma_start', out=vdst, in_=vsrc), reads=[self.rD["vc"]], writes=[rVBD[slot][hh]], dma=rVBD[slot][hh])

            def body(b):
                slot = b % NR
                R0 = 8 * b
                k0, k1 = krange(b)
                steps = []
                for kr in range(k0, k1):
                    valid = [r for r in range(R0, R0 + 8) if r0f(r) <= kr <= r0f(r) + 7]
                    if not valid:
                        continue
                    ra, rb = valid[0], valid[-1] + 1
                    assert valid == list(range(ra, rb))
                    e0 = ra - kr + 7
                    assert 0 <= e0 and e0 + (rb - ra) <= 15
                    steps.append((kr - k0, 64 * (ra - R0), 64 * (rb - ra), (64 * ra, e0)))
                numps, rnum, denps, rden = accd_.pop(b)

                def qrhs(c0, n, ex):
                    return QT[:, ex[0]:ex[0] + n]

                def biasf(c0, n, ex):
                    return BT[:, 64 * ex[1]:64 * ex[1] + n]

                self.pair_attn_block(steps, KBD[slot], [rKBD[slot]], VBD[slot], rVBD[slot], qrhs, rQT, biasf, rBT, 1.0,
                                     numps[:, :], rnum, denps[:, :], rden, ET, rET, ecount)
                fb = fcount[0] % 2
                fcount[0] += 1
                P.op('sp', I('dma_start', out=GT[fb], in_=dap(self.GATE_T, c * 128 * sm + 512 * b, [[sm, 128], [1, 512]])),
                     reads=[self.rD["gate_t"]], writes=[rGT[fb]], dma=rGT[fb])
                self.finalize_pair(numps[:, :], rnum, denps[:, :], rden, 512, GT[fb], rGT[fb], RD[fb], rRD[fb], OT[fb], rOT[fb])
                P.op('pool', I('dma_start', out=dap(self.MIX_T, c * 128 * sm + 512 * b, [[sm, 128], [1, 512]]), in_=OT[fb]),
                     reads=[rOT[fb]], writes=[self.rD["mix_t"]], dma=rOT[fb])

            prolog(0)
            for b in range(nblk):
                if b + 1 < nblk:
                    prolog(b + 1)
                body(b)

    def mixer_d(self, j, si):
        P = self.P
        S = self.seq_lens[si]
        sm = self.smax
        n1 = S // 64
        d1, tw, d2 = self.DFT1[S], self.TW[S], self.DFT2[S]
        C1 = d1[:, 0:n1]
        S1 = d1[:, n1:2 * n1]
        NS1 = d1[:, 2 * n1:3 * n1]
        self.areset()
        YB = [self.ab(8 * 512) for _ in range(2)]
        GP = [self.ab(8 * 512) for _ in range(2)]
        T1 = [self.af(256) for _ in range(2)]
        rYB = [Res("yb%d" % i) for i in range(2)]
        rGP = [Res("gp%d" % i) for i in range(2)]
        rT1 = [Res("t1%d" % i) for i in range(2)]
        for blk in range(8):
            b = blk % 2
            P.op('sp', I('dma_start', out=YB[b][0:n1, :].rearrange("p (s c) -> p s c", s=8),
                                                           in_=dap(self.YDd, blk * 8 * 512, [[64 * 512, n1], [512, 8], [1, 512]])),
                 reads=[self.rD["yd"]], writes=[rYB[b]], dma=rYB[b])
            yv = YB[b][0:n1, :].rearrange("p (s c) -> p s c", s=8)
            for pr in range(4):
                gr, rgr = self.bank()
                gi, rgi = self.bank()
                yr = yv[:, 2 * pr:2 * pr + 2, 0:256]
                yi = yv[:, 2 * pr:2 * pr + 2, 256:512]
                P.op('pe', I('matmul', gr[0:n1, :], C1, yr, start=True, stop=False), reads=[rYB[b], self.rC], writes=[rgr])
                P.op('pe', I('matmul', gr[0:n1, :], S1, yi, start=False, stop=True), reads=[rYB[b], self.rC], writes=[rgr])
                P.op('pe', I('matmul', gi[0:n1, :], C1, yi, start=True, stop=False), reads=[rYB[b], self.rC], writes=[rgi])
                P.op('pe', I('matmul', gi[0:n1, :], NS1, yr, start=False, stop=True), reads=[rYB[b], self.rC], writes=[rgi])
                for u in range(2):
                    s2 = blk * 8 + 2 * pr + u
                    tc_ = tw[:, s2:s2 + 1]
                    ts_ = tw[:, 64 + s2:64 + s2 + 1]
                    grs = gr[0:n1, 256 * u:256 * u + 256]
                    gis = gi[0:n1, 256 * u:256 * u + 256]
                    o0 = (2 * pr + u) * 512
                    t1a = T1[0][0:n1, :]
                    t1b = T1[1][0:n1, :]
                    P.op('dve', I('tensor_scalar', out=t1a, in0=gis, scalar1=ts_, scalar2=None, op0=ALU.mult),
                         reads=[rgi, self.rC], writes=[rT1[0]])
                    P.op('dve', I('scalar_tensor_tensor', out=GP[b][0:n1, o0:o0 + 256], in0=grs, scalar=tc_, in1=t1a,
                                                                                                       op0=ALU.mult, op1=ALU.add),
                         reads=[rgr, rT1[0], self.rC], writes=[rGP[b]])
                    P.op('dve', I('tensor_scalar', out=t1b, in0=grs, scalar1=ts_, scalar2=None, op0=ALU.mult),
                         reads=[rgr, self.rC], writes=[rT1[1]])
                    P.op('dve', I('scalar_tensor_tensor', out=GP[b][0:n1, o0 + 256:o0 + 512], in0=gis, scalar=tc_, in1=t1b,
                                                                                                       op0=ALU.mult, op1=ALU.subtract),
                         reads=[rgi, rT1[1], self.rC], writes=[rGP[b]])
            P.op('pool', I('dma_start', out=dap(self.GDd, blk * 8 * 512, [[64 * 512, n1], [1, 8 * 512]]), in_=GP[b][0:n1, :]),
                 reads=[rGP[b]], writes=[self.rD["gd"]], dma=rGP[b])
        self.bar()
        self.areset()
        FT = self.ab(2 * S)
        rFT = Res("ft")
        GB = [self.ab(8 * 512) for _ in range(2)]
        rGB = [Res("gb%d" % i) for i in range(2)]
        C2 = d2[:, 0:64]
        S2_ = d2[:, 64:128]
        for kb in range(n1 // 8):
            b = kb % 2
            P.op('sp', I('dma_start', out=GB[b][0:64, :].rearrange("p (k c) -> p k c", k=8),
                                                         in_=dap(self.GDd, kb * 8 * 64 * 512, [[512, 64], [64 * 512, 8], [1, 512]])),
                 reads=[self.rD["gd"]], writes=[rGB[b]], dma=rGB[b])
            for fc in range(2):
                ps, rps = self.bank()
                for ki in range(8):
                    gr_ = GB[b][0:64, ki * 512 + fc * 128:ki * 512 + fc * 128 + 128]
                    gi_ = GB[b][0:64, ki * 512 + 256 + fc * 128:ki * 512 + 256 + fc * 128 + 128]
                    P.op('pe', I('matmul', ps[:, 64 * ki:64 * ki + 64], gr_, C2, start=True, stop=False),
                         reads=[rGB[b], self.rC], writes=[rps])
                    P.op('pe', I('matmul', ps[:, 64 * ki:64 * ki + 64], gi_, S2_, start=False, stop=True),
                         reads=[rGB[b], self.rC], writes=[rps])
                dst = bass.AP(FT.tensor, FT.offset + fc * S + kb * 8, [list(FT.ap[0]), [1, 8], [n1, 64]])
                P.op('dve', I('tensor_copy', out=dst, in_=ps[:, :].rearrange("p (k q) -> p k q", k=8)), reads=[rps], writes=[rFT])
        LF = self.af(512)
        LB = self.ab(512)
        rLF, rLB = Res("lf"), Res("lb")
        P.op('sp', I('dma_start', out=LF.rearrange("p (c x) -> p c x", c=2), in_=dap(self.LIND, j * 256 * 256, [[256, 128], [128 * 256, 2], [1, 256]])),
             writes=[rLF], dma=rLF)
        P.op('dve', I('tensor_copy', out=LB, in_=LF), reads=[rLF], writes=[rLB])
        GT = [self.ab(512) for _ in range(2)]
        rGT = [Res("dgt%d" % i) for i in range(2)]
        OT = [self.ab(512) for _ in range(2)]
        rOT = [Res("dot%d" % i) for i in range(2)]
        cnt = 0
        for ec in range(2):
            for kb in range(S // 512):
                fb = cnt % 2
                cnt += 1
                P.op('sp', I('dma_start', out=GT[fb], in_=dap(self.GATE_T, (6 + ec) * 128 * sm + 512 * kb, [[sm, 128], [1, 512]])),
                     reads=[self.rD["gate_t"]], writes=[rGT[fb]], dma=rGT[fb])
                ps, rps = self.bank()
                for fc in range(2):
                    P.op('pe', I('matmul', ps[:, :], LB[:, fc * 256 + ec * 128:fc * 256 + ec * 128 + 128],
                                                                             FT[:, fc * S + 512 * kb:fc * S + 512 * kb + 512], start=(fc == 0), stop=(fc == 1)),
                         reads=[rLB, rFT], writes=[rps])
                P.op('dve', I('tensor_tensor', out=OT[fb], in0=ps[:, :], in1=GT[fb], op=ALU.mult), reads=[rps, rGT[fb]], writes=[rOT[fb]])
                P.op('pool', I('dma_start', out=dap(self.MIX_T, (6 + ec) * 128 * sm + 512 * kb, [[sm, 128], [1, 512]]), in_=OT[fb]),
                     reads=[rOT[fb]], writes=[self.rD["mix_t"]], dma=rOT[fb])

    def outproj(self, WOD, j, si, src, rsrc, dst, rdst):
        P = self.P
        S = self.seq_lens[si]
        sm = self.smax
        row0 = self.row0[si]
        self.areset()
        WO = self.ab(8 * 1024)
        rWO = Res("wo")
        st = [self.af(1024) for _ in range(2)]
        rst = [Res("wos0"), Res("wos1")]
        for k in range(8):
            b = k % 2
            P.op('sp', I('dma_start', out=st[b], in_=dap(WOD, (j * D + 128 * k) * D, [[D, 128], [1, D]])), writes=[rst[b]], dma=rst[b])
            P.op('pool' if k % 2 else 'dve', I('tensor_copy', out=WO[:, k * 1024:(k + 1) * 1024], in_=st[b]), reads=[rst[b]], writes=[rWO])
        MT = [self.ab(8 * 512) for _ in range(2)]
        rMT = [Res("mt%d" % i) for i in range(2)]
        XT = [self.af(1024) for _ in range(2)]
        rXT = [Res("oxt%d" % i) for i in range(2)]
        YT = [self.af(1024) for _ in range(2)]
        rYT = [Res("oyt%d" % i) for i in range(2)]
        SS = [self.af(8) for _ in range(2)]
        rSS = [Res("oss%d" % i) for i in range(2)]
        ntile = S // 128

        def loadg(g):
            gb = g % 2
            P.op('sp', I('dma_start', out=MT[gb].rearrange("p (c t) -> p c t", c=8), in_=dap(self.MIX_T, 512 * g, [[sm, 128], [128 * sm, 8], [1, 512]])),
                 reads=[self.rD["mix_t"]], writes=[rMT[gb]], dma=rMT[gb])

        loadg(0)
        for it in range(ntile):
            g, sub = it // 4, it % 4
            gb = g % 2
            b = it % 2
            if sub == 0 and g + 1 < S // 512:
                loadg(g + 1)
            r = row0 + 128 * it
            P.op('sp', I('dma_start', out=XT[b], in_=dap(src, r * D, [[D, 128], [1, D]])), reads=[rsrc], writes=[rXT[b]], dma=rXT[b])
            banks = [self.bank(), self.bank()]
            for nb in range(2):
                ps, rps = banks[nb]
                for c in range(8):
                    P.op('pe', I('matmul', ps[:, :], MT[gb][:, c * 512 + sub * 128:c * 512 + sub * 128 + 128],
                                                                                     WO[:, c * 1024 + nb * 512:c * 1024 + nb * 512 + 512],
                                                                                     start=(c == 0), stop=(c == 7)),
                         reads=[rMT[gb], rWO], writes=[rps])
                P.op('act', I('activation', out=self.JUNK[:, 0:512], in_=ps[:, :], func=AF.Square, accum_out=SS[b][:, nb:nb + 1]),
                     reads=[rps], writes=[rSS[b]])
            P.op('dve', I('tensor_tensor', out=SS[b][:, 2:3], in0=SS[b][:, 0:1], in1=SS[b][:, 1:2], op=ALU.add), reads=[rSS[b]], writes=[rSS[b]])
            P.op('act', I('activation', out=SS[b][:, 3:4], in_=SS[b][:, 2:3], func=AF.Sqrt, bias=self.EPST[:, 0:1], scale=1.0 / D),
                 reads=[rSS[b], self.rC], writes=[rSS[b]])
            P.op('dve', I('reciprocal', out=SS[b][:, 4:5], in_=SS[b][:, 3:4]), reads=[rSS[b]], writes=[rSS[b]])
            for nb in range(2):
                ps, rps = banks[nb]
                sl = slice(nb * 512, nb * 512 + 512)
                P.op('dve', I('scalar_tensor_tensor', out=YT[b][:, sl], in0=ps[:, :], scalar=SS[b][:, 4:5],
                                                                                       in1=self.ABC[:, 2048 + nb * 512:2048 + nb * 512 + 512],
                                                                                       op0=ALU.mult, op1=ALU.mult),
                     reads=[rps, rSS[b], self.rABC], writes=[rYT[b]])
            P.op('pool', I('tensor_tensor', out=YT[b], in0=YT[b], in1=XT[b], op=ALU.add), reads=[rYT[b], rXT[b]], writes=[rYT[b]])
            P.op('pool', I('dma_start', out=dap(dst, r * D, [[D, 128], [1, D]]), in_=YT[b]), reads=[rYT[b]], writes=[rdst], dma=rYT[b])


_W_EVEN_PERM = None


def _even_perm():
    idx = list(range(0, 768)) + list(range(1280, 2816)) + list(range(768, 1280)) + list(range(2816, 3328))
    return np.array(idx)


def _odd_perm():
    idx = list(range(1536, 2304)) + list(range(0, 1536)) + list(range(2304, 3072)) + list(range(3328, 3584)) + list(range(3072, 3328))
    return np.array(idx)


def _rpb_expand(rpb):
    n_odd = rpb.shape[0]
    kc = np.arange(64)[:, None]
    qc = np.arange(64)[None, :]
    cidx = np.clip(kc - qc + 15, 0, 30)
    out = np.zeros((n_odd, 6, 128, 15, 64), np.float32)
    for e in range(15):
        dr = 14 - e
        g = rpb[:, :, dr, :][:, :, cidx]
        g = g.reshape(n_odd, 6, 2, 64, 64).reshape(n_odd, 6, 128, 64)
        out[:, :, :, e, :] = g
    return np.ascontiguousarray(out.reshape(n_odd, 6, 128, 960))


def make_shared_inputs(seq_lens, pre_g, post_g, ada_w, ada_b, w_in_ab, w_out_ab, qn_a, kn_a, w_in_cd, w_out_cd, rpb_c, lin_d):
    f = lambda a: np.ascontiguousarray(np.asarray(a, dtype=np.float32))
    smax = max(seq_lens)
    ns = len(seq_lens)
    idb, onesbd, maskb, maskc, cs64 = const_tables()
    sel = np.zeros((ns, ns * 128), np.float32)
    for i in range(ns):
        sel[i, i * 128:(i + 1) * 128] = 1.0
    m = {
        "preg": f(pre_g), "postg": f(post_g), "adaw": f(ada_w), "adab": f(ada_b),
        "wab": f(np.asarray(w_in_ab)[:, :, _even_perm()]), "woab": f(w_out_ab),
        "qkn": f(np.concatenate([np.asarray(qn_a), np.asarray(kn_a)], axis=1)),
        "wcd": f(np.asarray(w_in_cd)[:, :, _odd_perm()]), "wocd": f(w_out_cd),
        "rpbx": _rpb_expand(np.asarray(rpb_c, dtype=np.float32)), "lind": f(lin_d),
        "rope": rope_table(smax), "idb": idb, "onesbd": onesbd, "sel": sel, "maskb": maskb, "maskc": maskc, "cs64": cs64,
    }
    for S in sorted(set(seq_lens)):
        d1, tw, d2 = dft_tables(S)
        m["dft1_%d" % S] = d1
        m["tw_%d" % S] = tw
        m["dft2_%d" % S] = d2
    return m


def core_inputs(shared, xs, cs):
    ns = len(xs)
    m = dict(shared)
    m["xin"] = np.ascontiguousarray(np.concatenate(xs, axis=0).astype(np.float32))
    c = np.stack(cs, axis=0).astype(np.float32)
    ct = c.reshape(ns, 8, 128).transpose(2, 1, 0).reshape(128, 8 * ns)
    m["ct"] = np.ascontiguousarray(ct)
    return m


_NC_CACHE = {}


def kernel(x_prompt, x_sample, c_prompt, c_sample, pre_g, post_g, ada_w, ada_b,
           w_in_ab, w_out_ab, qn_a, kn_a, w_in_cd, w_out_cd, rpb_c, lin_d):
    x_prompt = np.asarray(x_prompt)
    x_sample = np.asarray(x_sample)
    c_prompt = np.asarray(c_prompt)
    c_sample = np.asarray(c_sample)
    ncores = 8
    sp, ss = x_prompt.shape[1], x_sample.shape[1]
    seq_lens = [sp, sp, ss]
    depth = np.asarray(pre_g).shape[0]
    key = (tuple(seq_lens), depth)
    if key not in _NC_CACHE:
        _NC_CACHE[key] = Builder(seq_lens, depth, (depth + 1) // 2, depth // 2).build()
    nc = _NC_CACHE[key]
    shared = make_shared_inputs(seq_lens, pre_g, post_g, ada_w, ada_b, w_in_ab, w_out_ab, qn_a, kn_a, w_in_cd, w_out_cd, rpb_c, lin_d)
    in_maps = []
    for c in range(ncores):
        in_maps.append(core_inputs(shared, [x_prompt[2 * c], x_prompt[2 * c + 1], x_sample[c]],
                                   [c_prompt[2 * c], c_prompt[2 * c + 1], c_sample[c]]))
    res = run_bass_kernel_spmd(nc, in_maps, core_ids=list(range(ncores)))
    y_prompt = np.empty(x_prompt.shape, np.float32)
    y_sample = np.empty(x_sample.shape, np.float32)
    for c in range(ncores):
        y = np.asarray(res.results[c]["yout"])
        y_prompt[2 * c] = y[0:sp]
        y_prompt[2 * c + 1] = y[sp:2 * sp]
        y_sample[c] = y[2 * sp:2 * sp + ss]
    return (y_prompt, y_sample)
```

```python
import math
from contextlib import ExitStack

import numpy as np
import ml_dtypes
import concourse.bass as bass
import concourse.mybir as mybir
from concourse.bass_utils import run_bass_kernel_spmd

F32 = mybir.dt.float32
BF16 = mybir.dt.bfloat16
AF = mybir.ActivationFunctionType
ALU = mybir.AluOpType
AX = mybir.AxisListType
NPBF = ml_dtypes.bfloat16

ENGS = ['pe', 'act', 'dve', 'pool', 'sp']
D = 1024
EPS = 1e-6
NEG = -30000.0


class Res:
    __slots__ = ('name', 'writers', 'readers', 'multi', 'sem', 'semval')

    def __init__(self, name, multi=False):
        self.name = name
        self.writers = {}
        self.readers = {}
        self.multi = multi
        self.sem = {}
        self.semval = {}


class Op:
    __slots__ = ('eng', 'fn', 'deps', 'needsig', 'cnt', 'isdma', 'sem', 'semval', 'key', 'idx')


class Prog:
    def __init__(self, nc, stack, same_engine_sync=True):
        self.nc = nc
        self.stack = stack
        self.ops = {e: [] for e in ENGS}
        self.esem = {e: stack.enter_context(nc.semaphore('es_' + e)) for e in ENGS}
        self.same = same_engine_sync
        self.nosame = ('act', 'dve')
        self.nsem = 0
        self.free_sems = {}
        self.phase_res = []
        self.all_sems = []
        self.last = {}
        self.pending = {}
        self.n = 0
        self.nwaits = {}

    def _dma_sem(self, r, eng):
        if eng not in r.sem:
            fl = self.free_sems.setdefault(eng, [])
            if fl:
                r.sem[eng], r.semval[eng] = fl.pop()
            else:
                r.sem[eng] = self.stack.enter_context(self.nc.semaphore('ds%d' % self.nsem))
                r.semval[eng] = 0
                self.nsem += 1
            self.phase_res.append((r, eng))
        return r.sem[eng]

    def op(self, eng, fn, reads=(), writes=(), dma=None, extra=()):
        o = Op()
        o.eng = eng
        o.fn = fn
        o.needsig = False
        o.cnt = 0
        o.isdma = dma is not None
        o.idx = self.n
        self.n += 1
        deps = {}

        def add(d):
            if d is o:
                return
            if (not d.isdma) and d.eng == eng and (eng == 'pe' or (not self.same and eng in self.nosame)):
                return
            k = d.key
            if k not in deps or deps[k].idx < d.idx:
                deps[k] = d

        for d in extra:
            add(d)
        for r in reads:
            if r is None:
                continue
            for d in r.writers.values():
                add(d)
        for w in writes:
            if w is None:
                continue
            for d in w.readers.values():
                add(d)
            if not w.multi and not w.readers:
                for d in w.writers.values():
                    add(d)
        if o.isdma:
            o.sem = self._dma_sem(dma, eng)
            dma.semval[eng] += 16
            o.semval = dma.semval[eng]
            o.key = (eng, id(o.sem))
            self.pending[id(o.sem)] = o
        else:
            o.key = eng
            self.last[eng] = o
        for r in reads:
            if r is None:
                continue
            r.readers[o.key] = o
        for w in writes:
            if w is None:
                continue
            if w.multi:
                if w.readers:
                    w.writers = {o.key: o}
                    w.readers = {}
                else:
                    w.writers[o.key] = o
            else:
                w.writers = {o.key: o}
                w.readers = {}
        o.deps = list(deps.values())
        for d in o.deps:
            d.needsig = True
        self.ops[eng].append(o)
        return o

    def barrier(self, scratch_ap):
        deps = list(self.last.values()) + list(self.pending.values())
        b = self.op('pool', I('memset', scratch_ap, 0.0), extra=deps)
        for en in ('pe', 'act', 'dve', 'sp'):
            self.op(en, None, extra=[b])
        for r, en in self.phase_res:
            self.free_sems.setdefault(en, []).append((r.sem.pop(en), r.semval.pop(en)))
        self.phase_res = []
        self.pending = {}
        self.last = {'pool': b}

    def finalize(self):
        nc = self.nc
        for e in ENGS:
            c = 0
            for o in self.ops[e]:
                if o.isdma or o.fn is None:
                    continue
                if o.needsig:
                    c += 1
                    o.cnt = c
        prog = self
        final_waits = [(o.sem, o.semval) for o in self.pending.values()]

        def emit(engname, eobj):
            waited = {}
            for o in prog.ops[engname]:
                need = {}
                for d in o.deps:
                    if d.isdma:
                        s, v = d.sem, d.semval
                    else:
                        if d.fn is None:
                            continue
                        s, v = prog.esem[d.eng], d.cnt
                    k = id(s)
                    if k not in need or need[k][1] < v:
                        need[k] = (s, v)
                for k, (s, v) in need.items():
                    if waited.get(k, 0) >= v:
                        continue
                    waited[k] = v
                    eobj.wait_ge(s, v)
                    prog.nwaits[engname] = prog.nwaits.get(engname, 0) + 1
                if o.fn is None:
                    continue
                ins = o.fn(eobj)
                if o.isdma:
                    ins.then_inc(o.sem, 16)
                elif o.needsig:
                    ins.then_inc(prog.esem[engname], 1)

        with nc.Block() as block:
            @block.tensor
            def _(e):
                emit('pe', e)

            @block.scalar
            def _(e):
                emit('act', e)

            @block.vector
            def _(e):
                emit('dve', e)

            @block.gpsimd
            def _(e):
                emit('pool', e)
                for s, v in final_waits:
                    e.wait_ge(s, v)

            @block.sync
            def _(e):
                emit('sp', e)


def I(name, *a, **k):
    return lambda e: getattr(e, name)(*a, **k)


def dap(t, off, dims):
    return bass.AP(t, off, [list(d) for d in dims])


def rope_table(smax):
    t = np.arange(smax)
    row = (t // 64).astype(np.float32)
    col = (t % 64).astype(np.float32)
    inv_a = (10000.0 ** (-np.arange(0, 32, 2, dtype=np.float32) / 32)).astype(np.float32)
    inv_b = (500000.0 ** (-np.arange(0, 16, 2, dtype=np.float32) / 16)).astype(np.float32)
    ar = row[:, None] * inv_a[None, :]
    ac = col[:, None] * inv_a[None, :]
    ab = t.astype(np.float32)[:, None] * inv_b[None, :]
    tab = np.concatenate([np.cos(ar), np.cos(ac), np.sin(ar), np.sin(ac), np.cos(ab), np.sin(ab)], axis=1)
    return tab.astype(np.float32)


def dft_tables(S):
    n1 = S // 64
    s1 = np.arange(n1, dtype=np.float64)
    ang1 = 2 * np.pi * np.outer(s1, s1) / n1
    nrm = 1.0 / math.sqrt(S * 64.0)
    c1 = np.cos(ang1) * nrm
    sn1 = np.sin(ang1) * nrm
    d1 = np.concatenate([c1, sn1, -sn1], axis=1).astype(NPBF)
    k1 = np.arange(n1, dtype=np.float64)
    s2 = np.arange(64, dtype=np.float64)
    angt = 2 * np.pi * np.outer(k1, s2) / S
    tw = np.concatenate([np.cos(angt), np.sin(angt)], axis=1).astype(np.float32)
    ang2 = 2 * np.pi * np.outer(s2, s2) / 64.0
    d2 = np.concatenate([np.cos(ang2), np.sin(ang2)], axis=1).astype(NPBF)
    return d1, tw, d2


def const_tables():
    idb = np.eye(128, dtype=np.float32).astype(NPBF)
    onesbd = np.zeros((128, 128), np.float32)
    onesbd[:64, :64] = 1
    onesbd[64:, 64:] = 1
    kk = np.arange(64)[:, None]
    qq = np.arange(192)[None, :]
    mb = np.where((kk <= qq) & (qq <= kk + 128), 0.0, NEG).astype(np.float32)
    maskb = np.concatenate([mb, mb], axis=0).astype(NPBF)
    cols = np.arange(64)
    cstart = np.clip(cols - 8, 0, 48)
    kc = np.arange(64)[:, None]
    qc = np.arange(64)[None, :]
    mc = np.where((kc >= cstart[None, :]) & (kc < cstart[None, :] + 16), 0.0, NEG).astype(np.float32)
    maskc = np.concatenate([mc, mc], axis=0)
    c = np.arange(64, dtype=np.float64)
    ang = 2 * np.pi * np.outer(c, c) / 64.0
    cs = np.zeros((128, 256), np.float64)
    for g in range(2):
        cs[64 * g:64 * g + 64, 64 * g:64 * g + 64] = np.cos(ang)
        cs[64 * g:64 * g + 64, 128 + 64 * g:128 + 64 * g + 64] = -np.sin(ang)
    return idb, onesbd.astype(NPBF), maskb, maskc, cs.astype(NPBF)


class Builder:
    POOL_ACC = (0, 1, 2, 3)
    POOL_ST = (4, 5, 6, 7)

    def __init__(self, seq_lens, depth, n_even, n_odd, same_sync=True):
        self.seq_lens = list(seq_lens)
        self.depth = depth
        self.n_even = n_even
        self.n_odd = n_odd
        self.nseq = len(seq_lens)
        self.ntok = sum(seq_lens)
        self.smax = max(seq_lens)
        self.svals = sorted(set(seq_lens))
        self.row0 = [sum(seq_lens[:i]) for i in range(self.nseq)]
        self.same_sync = same_sync

    def areset(self):
        self.aoff = 0

    def af(self, n):
        o = self.aoff
        self.aoff += n
        assert self.aoff <= self.NA, ("arena overflow", self.aoff, self.NA)
        return self.AR[:, o:o + n]

    def ab(self, n):
        assert n % 2 == 0
        o = self.aoff
        self.aoff += n // 2
        assert self.aoff <= self.NA, ("arena overflow", self.aoff, self.NA)
        return self.AR[:, o:o + n // 2].bitcast(BF16)

    def bank(self, pool=None):
        if pool is None:
            i = self.bank_i
            self.bank_i = (self.bank_i + 1) % 8
        else:
            k = self.pool_i.get(pool, 0)
            self.pool_i[pool] = k + 1
            i = pool[k % len(pool)]
        return self.PS[i], self.rPS[i]

    def R(self, name, multi=False):
        return Res(name, multi)

    def bar(self):
        self.P.barrier(self.BSCR[:, 0:1])

    def build(self):
        nc = bass.Bass("TRN2", target_bir_lowering=False)
        self.nc = nc
        ns, nt, sm = self.nseq, self.ntok, self.smax
        dt = nc.dram_tensor
        X = dict(kind="ExternalInput")
        self.XIN = dt("xin", [nt, D], F32, **X)
        self.CTd = dt("ct", [128, 8 * ns], F32, **X)
        self.PREG = dt("preg", [self.depth, D], F32, **X)
        self.POSTG = dt("postg", [self.depth, D], F32, **X)
        self.ADAW = dt("adaw", [self.depth, D, 3 * D], F32, **X)
        self.ADAB = dt("adab", [self.depth, 3 * D], F32, **X)
        self.WAB = dt("wab", [max(self.n_even, 1), D, 3328], F32, **X)
        self.WOAB = dt("woab", [max(self.n_even, 1), D, D], F32, **X)
        self.QKN = dt("qkn", [max(self.n_even, 1), 128], F32, **X)
        self.WCD = dt("wcd", [max(self.n_odd, 1), D, 3584], F32, **X)
        self.WOCD = dt("wocd", [max(self.n_odd, 1), D, D], F32, **X)
        self.RPBX = dt("rpbx", [max(self.n_odd, 1), 6, 128, 960], F32, **X)
        self.LIND = dt("lind", [max(self.n_odd, 1), 256, 256], F32, **X)
        self.ROPE = dt("rope", [sm, 80], F32, **X)
        self.IDBd = dt("idb", [128, 128], BF16, **X)
        self.ONESd = dt("onesbd", [128, 128], BF16, **X)
        self.SELd = dt("sel", [ns, ns * 128], F32, **X)
        self.MASKBd = dt("maskb", [128, 192], BF16, **X)
        self.MASKCd = dt("maskc", [128, 64], F32, **X)
        self.CS64d = dt("cs64", [128, 256], BF16, **X)
        self.DFT1d, self.TWd, self.DFT2d = {}, {}, {}
        for S in self.svals:
            n1 = S // 64
            self.DFT1d[S] = dt("dft1_%d" % S, [n1, 3 * n1], BF16, **X)
            self.TWd[S] = dt("tw_%d" % S, [n1, 128], F32, **X)
            self.DFT2d[S] = dt("dft2_%d" % S, [64, 128], BF16, **X)
        self.YOUT = dt("yout", [nt, D], F32, kind="ExternalOutput")
        self.XS = dt("xs", [nt, D], F32)
        self.QA_T = dt("qa_t", [4, 128, sm], BF16)
        self.KA_T = dt("ka_t", [128, sm], BF16)
        self.VAUGd = dt("vaug", [sm, 384], BF16)
        self.GATE_T = dt("gate_t", [8, 128, sm], BF16)
        self.QB_T = dt("qb_t", [4, 128, sm], BF16)
        self.KB_T = dt("kb_t", [4, 128, sm], BF16)
        self.VBd = dt("vb", [sm, 512], BF16)
        self.MIX_T = dt("mix_t", [8, 128, sm], BF16)
        self.QC_T = dt("qc_t", [6, 128, sm], BF16)
        self.KC_T = dt("kc_t", [6, 128, sm], BF16)
        self.VCd = dt("vc", [sm, 768], BF16)
        self.YDd = dt("yd", [sm, 512], BF16)
        self.GDd = dt("gd", [128, 64, 512], BF16)
        self.MODD = dt("modd", [ns, 3 * D], F32)

        with ExitStack() as st:
            self.P = Prog(nc, st, self.same_sync)
            sbt = lambda n, s, d: st.enter_context(nc.sbuf_tensor(n, s, d))
            self.NA = 40960
            self.AR = sbt("arena", [128, self.NA], F32)
            self.IDB = sbt("idb_s", [128, 128], BF16)
            self.ONESBD = sbt("ones_s", [128, 128], BF16)
            self.MASKB = sbt("maskb_s", [128, 192], BF16)
            self.MASKC = sbt("maskc_s", [128, 64], F32)
            self.CS64 = sbt("cs64_s", [128, 256], BF16)
            self.SEL = sbt("sel_s", [ns, ns * 128], F32)
            self.CT = sbt("ct_s", [128, 8 * ns], F32)
            self.QKNB = sbt("qkn_s", [128, 128], F32)
            self.ABC = sbt("abc", [128, 3 * D], F32)
            self.EPST = sbt("epst", [128, 1], F32)
            self.BSCR = sbt("bscr", [128, 2], F32)
            self.NEGH = sbt("negh", [128, 16], F32)
            self.JUNK = sbt("junk", [128, 1024], BF16)
            self.DFT1, self.TW, self.DFT2 = {}, {}, {}
            for S in self.svals:
                n1 = S // 64
                self.DFT1[S] = sbt("dft1s_%d" % S, [n1, 3 * n1], BF16)
                self.TW[S] = sbt("tws_%d" % S, [n1, 128], F32)
                self.DFT2[S] = sbt("dft2s_%d" % S, [64, 128], BF16)
            self.PS2 = [st.enter_context(nc.psum_tensor("ps%d" % i, [128, 1024], F32)) for i in range(4)]
            self.PS = [self.PS2[i // 2][:, (i % 2) * 512:(i % 2 + 1) * 512] for i in range(8)]
            self.st2_i = 0
            self.rPS = [Res("ps%d" % i) for i in range(8)]
            self.bank_i = 0
            self.pool_i = {}
            self.rD = {n: Res(n, multi=True) for n in
                       ["modd", "xs", "yout", "qa_t", "ka_t", "vaug", "gate_t", "qb_t", "kb_t", "vb", "mix_t",
                        "qc_t", "kc_t", "vc", "yd", "gd"]}
            self.rC = Res("consts")
            self.rMOD = Res("modrow")
            self.rABC = Res("abc")
            self.rQKN = Res("qkn")

            self.setup()
            prev = (self.XIN, None)
            for l in range(self.depth):
                src, rsrc = prev
                if (self.depth - 1 - l) % 2 == 0:
                    dst, rdst = self.YOUT, self.rD["yout"]
                else:
                    dst, rdst = self.XS, self.rD["xs"]
                prev = (dst, rdst)
                self.adaln(l)
                for si in range(self.nseq):
                    S = self.seq_lens[si]
                    self.modprep(l, si)
                    if l % 2 == 0:
                        j = l // 2
                        self.load_w(self.WAB, j, 3328)
                        self.proj(l, si, src, rsrc, even=True, j=j)
                        self.bar()
                        self.mixer_a(si)
                        self.bar()
                        self.mixer_b(si)
                        self.bar()
                        self.outproj(self.WOAB, j, si, src, rsrc, dst, rdst)
                        self.bar()
                    else:
                        j = l // 2
                        self.load_w(self.WCD, j, 3584)
                        self.proj(l, si, src, rsrc, even=False, j=j)
                        self.bar()
                        self.mixer_c(j, si)
                        self.bar()
                        self.mixer_d(j, si)
                        self.bar()
                        self.outproj(self.WOCD, j, si, src, rsrc, dst, rdst)
                        self.bar()
            self.P.finalize()
        return nc

    def setup(self):
        P = self.P
        rc = self.rC
        ld = lambda t, d: P.op('sp', I('dma_start', out=t[:], in_=d.ap()), writes=[rc], dma=Res("c"))
        ld(self.IDB, self.IDBd)
        ld(self.ONESBD, self.ONESd)
        ld(self.MASKB, self.MASKBd)
        ld(self.MASKC, self.MASKCd)
        ld(self.CS64, self.CS64d)
        ld(self.SEL, self.SELd)
        ld(self.CT, self.CTd)
        for S in self.svals:
            ld(self.DFT1[S], self.DFT1d[S])
            ld(self.TW[S], self.TWd[S])
            ld(self.DFT2[S], self.DFT2d[S])
        P.op('pool', I('memset', self.EPST[:], EPS), writes=[rc])
        P.op('pool', I('memset', self.BSCR[:], 0.0), writes=[rc])
        P.op('pool', I('memset', self.NEGH[:], -0.5), writes=[rc])
        self.bar()
        P.op('act', I('activation', out=self.CT[:], in_=self.CT[:], func=AF.Silu), writes=[rc])
        self.bar()

    def adaln(self, l):
        P = self.P
        ns = self.nseq
        self.areset()
        MR = self.af(3072)
        rMR = Res("mr")
        wts = [self.af(3072) for _ in range(8)]
        rw = [Res("adaw%d" % k) for k in range(8)]
        bb = self.af(3072)
        rb = Res("adab")
        for k in range(8):
            P.op('sp', I('dma_start', out=wts[k], in_=dap(self.ADAW, (l * D + 128 * k) * 3 * D, [[3 * D, 128], [1, 3 * D]])),
                 writes=[rw[k]], dma=rw[k])
        P.op('sp', I('dma_start', out=bb[0:ns, :], in_=dap(self.ADAB, l * 3 * D, [[0, ns], [1, 3 * D]])),
             writes=[rb], dma=rb)
        for half in range(2):
            banks = [self.bank() for _ in range(3)]
            for k in range(8):
                for b in range(3):
                    c0 = half * 1536 + b * 512
                    ps, rps = banks[b]
                    P.op('pe', I('matmul', ps[0:ns, :], self.CT[:, k * ns:(k + 1) * ns], wts[k][:, c0:c0 + 512],
                                                                     start=(k == 0), stop=(k == 7)),
                         reads=[rw[k], self.rC], writes=[rps])
            for b in range(3):
                c0 = half * 1536 + b * 512
                ps, rps = banks[b]
                P.op('dve', I('tensor_tensor', out=MR[0:ns, c0:c0 + 512], in0=ps[0:ns, :], in1=bb[0:ns, c0:c0 + 512], op=ALU.add),
                     reads=[rps, rb], writes=[rMR])
        P.op('pool', I('dma_start', out=self.MODD.ap(), in_=MR[0:ns, :]), reads=[rMR], writes=[self.rD["modd"]], dma=rMR)
        self.bar()

    def modprep(self, l, si):
        P = self.P
        ns = self.nseq
        self.areset()
        pg = self.af(1024)
        qg = self.af(1024)
        rpg, rqg = Res("pg"), Res("qg")
        P.op('sp', I('dma_start', out=pg, in_=dap(self.PREG, l * D, [[0, 128], [1, D]])), writes=[rpg], dma=rpg)
        P.op('sp', I('dma_start', out=qg, in_=dap(self.POSTG, l * D, [[0, 128], [1, D]])), writes=[rqg], dma=rqg)
        MR = self.af(3072)
        rMR = Res("mr")
        P.op('sp', I('dma_start', out=MR[0:ns, :], in_=self.MODD.ap()), reads=[self.rD["modd"]], writes=[rMR], dma=rMR)
        sel = self.SEL[:, si * 128:(si + 1) * 128]
        for part in range(3):
            for h in range(2):
                ps, rps = self.bank()
                c0 = part * 1024 + h * 512
                P.op('pe', I('matmul', ps[:, :], sel, MR[0:ns, c0:c0 + 512], start=True, stop=True),
                     reads=[rMR, self.rC], writes=[rps])
                if part == 0:
                    P.op('dve', I('tensor_copy', out=self.ABC[:, 1024 + h * 512:1024 + (h + 1) * 512], in_=ps[:, :]),
                         reads=[rps], writes=[self.rABC])
                elif part == 1:
                    P.op('dve', I('scalar_tensor_tensor', out=self.ABC[:, h * 512:(h + 1) * 512], in0=ps[:, :], scalar=1.0,
                                                                             in1=pg[:, h * 512:(h + 1) * 512], op0=ALU.add, op1=ALU.mult),
                         reads=[rps, rpg], writes=[self.rABC])
                else:
                    P.op('dve', I('tensor_tensor', out=self.ABC[:, 2048 + h * 512:2048 + (h + 1) * 512], in0=ps[:, :],
                                                                      in1=qg[:, h * 512:(h + 1) * 512], op=ALU.mult),
                         reads=[rps, rqg], writes=[self.rABC])
        self.bar()

    def load_w(self, WD, j, ncol):
        P = self.P
        self.areset()
        self.WB = self.ab(8 * ncol)
        self.wncol = ncol
        self.rWB = Res("wb")
        mark = self.aoff
        st = [self.af(ncol) for _ in range(2)]
        rst = [Res("wst0"), Res("wst1")]
        engs = ['pool', 'act', 'dve']
        for k in range(8):
            b = k % 2
            P.op('sp', I('dma_start', out=st[b], in_=dap(WD, (j * D + 128 * k) * ncol, [[ncol, 128], [1, ncol]])),
                 writes=[rst[b]], dma=rst[b])
            en = engs[k % 3]
            dst = self.WB[:, k * ncol:(k + 1) * ncol]
            if en == 'act':
                P.op('act', I('copy', out=dst, in_=st[b]), reads=[rst[b]], writes=[self.rWB])
            else:
                P.op(en, I('tensor_copy', out=dst, in_=st[b]), reads=[rst[b]], writes=[self.rWB])
        self.bar()
        self.aoff = mark

    def norm_tile(self, xt, rxt, hb, rhb, hf, rhf, ss, rss):
        P = self.P
        P.op('act', I('activation', out=self.JUNK[:], in_=xt, func=AF.Square, accum_out=ss[:, 0:1]), reads=[rxt], writes=[rss])
        P.op('act', I('activation', out=ss[:, 1:2], in_=ss[:, 0:1], func=AF.Sqrt, bias=self.EPST[:, 0:1], scale=1.0 / D),
             reads=[rss, self.rC], writes=[rss])
        P.op('dve', I('reciprocal', out=ss[:, 2:3], in_=ss[:, 1:2]), reads=[rss], writes=[rss])
        P.op('dve', I('scalar_tensor_tensor', out=hf, in0=xt, scalar=ss[:, 2:3], in1=self.ABC[:, 0:1024], op0=ALU.mult, op1=ALU.mult),
             reads=[rxt, rss, self.rABC], writes=[rhf])
        P.op('pool', I('tensor_tensor', out=hb, in0=hf, in1=self.ABC[:, 1024:2048], op=ALU.add), reads=[rhf, self.rABC], writes=[rhb])

    def proj(self, l, si, src, rsrc, even, j):
        P = self.P
        S = self.seq_lens[si]
        row0 = self.row0[si]
        ncol = self.wncol
        WB = self.WB
        ntile = S // 128
        ngrp = S // 512
        ntm = 2304 if even else 768
        nfm = (ncol - ntm) // 128
        XT = [self.af(1024) for _ in range(2)]
        HF = [self.af(1024) for _ in range(2)]
        HB = [self.ab(1024) for _ in range(2)]
        RT = [self.af(80) for _ in range(2)]
        SS = [self.af(4) for _ in range(2)]
        HTg = [self.ab(8 * 512) for _ in range(2)]
        FMst = [self.ab(nfm * 512) for _ in range(2)]
        rXT = [Res("xt%d" % i) for i in range(2)]
        rHF = [Res("hf%d" % i) for i in range(2)]
        rHB = [Res("hb%d" % i) for i in range(2)]
        rRT = [Res("rt%d" % i) for i in range(2)]
        rSS = [Res("ss%d" % i) for i in range(2)]
        rHT = [[Res("ht%d_%d" % (i, s)) for s in range(4)] for i in range(2)]
        rFM = [Res("fm%d" % i) for i in range(2)]
        rS2 = [Res("s2_%d" % i) for i in range(2)]
        rS3 = [Res("s3_%d" % i) for i in range(2)]
        rS4 = [Res("s4_%d" % i) for i in range(2)]
        if even:
            QSQ = self.af(640)
            QN = [self.af(640) for _ in range(2)]
            TMP = [self.af(320) for _ in range(4)]
            QST = self.af(32)
            QATOK = [self.ab(640) for _ in range(2)]
            VAST = [self.ab(384) for _ in range(2)]
            QBTOK = [self.ab(1024) for _ in range(2)]
            VBST = [self.ab(512) for _ in range(2)]
            TB = [self.af(64) for _ in range(4)]
            QATst = [self.ab(5 * 512) for _ in range(2)]
            QBTst = [self.ab(8 * 512) for _ in range(2)]
            rQSQ, rQST = Res("qsq"), Res("qst")
            rQN = [Res("qn%d" % i) for i in range(2)]
            rTMP = [Res("tmp%d" % i) for i in range(4)]
            rTB = [Res("tb%d" % i) for i in range(4)]
            rQATOK = [Res("qatok%d" % i) for i in range(2)]
            rVAST = [Res("vast%d" % i) for i in range(2)]
            rQBTOK = [[Res("qbtok%d_%d" % (i, w)) for w in range(2)] for i in range(2)]
            rVBST = [Res("vbst%d" % i) for i in range(2)]
            rQATst = [Res("qatst%d" % i) for i in range(2)]
            rQBTst = [Res("qbtst%d" % i) for i in range(2)]
            P.op('sp', I('dma_start', out=self.QKNB[:], in_=dap(self.QKN, j * 128, [[0, 128], [1, 128]])), writes=[self.rQKN], dma=self.rQKN)
            for b in range(2):
                P.op('pool', I('memset', VAST[b], 1.0), writes=[rVAST[b]])
        else:
            VCST = [self.ab(768) for _ in range(2)]
            UDT = [self.ab(2 * 512) for _ in range(2)]
            YST = [self.ab(512) for _ in range(2)]
            rVCST = [Res("vcst%d" % i) for i in range(2)]
            rUDT = [Res("udt%d" % i) for i in range(2)]
            rYST = [Res("yst%d" % i) for i in range(2)]
        wview = lambda k, c0, n: WB[:, k * ncol + c0:k * ncol + c0 + n]

        def s1(it):
            b = it % 2
            r = row0 + 128 * it
            P.op('sp', I('dma_start', out=XT[b], in_=dap(src, r * D, [[D, 128], [1, D]])), reads=[rsrc], writes=[rXT[b]], dma=rXT[b])
            if even:
                P.op('sp', I('dma_start', out=RT[b], in_=dap(self.ROPE, 128 * it * 80, [[80, 128], [1, 80]])), writes=[rRT[b]], dma=rRT[b])
            self.norm_tile(XT[b], rXT[b], HB[b], rHB[b], HF[b], rHF[b], SS[b], rSS[b])

        def s2(it):
            b = it % 2
            gb = (it // 4) % 2
            sub = it % 4
            ps, rps = self.bank()
            pb = ps[:, :].bitcast(BF16)
            for k in range(8):
                P.op('pe', I('transpose', out=pb[:, 128 * k:128 * k + 128], in_=HB[b][:, 128 * k:128 * k + 128], identity=self.IDB[:]),
                     reads=[rHB[b], self.rC], writes=[rps])
            dst = HTg[gb].rearrange("p (k t) -> p k t", k=8)[:, :, sub * 128:(sub + 1) * 128]
            P.op('act', I('copy', out=dst, in_=pb[:, 0:1024].rearrange("p (k t) -> p k t", k=8)), reads=[rps], writes=[rHT[gb][sub]])

        def tm_block(it, c0, n):
            gb = (it // 4) % 2
            sub = it % 4
            ps, rps = self.bank()
            for k in range(8):
                P.op('pe', I('matmul', ps[:, 0:n], HTg[gb][:, k * 512 + sub * 128:k * 512 + sub * 128 + 128], wview(k, c0, n),
                                                   start=(k == 0), stop=(k == 7)),
                     reads=[rHT[gb][sub], self.rWB], writes=[rps])
            return ps, rps

        def qk_norm_rope(it, ps, rps, c0, nh, gain0, dstap, rdst):
            b = it % 2
            n = 64 * nh
            xin = ps[:, c0:c0 + n]
            P.op('act', I('activation', out=QSQ[:, 0:n], in_=xin, func=AF.Square), reads=[rps], writes=[rQSQ])
            P.op('dve', I('tensor_reduce', out=QST[:, 0:nh], in_=QSQ[:, 0:n].rearrange("p (h d) -> p h d", h=nh), axis=AX.X, op=ALU.add),
                 reads=[rQSQ], writes=[rQST])
            P.op('dve', I('tensor_scalar', out=QST[:, 8:8 + nh], in0=QST[:, 0:nh], scalar1=1.0 / 64, scalar2=EPS, op0=ALU.mult, op1=ALU.add),
                 reads=[rQST], writes=[rQST])
            P.op('pool', I('tensor_tensor', out=QST[:, 16:16 + nh], in0=QST[:, 8:8 + nh], in1=self.NEGH[:, 0:nh], op=ALU.pow),
                 reads=[rQST, self.rC], writes=[rQST])
            qn = QN[b][:, 0:n]
            P.op('dve', I('tensor_tensor', out=qn.rearrange("p (h d) -> p h d", h=nh), in0=xin.rearrange("p (h d) -> p h d", h=nh),
                                                  in1=QST[:, 16:16 + nh].unsqueeze(2).broadcast_to([128, nh, 64]), op=ALU.mult),
                 reads=[rps, rQST], writes=[rQN[b]])
            P.op('dve', I('tensor_tensor', out=qn.rearrange("p (h d) -> p h d", h=nh), in0=qn.rearrange("p (h d) -> p h d", h=nh),
                                                   in1=self.QKNB[:, gain0:gain0 + 64].unsqueeze(1).broadcast_to([128, nh, 64]), op=ALU.mult),
                 reads=[rQN[b], self.rQKN], writes=[rQN[b]])
            v = qn.rearrange("p (h a x f) -> p h a x f", h=nh, a=2, x=2)
            o = dstap.rearrange("p (h a x f) -> p h a x f", h=nh, a=2, x=2)
            cosv = RT[b][:, 0:32].rearrange("p (a f) -> p a f", a=2).unsqueeze(1).broadcast_to([128, nh, 2, 16])
            sinv = RT[b][:, 32:64].rearrange("p (a f) -> p a f", a=2).unsqueeze(1).broadcast_to([128, nh, 2, 16])
            x1 = v[:, :, :, 0, :]
            x2 = v[:, :, :, 1, :]
            m = nh * 32
            t = [TMP[i][:, 0:m].rearrange("p (h a f) -> p h a f", h=nh, a=2) for i in range(4)]
            P.op('dve', I('tensor_tensor', out=t[0], in0=x1, in1=cosv, op=ALU.mult), reads=[rQN[b], rRT[b]], writes=[rTMP[0]])
            P.op('dve', I('tensor_tensor', out=t[1], in0=x2, in1=sinv, op=ALU.mult), reads=[rQN[b], rRT[b]], writes=[rTMP[1]])
            P.op('dve', I('tensor_tensor', out=o[:, :, :, 0, :], in0=t[0], in1=t[1], op=ALU.subtract), reads=[rTMP[0], rTMP[1]], writes=[rdst])
            P.op('pool', I('tensor_tensor', out=t[2], in0=x1, in1=sinv, op=ALU.mult), reads=[rQN[b], rRT[b]], writes=[rTMP[2]])
            P.op('pool', I('tensor_tensor', out=t[3], in0=x2, in1=cosv, op=ALU.mult), reads=[rQN[b], rRT[b]], writes=[rTMP[3]])
            P.op('pool', I('tensor_tensor', out=o[:, :, :, 1, :], in0=t[2], in1=t[3], op=ALU.add), reads=[rTMP[2], rTMP[3]], writes=[rdst])

        def partial_rope(it, ps, rps, dst, rdst):
            b = it % 2
            pv = ps[:, 0:512].rearrange("p (h d) -> p h d", h=8)
            dv = dst.rearrange("p (h d) -> p h d", h=8)
            x1 = pv[:, :, 0:8]
            x2 = pv[:, :, 8:16]
            cosv = RT[b][:, 64:72].unsqueeze(1).broadcast_to([128, 8, 8])
            sinv = RT[b][:, 72:80].unsqueeze(1).broadcast_to([128, 8, 8])
            t = [TB[i][:, 0:64].rearrange("p (h f) -> p h f", h=8) for i in range(4)]
            P.op('dve', I('tensor_tensor', out=t[0], in0=x1, in1=cosv, op=ALU.mult), reads=[rps, rRT[b], rdst], writes=[rTB[0]])
            P.op('dve', I('tensor_tensor', out=t[1], in0=x2, in1=sinv, op=ALU.mult), reads=[rps, rRT[b]], writes=[rTB[1]])
            P.op('dve', I('tensor_tensor', out=dv[:, :, 0:8], in0=t[0], in1=t[1], op=ALU.subtract), reads=[rTB[0], rTB[1]], writes=[rdst])
            P.op('dve', I('tensor_tensor', out=t[2], in0=x1, in1=sinv, op=ALU.mult), reads=[rps, rRT[b]], writes=[rTB[2]])
            P.op('dve', I('tensor_tensor', out=t[3], in0=x2, in1=cosv, op=ALU.mult), reads=[rps, rRT[b]], writes=[rTB[3]])
            P.op('dve', I('tensor_tensor', out=dv[:, :, 8:16], in0=t[2], in1=t[3], op=ALU.add), reads=[rTB[2], rTB[3]], writes=[rdst])

        def s3_even(it):
            b = it % 2
            gb = (it // 4) % 2
            sub = it % 4
            r = 128 * it
            ps, rps = tm_block(it, 0, 512)
            qk_norm_rope(it, ps, rps, 0, 8, 0, QATOK[b][:, 0:512], rQATOK[b])
            ps, rps = tm_block(it, 512, 256)
            qk_norm_rope(it, ps, rps, 0, 2, 64, QATOK[b][:, 512:640], rQATOK[b])
            P.op('dve', I('tensor_copy', out=VAST[b].rearrange("p (k c) -> p k c", k=2)[:, :, 64:128],
                                                in_=ps[:, 128:256].rearrange("p (k c) -> p k c", k=2)), reads=[rps, rQSQ], writes=[rVAST[b]])
            P.op('pool', I('dma_start', out=dap(self.VAUGd, r * 384, [[384, 128], [1, 384]]), in_=VAST[b]),
                 reads=[rVAST[b]], writes=[self.rD["vaug"]], dma=rVAST[b])
            for which in range(2):
                ps, rps = tm_block(it, 768 + 512 * which, 512)
                dstb = QBTOK[b][:, 512 * which:512 * which + 512]
                P.op('act', I('copy', out=dstb, in_=ps[:, 0:512]), reads=[rps], writes=[rQBTOK[b][which]])
                partial_rope(it, ps, rps, dstb, rQBTOK[b][which])
            ps, rps = tm_block(it, 1792, 512)
            P.op('act', I('copy', out=VBST[b], in_=ps[:, 0:512]), reads=[rps], writes=[rVBST[b]])
            P.op('pool', I('dma_start', out=dap(self.VBd, r * 512, [[512, 128], [1, 512]]), in_=VBST[b]),
                 reads=[rVBST[b]], writes=[self.rD["vb"]], dma=rVBST[b])

        def s3b_even(it):
            b = it % 2
            gb = (it // 4) % 2
            sub = it % 4
            ps2, rps2 = self.bank()
            pb = ps2[:, :].bitcast(BF16)
            for c in range(5):
                P.op('pe', I('transpose', out=pb[:, 128 * c:128 * c + 128], in_=QATOK[b][:, 128 * c:128 * c + 128], identity=self.IDB[:]),
                     reads=[rQATOK[b], self.rC], writes=[rps2])
            dst = QATst[gb].rearrange("p (c t) -> p c t", c=5)[:, :, sub * 128:(sub + 1) * 128]
            P.op('act', I('copy', out=dst, in_=pb[:, 0:640].rearrange("p (c t) -> p c t", c=5)), reads=[rps2], writes=[rQATst[gb]])
            ps3, rps3 = self.bank()
            pb3 = ps3[:, :].bitcast(BF16)
            for c in range(8):
                P.op('pe', I('transpose', out=pb3[:, 128 * c:128 * c + 128], in_=QBTOK[b][:, 128 * c:128 * c + 128], identity=self.IDB[:]),
                     reads=rQBTOK[b] + [self.rC], writes=[rps3])
            dst3 = QBTst[gb].rearrange("p (c t) -> p c t", c=8)[:, :, sub * 128:(sub + 1) * 128]
            P.op('act', I('copy', out=dst3, in_=pb3[:, 0:1024].rearrange("p (c t) -> p c t", c=8)), reads=[rps3], writes=[rQBTst[gb]])
        def s3_odd(it):
            b = it % 2
            r = 128 * it
            ps, rps = tm_block(it, 0, 512)
            P.op('act', I('copy', out=VCST[b][:, 0:512], in_=ps[:, 0:512]), reads=[rps], writes=[rVCST[b]])
            ps, rps = tm_block(it, 512, 256)
            P.op('act', I('copy', out=VCST[b][:, 512:768], in_=ps[:, 0:256]), reads=[rps], writes=[rVCST[b]])
            P.op('pool', I('dma_start', out=dap(self.VCd, r * 768, [[768, 128], [1, 768]]), in_=VCST[b]),
                 reads=[rVCST[b]], writes=[self.rD["vc"]], dma=rVCST[b])

        def s4(g):
            gb = g % 2
            sm = self.smax
            t0 = 512 * g
            for c in range(nfm):
                ps, rps = self.bank()
                for k in range(8):
                    P.op('pe', I('matmul', ps[:, :], wview(k, ntm + 128 * c, 128), HTg[gb][:, k * 512:(k + 1) * 512],
                                                                   start=(k == 0), stop=(k == 7)),
                         reads=rHT[gb] + [self.rWB], writes=[rps])
                dst = FMst[gb][:, c * 512:(c + 1) * 512]
                if even:
                    P.op('act', I('activation', out=dst, in_=ps[:, :], func=AF.Silu), reads=[rps], writes=[rFM[gb]])
                else:
                    if c < 6:
                        P.op('act', I('mul', dst, ps[:, :], 0.125), reads=[rps], writes=[rFM[gb]])
                    elif c < 12:
                        P.op('dve', I('tensor_copy', out=dst, in_=ps[:, :]), reads=[rps], writes=[rFM[gb]])
                    elif c < 20:
                        P.op('act', I('activation', out=dst, in_=ps[:, :], func=AF.Silu), reads=[rps], writes=[rFM[gb]])
                    else:
                        P.op('dve', I('tensor_copy', out=UDT[gb][:, (c - 20) * 512:(c - 19) * 512], in_=ps[:, :]), reads=[rps], writes=[rUDT[gb]])
            fm3 = lambda c0, n: FMst[gb][:, c0 * 512:(c0 + n) * 512].rearrange("p (c t) -> p c t", c=n)
            dd = lambda T, c0, n: dap(T, c0 * 128 * sm + t0, [[sm, 128], [128 * sm, n], [1, 512]])
            if even:
                P.op('pool', I('dma_start', out=dd(self.GATE_T, 0, 8), in_=fm3(0, 8)), reads=[rFM[gb]], writes=[self.rD["gate_t"]], dma=rFM[gb])
                q3 = QATst[gb][:, 0:4 * 512].rearrange("p (c t) -> p c t", c=4)
                P.op('pool', I('dma_start', out=dd(self.QA_T, 0, 4), in_=q3), reads=[rQATst[gb]], writes=[self.rD["qa_t"]], dma=rQATst[gb])
                P.op('pool', I('dma_start', out=dap(self.KA_T, t0, [[sm, 128], [1, 512]]), in_=QATst[gb][:, 4 * 512:5 * 512]),
                     reads=[rQATst[gb]], writes=[self.rD["ka_t"]], dma=rS2[gb])
                qb3 = QBTst[gb][:, 0:4 * 512].rearrange("p (c t) -> p c t", c=4)
                kb3 = QBTst[gb][:, 4 * 512:8 * 512].rearrange("p (c t) -> p c t", c=4)
                P.op('pool', I('dma_start', out=dd(self.QB_T, 0, 4), in_=qb3), reads=[rQBTst[gb]], writes=[self.rD["qb_t"]], dma=rQBTst[gb])
                P.op('pool', I('dma_start', out=dd(self.KB_T, 0, 4), in_=kb3), reads=[rQBTst[gb]], writes=[self.rD["kb_t"]], dma=rS3[gb])
            else:
                P.op('pool', I('dma_start', out=dd(self.QC_T, 0, 6), in_=fm3(0, 6)), reads=[rFM[gb]], writes=[self.rD["qc_t"]], dma=rFM[gb])
                P.op('pool', I('dma_start', out=dd(self.KC_T, 0, 6), in_=fm3(6, 6)), reads=[rFM[gb]], writes=[self.rD["kc_t"]], dma=rS2[gb])
                P.op('pool', I('dma_start', out=dd(self.GATE_T, 0, 8), in_=fm3(12, 8)), reads=[rFM[gb]], writes=[self.rD["gate_t"]], dma=rS3[gb])
                for sub in range(4):
                    yb = (4 * g + sub) % 2
                    ps, rps = self.bank()
                    for jc in range(2):
                        P.op('pe', I('matmul', ps[:, 256 * jc:256 * jc + 256],
                                                                             UDT[gb][:, jc * 512 + sub * 128:jc * 512 + sub * 128 + 128],
                                                                             self.CS64[:, :], start=True, stop=True),
                             reads=[rUDT[gb], self.rC], writes=[rps])
                    src_v = ps[:, :].rearrange("p (j r m) -> p r j m", j=2, r=2)
                    dst_v = YST[yb].rearrange("p (r j m) -> p r j m", r=2, j=2)
                    P.op('act', I('copy', out=dst_v, in_=src_v), reads=[rps], writes=[rYST[yb]])
                    r = t0 + 128 * sub
                    P.op('pool', I('dma_start', out=dap(self.YDd, r * 512, [[512, 128], [1, 512]]), in_=YST[yb]),
                         reads=[rYST[yb]], writes=[self.rD["yd"]], dma=rYST[yb])

        s3 = s3_even if even else s3_odd
        s1(0)
        for it in range(ntile):
            if it + 1 < ntile:
                s1(it + 1)
            s2(it)
            s3(it)
            if even:
                if it % 4 > 0:
                    s3b_even(it - 1)
                if it % 4 == 3:
                    s3b_even(it)
            if it % 4 == 3:
                s4(it // 4)

    def finalize_pair(self, numps, rnum, denps, rden, n, gate, rgate, rd, rrd, ot, rot):
        P = self.P
        P.op('dve', I('reciprocal', out=rd[:, 0:n], in_=denps), reads=[rden], writes=[rrd])
        P.op('dve', I('tensor_tensor', out=rd[:, 0:n], in0=numps, in1=rd[:, 0:n], op=ALU.mult), reads=[rnum, rrd], writes=[rrd])
        P.op('pool', I('tensor_tensor', out=ot[:, 0:n], in0=rd[:, 0:n], in1=gate[:, 0:n], op=ALU.mult), reads=[rrd, rgate], writes=[rot])

    def mixer_a(self, si):
        P = self.P
        S = self.seq_lens[si]
        sm = self.smax
        nch = S // 128
        nqb = S // 512
        self.areset()
        KA2 = [self.ab(S) for _ in range(2)]
        rKA2 = [Res("ka2_%d" % i) for i in range(2)]
        VA = self.ab(nch * 384)
        rVA = Res("va")
        for kv in range(2):
            for hh in range(2):
                rr = Res("ka2l")
                P.op('sp', I('dma_start', out=KA2[kv][64 * hh:64 * hh + 64, :], in_=dap(self.KA_T, 64 * kv * sm, [[sm, 64], [1, S]])),
                     reads=[self.rD["ka_t"]], writes=[rKA2[kv]], dma=rr)
        npiece = (nch + 7) // 8
        rVAp = [Res("va%d" % i) for i in range(npiece)]
        for pi in range(npiece):
            c0 = 8 * pi
            cn = min(8, nch - c0)
            P.op('sp', I('dma_start', out=VA[:, c0 * 384:(c0 + cn) * 384].rearrange("p (c x) -> p c x", c=cn),
                         in_=dap(self.VAUGd, c0 * 128 * 384, [[384, 128], [128 * 384, cn], [1, 384]])),
                 reads=[self.rD["vaug"]], writes=[rVAp[pi]], dma=rVAp[pi])
        NQ = 3
        QT = [self.ab(512) for _ in range(NQ)]
        GT = [self.ab(512) for _ in range(NQ)]
        rQT = [Res("qt%d" % i) for i in range(NQ)]
        rGT = [Res("gt%d" % i) for i in range(NQ)]
        NE = 3
        ET = [self.ab(1024) for _ in range(NE)]
        rET = [Res("et%d" % i) for i in range(NE)]
        RD = [self.af(512) for _ in range(2)]
        rRD = [Res("rd%d" % i) for i in range(2)]
        OT = [self.ab(512) for _ in range(2)]
        rOT = [Res("ot%d" % i) for i in range(2)]
        items = [(qb, c) for qb in range(nqb) for c in range(4)]

        def load(ix):
            qb, c = items[ix]
            s = ix % NQ
            P.op('sp', I('dma_start', out=QT[s], in_=dap(self.QA_T, c * 128 * sm + 512 * qb, [[sm, 128], [1, 512]])),
                 reads=[self.rD["qa_t"]], writes=[rQT[s]], dma=rQT[s])
            P.op('sp', I('dma_start', out=GT[s], in_=dap(self.GATE_T, c * 128 * sm + 512 * qb, [[sm, 128], [1, 512]])),
                 reads=[self.rD["gate_t"]], writes=[rGT[s]], dma=rGT[s])

        ecount = 0
        load(0)
        for ix, (qb, c) in enumerate(items):
            if ix + 1 < len(items):
                load(ix + 1)
            s = ix % NQ
            kv = c // 2
            accs = [self.bank(self.POOL_ACC), self.bank(self.POOL_ACC)]
            pend = []

            def pv(jj, es):
                for e_ in range(2):
                    aps, raps = accs[e_]
                    lo = 64 if e_ == 0 else 0
                    vap = VA[:, jj * 384 + kv * 192 + lo:jj * 384 + kv * 192 + lo + 128]
                    P.op('pe', I('matmul', aps[:, :], vap, ET[es][:, 512 * e_:512 * e_ + 512], start=(jj == 0), stop=(jj == nch - 1)),
                         reads=[rVAp[jj // 8], rET[es]], writes=[raps])

            for jj in range(nch):
                if len(pend) == 2:
                    pv(*pend.pop(0))
                k2 = self.st2_i % 2
                self.st2_i += 1
                st2 = self.PS2[2 + k2]
                rst = [self.rPS[4 + 2 * k2], self.rPS[5 + 2 * k2]]
                for e_ in range(2):
                    P.op('pe', I('matmul', st2[:, 512 * e_:512 * e_ + 512], KA2[kv][64 * e_:64 * e_ + 64, 128 * jj:128 * jj + 128],
                                 QT[s][64 * e_:64 * e_ + 64, :], start=True, stop=True),
                         reads=[rKA2[kv], rQT[s]], writes=[rst[e_]])
                es = ecount % NE
                ecount += 1
                P.op('act', I('activation', out=ET[es], in_=st2[:, 0:1024], func=AF.Exp, scale=0.125), reads=rst, writes=[rET[es]])
                pend.append((jj, es))
            while pend:
                pv(*pend.pop(0))
            fb = ix % 2
            a0, ra0 = accs[0]
            a1, ra1 = accs[1]
            P.op('dve', I('reciprocal', out=RD[fb][0:64, :], in_=a0[64:128, :]), reads=[ra0], writes=[rRD[fb]])
            P.op('dve', I('reciprocal', out=RD[fb][64:128, :], in_=a1[0:64, :]), reads=[ra1], writes=[rRD[fb]])
            P.op('dve', I('tensor_tensor', out=RD[fb][0:64, :], in0=a0[0:64, :], in1=RD[fb][0:64, :], op=ALU.mult), reads=[ra0, rRD[fb]], writes=[rRD[fb]])
            P.op('dve', I('tensor_tensor', out=RD[fb][64:128, :], in0=a1[64:128, :], in1=RD[fb][64:128, :], op=ALU.mult), reads=[ra1, rRD[fb]], writes=[rRD[fb]])
            P.op('pool', I('tensor_tensor', out=OT[fb], in0=RD[fb], in1=GT[s], op=ALU.mult), reads=[rRD[fb], rGT[s]], writes=[rOT[fb]])
            P.op('pool', I('dma_start', out=dap(self.MIX_T, c * 128 * sm + 512 * qb, [[sm, 128], [1, 512]]), in_=OT[fb]),
                 reads=[rOT[fb]], writes=[self.rD["mix_t"]], dma=rOT[fb])

    def acc_alloc(self, n):
        P = self.P
        numps, rnum = self.bank(self.POOL_ACC)
        denps, rden = self.bank(self.POOL_ACC)
        P.op('dve', I('memset', numps[:, 0:n], 0.0), writes=[rnum])
        P.op('dve', I('memset', denps[:, 0:n], 0.0), writes=[rden])
        return numps, rnum, denps, rden

    def pair_attn_block(self, steps, kbd, rkbd, vbd, rvbd, qrhs_fn, rq, bias_fn, rbias, scale, numps, rnum, denps, rden, ET, rET, ecount):
        P = self.P
        NE = len(ET)
        pend = []
        groups = []
        cur, tot = [], 0
        for (jj, c0, n, ex) in steps:
            if cur and tot + n > 512:
                groups.append(cur)
                cur, tot = [], 0
            cur.append((jj, c0, n, ex, tot))
            tot += n
        if cur:
            groups.append(cur)

        def qk(grp):
            ps, rps = self.bank(self.POOL_ST)
            tot = 0
            for (jj, c0, n, ex, off) in grp:
                P.op('pe', I('matmul', ps[:, off:off + n], kbd[:, 128 * jj:128 * jj + 128], qrhs_fn(c0, n, ex), start=True, stop=False),
                     reads=list(rkbd) + [rq], writes=[rps])
                P.op('pe', I('matmul', ps[:, off:off + n], self.IDB[:, :], bias_fn(c0, n, ex), start=False, stop=True),
                     reads=[rbias, self.rC], writes=[rps])
                tot = off + n
            es = ecount[0] % NE
            ecount[0] += 1
            P.op('act', I('activation', out=ET[es][:, 0:tot], in_=ps[:, 0:tot], func=AF.Exp, scale=scale), reads=[rps], writes=[rET[es]])
            return es

        def pv(grp, es):
            for (jj, c0, n, ex, off) in grp:
                P.op('pe', I('matmul', numps[:, c0:c0 + n], vbd[:, 128 * jj:128 * jj + 128], ET[es][:, off:off + n], start=False, stop=False, skip_group_check=True),
                     reads=list(rvbd) + [rET[es]], writes=[rnum])
                P.op('pe', I('matmul', denps[:, c0:c0 + n], self.ONESBD[:, :], ET[es][:, off:off + n], start=False, stop=False, skip_group_check=True),
                     reads=[self.rC, rET[es]], writes=[rden])

        for grp in groups:
            if len(pend) == 2:
                pv(*pend.pop(0))
            es = qk(grp)
            pend.append((grp, es))
        while pend:
            pv(*pend.pop(0))

    def mixer_b(self, si):
        P = self.P
        S = self.seq_lens[si]
        sm = self.smax
        self.areset()
        QT = self.ab(S)
        KT = self.ab(S)
        ACCN = self.af(S)
        ACCD = self.af(S)
        rQT, rKT, rACCN, rACCD = Res("bq"), Res("bk"), Res("accn"), Res("accd")
        NR = 4
        KBD = [self.ab(10 * 128) for _ in range(NR)]
        VBD = [self.ab(10 * 128) for _ in range(NR)]
        rKBD = [Res("kbd%d" % i) for i in range(NR)]
        rKBD2 = [Res("kbd2_%d" % i) for i in range(NR)]
        rVBD = [[Res("vbd%d_%d" % (i, h)) for h in range(2)] for i in range(NR)]
        NE = 4
        ET = [self.ab(512) for _ in range(NE)]
        rET = [Res("bet%d" % i) for i in range(NE)]
        GT = [self.ab(512) for _ in range(2)]
        rGT = [Res("bgt%d" % i) for i in range(2)]
        OT = [self.ab(512) for _ in range(2)]
        rOT = [Res("bot%d" % i) for i in range(2)]
        rVL = [[Res("vl%d_%d" % (i, h)) for h in range(2)] for i in range(NR)]
        for i in range(NR):
            P.op('pool', I('memset', KBD[i], 0.0), writes=[rKBD[i], rKBD2[i]])
            P.op('pool', I('memset', VBD[i], 0.0), writes=rVBD[i])
        ecount = [0]
        bcount = [0]
        for c in range(4):
            P.op('sp', I('dma_start', out=QT, in_=dap(self.QB_T, c * 128 * sm, [[sm, 128], [1, S]])), reads=[self.rD["qb_t"]], writes=[rQT], dma=rQT)
            P.op('sp', I('dma_start', out=KT, in_=dap(self.KB_T, c * 128 * sm, [[sm, 128], [1, S]])), reads=[self.rD["kb_t"]], writes=[rKT], dma=rKT)
            P.op('pool', I('memset', ACCN, 0.0), writes=[rACCN])
            P.op('pool', I('memset', ACCD, 0.0), writes=[rACCD])
            blocks = []
            for dil in (1, 4, 16):
                L = S // dil
                QBW = min(512, L)
                for rho in range(dil):
                    for q0 in range(0, L, QBW):
                        blocks.append((dil, rho, q0, QBW, L))

            accd_ = {}

            def prolog(bi):
                dil, rho, q0, QBW, L = blocks[bi]
                slot = bi % NR
                accd_[bi] = self.acc_alloc(QBW)
                j0 = max(0, q0 // 64 - 1)
                j1 = min(L // 64, (q0 + QBW) // 64 + 1)
                nk = j1 - j0
                for hh in range(2):
                    srcap = KT[64 * hh:64 * hh + 64, :]
                    srcv = bass.AP(srcap.tensor, srcap.offset + rho + dil * 64 * j0, [list(srcap.ap[0]), [64 * dil, nk], [dil, 64]])
                    dstv = KBD[slot][64 * hh:64 * hh + 64, 0:nk * 128].rearrange("p (j x) -> p j x", j=nk)[:, :, 64 * hh:64 * hh + 64]
                    if hh == 0:
                        P.op('pool', I('tensor_copy', out=dstv, in_=srcv), reads=[rKT], writes=[rKBD[slot]])
                    else:
                        P.op('act', I('copy', out=dstv, in_=srcv), reads=[rKT], writes=[rKBD2[slot]])
                    vsrc = dap(self.VBd, (rho + dil * 64 * j0) * 512 + (2 * c + hh) * 64, [[dil * 512, 64], [64 * dil * 512, nk], [1, 64]])
                    vdst = VBD[slot][64 * hh:64 * hh + 64, 0:nk * 128].rearrange("p (j x) -> p j x", j=nk)[:, :, 64 * hh:64 * hh + 64]
                    P.op('sp', I('dma_start', out=vdst, in_=vsrc), reads=[self.rD["vb"]], writes=[rVBD[slot][hh]], dma=rVBD[slot][hh])

            def body(bi):
                dil, rho, q0, QBW, L = blocks[bi]
                slot = bi % NR
                j0 = max(0, q0 // 64 - 1)
                j1 = min(L // 64, (q0 + QBW) // 64 + 1)
                steps = []
                for j in range(j0, j1):
                    qa = max(q0, 64 * j - 64)
                    qe = min(q0 + QBW, 64 * j + 128, L)
                    if qe <= qa:
                        continue
                    steps.append((j - j0, qa - q0, qe - qa, (qa, qa - (64 * j - 64))))
                numps, rnum, denps, rden = accd_.pop(bi)

                def qrhs(c0, n, ex):
                    return bass.AP(QT.tensor, QT.offset + rho + dil * ex[0], [list(QT.ap[0]), [dil, n]])

                def biasf(c0, n, ex):
                    return self.MASKB[:, ex[1]:ex[1] + n]

                self.pair_attn_block(steps, KBD[slot], [rKBD[slot], rKBD2[slot]], VBD[slot], rVBD[slot], qrhs, rQT, biasf, self.rC, 0.125,
                                     numps[:, 0:QBW], rnum, denps[:, 0:QBW], rden, ET, rET, ecount)
                accn = bass.AP(ACCN.tensor, ACCN.offset + rho + dil * q0, [list(ACCN.ap[0]), [dil, QBW]])
                accd = bass.AP(ACCD.tensor, ACCD.offset + rho + dil * q0, [list(ACCD.ap[0]), [dil, QBW]])
                P.op('dve', I('tensor_tensor', out=accn, in0=numps[:, 0:QBW], in1=accn, op=ALU.add), reads=[rnum, rACCN], writes=[rACCN])
                P.op('dve', I('tensor_tensor', out=accd, in0=denps[:, 0:QBW], in1=accd, op=ALU.add), reads=[rden, rACCD], writes=[rACCD])

            prolog(0)
            for bi in range(len(blocks)):
                if bi + 1 < len(blocks):
                    prolog(bi + 1)
                body(bi)
            for qb in range(S // 512):
                fb = bcount[0] % 2
                bcount[0] += 1
                P.op('sp', I('dma_start', out=GT[fb], in_=dap(self.GATE_T, (4 + c) * 128 * sm + 512 * qb, [[sm, 128], [1, 512]])),
                     reads=[self.rD["gate_t"]], writes=[rGT[fb]], dma=rGT[fb])
                sl = slice(512 * qb, 512 * qb + 512)
                P.op('dve', I('reciprocal', out=ACCD[:, sl], in_=ACCD[:, sl]), reads=[rACCD], writes=[rACCD])
                P.op('dve', I('tensor_tensor', out=ACCN[:, sl], in0=ACCN[:, sl], in1=ACCD[:, sl], op=ALU.mult), reads=[rACCD, rACCN], writes=[rACCN])
                P.op('pool', I('tensor_tensor', out=OT[fb], in0=ACCN[:, sl], in1=GT[fb], op=ALU.mult), reads=[rACCN, rGT[fb]], writes=[rOT[fb]])
                P.op('pool', I('dma_start', out=dap(self.MIX_T, (4 + c) * 128 * sm + 512 * qb, [[sm, 128], [1, 512]]), in_=OT[fb]),
                     reads=[rOT[fb]], writes=[self.rD["mix_t"]], dma=rOT[fb])

    def mixer_c(self, j, si):
        P = self.P
        S = self.seq_lens[si]
        sm = self.smax
        rows = S // 64
        self.areset()
        QT = self.ab(S)
        KT = self.ab(S)
        rQT, rKT = Res("cq"), Res("ck")
        BTF = self.af(960)
        BT = self.ab(960)
        rBTF, rBT = Res("btf"), Res("bt")
        NR = 4
        KBD = [self.ab(16 * 128) for _ in range(NR)]
        VBD = [self.ab(16 * 128) for _ in range(NR)]
        rKBD = [Res("ckbd%d" % i) for i in range(NR)]
        rVBD = [[Res("cvbd%d_%d" % (i, h)) for h in range(2)] for i in range(NR)]
        NE = 4
        ET = [self.ab(512) for _ in range(NE)]
        rET = [Res("cet%d" % i) for i in range(NE)]
        GT = [self.ab(512) for _ in range(2)]
        rGT = [Res("cgt%d" % i) for i in range(2)]
        RD = [self.af(512) for _ in range(2)]
        rRD = [Res("crd%d" % i) for i in range(2)]
        OT = [self.ab(512) for _ in range(2)]
        rOT = [Res("cot%d" % i) for i in range(2)]
        rVL = [[Res("cvl%d_%d" % (i, h)) for h in range(2)] for i in range(NR)]
        for i in range(NR):
            P.op('pool', I('memset', KBD[i], 0.0), writes=[rKBD[i]])
            P.op('pool', I('memset', VBD[i], 0.0), writes=rVBD[i])
        r0f = lambda r: min(max(r - 4, 0), rows - 8)
        nblk = rows // 8
        ecount = [0]
        fcount = [0]
        for c in range(6):
            P.op('sp', I('dma_start', out=QT, in_=dap(self.QC_T, c * 128 * sm, [[sm, 128], [1, S]])), reads=[self.rD["qc_t"]], writes=[rQT], dma=rQT)
            P.op('sp', I('dma_start', out=KT, in_=dap(self.KC_T, c * 128 * sm, [[sm, 128], [1, S]])), reads=[self.rD["kc_t"]], writes=[rKT], dma=rKT)
            P.op('sp', I('dma_start', out=BTF, in_=dap(self.RPBX, (j * 6 + c) * 128 * 960, [[960, 128], [1, 960]])), writes=[rBTF], dma=rBTF)
            P.op('dve', I('tensor_tensor', out=BT.rearrange("p (a q) -> p a q", a=15), in0=BTF.rearrange("p (a q) -> p a q", a=15),
                                                  in1=self.MASKC[:, :].unsqueeze(1).broadcast_to([128, 15, 64]), op=ALU.add),
                 reads=[rBTF, self.rC], writes=[rBT])

            def krange(b):
                R0 = 8 * b
                return r0f(R0), r0f(R0 + 7) + 8

            accd_ = {}

            def prolog(b):
                slot = b % NR
                accd_[b] = self.acc_alloc(512)
                k0, k1 = krange(b)
                nk = k1 - k0
                for hh in range(2):
                    srcv = KT[64 * hh:64 * hh + 64, 64 * k0:64 * k1].rearrange("p (j x) -> p j x", j=nk)
                    dstv = KBD[slot][64 * hh:64 * hh + 64, 0:nk * 128].rearrange("p (j x) -> p j x", j=nk)[:, :, 64 * hh:64 * hh + 64]
                    P.op('pool', I('tensor_copy', out=dstv, in_=srcv), reads=[rKT], writes=[rKBD[slot]])
                    vsrc = dap(self.VCd, 64 * k0 * 768 + (2 * c + hh) * 64, [[768, 64], [64 * 768, nk], [1, 64]])
                    vdst = VBD[slot][64 * hh:64 * hh + 64, 0:nk * 128].rearrange("p (j x) -> p j x", j=nk)[:, :, 64 * hh:64 * hh + 64]
                    P.op('sp', I('dma_start', out=vdst, in_=vsrc), reads=[self.rD["vc"]], writes=[rVBD[slot][hh]], dma=rVBD[slot][hh])

            def body(b):
                slot = b % NR
                R0 = 8 * b
                k0, k1 = krange(b)
                steps = []
                for kr in range(k0, k1):
                    valid = [r for r in range(R0, R0 + 8) if r0f(r) <= kr <= r0f(r) + 7]
                    if not valid:
                        continue
                    ra, rb = valid[0], valid[-1] + 1
                    assert valid == list(range(ra, rb))
                    e0 = ra - kr + 7
                    assert 0 <= e0 and e0 + (rb - ra) <= 15
                    steps.append((kr - k0, 64 * (ra - R0), 64 * (rb - ra), (64 * ra, e0)))
                numps, rnum, denps, rden = accd_.pop(b)

                def qrhs(c0, n, ex):
                    return QT[:, ex[0]:ex[0] + n]

                def biasf(c0, n, ex):
                    return BT[:, 64 * ex[1]:64 * ex[1] + n]

                self.pair_attn_block(steps, KBD[slot], [rKBD[slot]], VBD[slot], rVBD[slot], qrhs, rQT, biasf, rBT, 1.0,
                                     numps[:, :], rnum, denps[:, :], rden, ET, rET, ecount)
                fb = fcount[0] % 2
                fcount[0] += 1
                P.op('sp', I('dma_start', out=GT[fb], in_=dap(self.GATE_T, c * 128 * sm + 512 * b, [[sm, 128], [1, 512]])),
                     reads=[self.rD["gate_t"]], writes=[rGT[fb]], dma=rGT[fb])
                self.finalize_pair(numps[:, :], rnum, denps[:, :], rden, 512, GT[fb], rGT[fb], RD[fb], rRD[fb], OT[fb], rOT[fb])
                P.op('pool', I('dma_start', out=dap(self.MIX_T, c * 128 * sm + 512 * b, [[sm, 128], [1, 512]]), in_=OT[fb]),
                     reads=[rOT[fb]], writes=[self.rD["mix_t"]], dma=rOT[fb])

            prolog(0)
            for b in range(nblk):
                if b + 1 < nblk:
                    prolog(b + 1)
                body(b)

    def mixer_d(self, j, si):
        P = self.P
        S = self.seq_lens[si]
        sm = self.smax
        n1 = S // 64
        d1, tw, d2 = self.DFT1[S], self.TW[S], self.DFT2[S]
        C1 = d1[:, 0:n1]
        S1 = d1[:, n1:2 * n1]
        NS1 = d1[:, 2 * n1:3 * n1]
        self.areset()
        YB = [self.ab(8 * 512) for _ in range(2)]
        GP = [self.ab(8 * 512) for _ in range(2)]
        T1 = [self.af(256) for _ in range(2)]
        rYB = [Res("yb%d" % i) for i in range(2)]
        rGP = [Res("gp%d" % i) for i in range(2)]
        rT1 = [Res("t1%d" % i) for i in range(2)]
        for blk in range(8):
            b = blk % 2
            P.op('sp', I('dma_start', out=YB[b][0:n1, :].rearrange("p (s c) -> p s c", s=8),
                                                           in_=dap(self.YDd, blk * 8 * 512, [[64 * 512, n1], [512, 8], [1, 512]])),
                 reads=[self.rD["yd"]], writes=[rYB[b]], dma=rYB[b])
            yv = YB[b][0:n1, :].rearrange("p (s c) -> p s c", s=8)
            for pr in range(4):
                gr, rgr = self.bank()
                gi, rgi = self.bank()
                yr = yv[:, 2 * pr:2 * pr + 2, 0:256]
                yi = yv[:, 2 * pr:2 * pr + 2, 256:512]
                P.op('pe', I('matmul', gr[0:n1, :], C1, yr, start=True, stop=False), reads=[rYB[b], self.rC], writes=[rgr])
                P.op('pe', I('matmul', gr[0:n1, :], S1, yi, start=False, stop=True), reads=[rYB[b], self.rC], writes=[rgr])
                P.op('pe', I('matmul', gi[0:n1, :], C1, yi, start=True, stop=False), reads=[rYB[b], self.rC], writes=[rgi])
                P.op('pe', I('matmul', gi[0:n1, :], NS1, yr, start=False, stop=True), reads=[rYB[b], self.rC], writes=[rgi])
                for u in range(2):
                    s2 = blk * 8 + 2 * pr + u
                    tc_ = tw[:, s2:s2 + 1]
                    ts_ = tw[:, 64 + s2:64 + s2 + 1]
                    grs = gr[0:n1, 256 * u:256 * u + 256]
                    gis = gi[0:n1, 256 * u:256 * u + 256]
                    o0 = (2 * pr + u) * 512
                    t1a = T1[0][0:n1, :]
                    t1b = T1[1][0:n1, :]
                    P.op('dve', I('tensor_scalar', out=t1a, in0=gis, scalar1=ts_, scalar2=None, op0=ALU.mult),
                         reads=[rgi, self.rC], writes=[rT1[0]])
                    P.op('dve', I('scalar_tensor_tensor', out=GP[b][0:n1, o0:o0 + 256], in0=grs, scalar=tc_, in1=t1a,
                                                                                                       op0=ALU.mult, op1=ALU.add),
                         reads=[rgr, rT1[0], self.rC], writes=[rGP[b]])
                    P.op('dve', I('tensor_scalar', out=t1b, in0=grs, scalar1=ts_, scalar2=None, op0=ALU.mult),
                         reads=[rgr, self.rC], writes=[rT1[1]])
                    P.op('dve', I('scalar_tensor_tensor', out=GP[b][0:n1, o0 + 256:o0 + 512], in0=gis, scalar=tc_, in1=t1b,
                                                                                                       op0=ALU.mult, op1=ALU.subtract),
                         reads=[rgi, rT1[1], self.rC], writes=[rGP[b]])
            P.op('pool', I('dma_start', out=dap(self.GDd, blk * 8 * 512, [[64 * 512, n1], [1, 8 * 512]]), in_=GP[b][0:n1, :]),
                 reads=[rGP[b]], writes=[self.rD["gd"]], dma=rGP[b])
        self.bar()
        self.areset()
        FT = self.ab(2 * S)
        rFT = Res("ft")
        GB = [self.ab(8 * 512) for _ in range(2)]
        rGB = [Res("gb%d" % i) for i in range(2)]
        C2 = d2[:, 0:64]
        S2_ = d2[:, 64:128]
        for kb in range(n1 // 8):
            b = kb % 2
            P.op('sp', I('dma_start', out=GB[b][0:64, :].rearrange("p (k c) -> p k c", k=8),
                                                         in_=dap(self.GDd, kb * 8 * 64 * 512, [[512, 64], [64 * 512, 8], [1, 512]])),
                 reads=[self.rD["gd"]], writes=[rGB[b]], dma=rGB[b])
            for fc in range(2):
                ps, rps = self.bank()
                for ki in range(8):
                    gr_ = GB[b][0:64, ki * 512 + fc * 128:ki * 512 + fc * 128 + 128]
                    gi_ = GB[b][0:64, ki * 512 + 256 + fc * 128:ki * 512 + 256 + fc * 128 + 128]
                    P.op('pe', I('matmul', ps[:, 64 * ki:64 * ki + 64], gr_, C2, start=True, stop=False),
                         reads=[rGB[b], self.rC], writes=[rps])
                    P.op('pe', I('matmul', ps[:, 64 * ki:64 * ki + 64], gi_, S2_, start=False, stop=True),
                         reads=[rGB[b], self.rC], writes=[rps])
                dst = bass.AP(FT.tensor, FT.offset + fc * S + kb * 8, [list(FT.ap[0]), [1, 8], [n1, 64]])
                P.op('dve', I('tensor_copy', out=dst, in_=ps[:, :].rearrange("p (k q) -> p k q", k=8)), reads=[rps], writes=[rFT])
        LF = self.af(512)
        LB = self.ab(512)
        rLF, rLB = Res("lf"), Res("lb")
        P.op('sp', I('dma_start', out=LF.rearrange("p (c x) -> p c x", c=2), in_=dap(self.LIND, j * 256 * 256, [[256, 128], [128 * 256, 2], [1, 256]])),
             writes=[rLF], dma=rLF)
        P.op('dve', I('tensor_copy', out=LB, in_=LF), reads=[rLF], writes=[rLB])
        GT = [self.ab(512) for _ in range(2)]
        rGT = [Res("dgt%d" % i) for i in range(2)]
        OT = [self.ab(512) for _ in range(2)]
        rOT = [Res("dot%d" % i) for i in range(2)]
        cnt = 0
        for ec in range(2):
            for kb in range(S // 512):
                fb = cnt % 2
                cnt += 1
                P.op('sp', I('dma_start', out=GT[fb], in_=dap(self.GATE_T, (6 + ec) * 128 * sm + 512 * kb, [[sm, 128], [1, 512]])),
                     reads=[self.rD["gate_t"]], writes=[rGT[fb]], dma=rGT[fb])
                ps, rps = self.bank()
                for fc in range(2):
                    P.op('pe', I('matmul', ps[:, :], LB[:, fc * 256 + ec * 128:fc * 256 + ec * 128 + 128],
                                                                             FT[:, fc * S + 512 * kb:fc * S + 512 * kb + 512], start=(fc == 0), stop=(fc == 1)),
                         reads=[rLB, rFT], writes=[rps])
                P.op('dve', I('tensor_tensor', out=OT[fb], in0=ps[:, :], in1=GT[fb], op=ALU.mult), reads=[rps, rGT[fb]], writes=[rOT[fb]])
                P.op('pool', I('dma_start', out=dap(self.MIX_T, (6 + ec) * 128 * sm + 512 * kb, [[sm, 128], [1, 512]]), in_=OT[fb]),
                     reads=[rOT[fb]], writes=[self.rD["mix_t"]], dma=rOT[fb])

    def outproj(self, WOD, j, si, src, rsrc, dst, rdst):
        P = self.P
        S = self.seq_lens[si]
        sm = self.smax
        row0 = self.row0[si]
        self.areset()
        WO = self.ab(8 * 1024)
        rWO = Res("wo")
        st = [self.af(1024) for _ in range(2)]
        rst = [Res("wos0"), Res("wos1")]
        for k in range(8):
            b = k % 2
            P.op('sp', I('dma_start', out=st[b], in_=dap(WOD, (j * D + 128 * k) * D, [[D, 128], [1, D]])), writes=[rst[b]], dma=rst[b])
            P.op('pool' if k % 2 else 'dve', I('tensor_copy', out=WO[:, k * 1024:(k + 1) * 1024], in_=st[b]), reads=[rst[b]], writes=[rWO])
        MT = [self.ab(8 * 512) for _ in range(2)]
        rMT = [Res("mt%d" % i) for i in range(2)]
        XT = [self.af(1024) for _ in range(2)]
        rXT = [Res("oxt%d" % i) for i in range(2)]
        YT = [self.af(1024) for _ in range(2)]
        rYT = [Res("oyt%d" % i) for i in range(2)]
        SS = [self.af(8) for _ in range(2)]
        rSS = [Res("oss%d" % i) for i in range(2)]
        ntile = S // 128

        def loadg(g):
            gb = g % 2
            P.op('sp', I('dma_start', out=MT[gb].rearrange("p (c t) -> p c t", c=8), in_=dap(self.MIX_T, 512 * g, [[sm, 128], [128 * sm, 8], [1, 512]])),
                 reads=[self.rD["mix_t"]], writes=[rMT[gb]], dma=rMT[gb])

        loadg(0)
        for it in range(ntile):
            g, sub = it // 4, it % 4
            gb = g % 2
            b = it % 2
            if sub == 0 and g + 1 < S // 512:
                loadg(g + 1)
            r = row0 + 128 * it
            P.op('sp', I('dma_start', out=XT[b], in_=dap(src, r * D, [[D, 128], [1, D]])), reads=[rsrc], writes=[rXT[b]], dma=rXT[b])
            banks = [self.bank(), self.bank()]
            for nb in range(2):
                ps, rps = banks[nb]
                for c in range(8):
                    P.op('pe', I('matmul', ps[:, :], MT[gb][:, c * 512 + sub * 128:c * 512 + sub * 128 + 128],
                                                                                     WO[:, c * 1024 + nb * 512:c * 1024 + nb * 512 + 512],
                                                                                     start=(c == 0), stop=(c == 7)),
                         reads=[rMT[gb], rWO], writes=[rps])
                P.op('act', I('activation', out=self.JUNK[:, 0:512], in_=ps[:, :], func=AF.Square, accum_out=SS[b][:, nb:nb + 1]),
                     reads=[rps], writes=[rSS[b]])
            P.op('dve', I('tensor_tensor', out=SS[b][:, 2:3], in0=SS[b][:, 0:1], in1=SS[b][:, 1:2], op=ALU.add), reads=[rSS[b]], writes=[rSS[b]])
            P.op('act', I('activation', out=SS[b][:, 3:4], in_=SS[b][:, 2:3], func=AF.Sqrt, bias=self.EPST[:, 0:1], scale=1.0 / D),
                 reads=[rSS[b], self.rC], writes=[rSS[b]])
            P.op('dve', I('reciprocal', out=SS[b][:, 4:5], in_=SS[b][:, 3:4]), reads=[rSS[b]], writes=[rSS[b]])
            for nb in range(2):
                ps, rps = banks[nb]
                sl = slice(nb * 512, nb * 512 + 512)
                P.op('dve', I('scalar_tensor_tensor', out=YT[b][:, sl], in0=ps[:, :], scalar=SS[b][:, 4:5],
                                                                                       in1=self.ABC[:, 2048 + nb * 512:2048 + nb * 512 + 512],
                                                                                       op0=ALU.mult, op1=ALU.mult),
                     reads=[rps, rSS[b], self.rABC], writes=[rYT[b]])
            P.op('pool', I('tensor_tensor', out=YT[b], in0=YT[b], in1=XT[b], op=ALU.add), reads=[rYT[b], rXT[b]], writes=[rYT[b]])
            P.op('pool', I('dma_start', out=dap(dst, r * D, [[D, 128], [1, D]]), in_=YT[b]), reads=[rYT[b]], writes=[rdst], dma=rYT[b])


_W_EVEN_PERM = None


def _even_perm():
    idx = list(range(0, 768)) + list(range(1280, 2816)) + list(range(768, 1280)) + list(range(2816, 3328))
    return np.array(idx)


def _odd_perm():
    idx = list(range(1536, 2304)) + list(range(0, 1536)) + list(range(2304, 3072)) + list(range(3328, 3584)) + list(range(3072, 3328))
    return np.array(idx)


def _rpb_expand(rpb):
    n_odd = rpb.shape[0]
    kc = np.arange(64)[:, None]
    qc = np.arange(64)[None, :]
    cidx = np.clip(kc - qc + 15, 0, 30)
    out = np.zeros((n_odd, 6, 128, 15, 64), np.float32)
    for e in range(15):
        dr = 14 - e
        g = rpb[:, :, dr, :][:, :, cidx]
        g = g.reshape(n_odd, 6, 2, 64, 64).reshape(n_odd, 6, 128, 64)
        out[:, :, :, e, :] = g
    return np.ascontiguousarray(out.reshape(n_odd, 6, 128, 960))


def make_shared_inputs(seq_lens, pre_g, post_g, ada_w, ada_b, w_in_ab, w_out_ab, qn_a, kn_a, w_in_cd, w_out_cd, rpb_c, lin_d):
    f = lambda a: np.ascontiguousarray(np.asarray(a, dtype=np.float32))
    smax = max(seq_lens)
    ns = len(seq_lens)
    idb, onesbd, maskb, maskc, cs64 = const_tables()
    sel = np.zeros((ns, ns * 128), np.float32)
    for i in range(ns):
        sel[i, i * 128:(i + 1) * 128] = 1.0
    m = {
        "preg": f(pre_g), "postg": f(post_g), "adaw": f(ada_w), "adab": f(ada_b),
        "wab": f(np.asarray(w_in_ab)[:, :, _even_perm()]), "woab": f(w_out_ab),
        "qkn": f(np.concatenate([np.asarray(qn_a), np.asarray(kn_a)], axis=1)),
        "wcd": f(np.asarray(w_in_cd)[:, :, _odd_perm()]), "wocd": f(w_out_cd),
        "rpbx": _rpb_expand(np.asarray(rpb_c, dtype=np.float32)), "lind": f(lin_d),
        "rope": rope_table(smax), "idb": idb, "onesbd": onesbd, "sel": sel, "maskb": maskb, "maskc": maskc, "cs64": cs64,
    }
    for S in sorted(set(seq_lens)):
        d1, tw, d2 = dft_tables(S)
        m["dft1_%d" % S] = d1
        m["tw_%d" % S] = tw
        m["dft2_%d" % S] = d2
    return m


def core_inputs(shared, xs, cs):
    ns = len(xs)
    m = dict(shared)
    m["xin"] = np.ascontiguousarray(np.concatenate(xs, axis=0).astype(np.float32))
    c = np.stack(cs, axis=0).astype(np.float32)
    ct = c.reshape(ns, 8, 128).transpose(2, 1, 0).reshape(128, 8 * ns)
    m["ct"] = np.ascontiguousarray(ct)
    return m


_NC_CACHE = {}


def kernel(x_prompt, x_sample, c_prompt, c_sample, pre_g, post_g, ada_w, ada_b,
           w_in_ab, w_out_ab, qn_a, kn_a, w_in_cd, w_out_cd, rpb_c, lin_d):
    x_prompt = np.asarray(x_prompt)
    x_sample = np.asarray(x_sample)
    c_prompt = np.asarray(c_prompt)
    c_sample = np.asarray(c_sample)
    ncores = 8
    sp, ss = x_prompt.shape[1], x_sample.shape[1]
    seq_lens = [sp, sp, ss]
    depth = np.asarray(pre_g).shape[0]
    key = (tuple(seq_lens), depth)
    if key not in _NC_CACHE:
        _NC_CACHE[key] = Builder(seq_lens, depth, (depth + 1) // 2, depth // 2).build()
    nc = _NC_CACHE[key]
    shared = make_shared_inputs(seq_lens, pre_g, post_g, ada_w, ada_b, w_in_ab, w_out_ab, qn_a, kn_a, w_in_cd, w_out_cd, rpb_c, lin_d)
    in_maps = []
    for c in range(ncores):
        in_maps.append(core_inputs(shared, [x_prompt[2 * c], x_prompt[2 * c + 1], x_sample[c]],
                                   [c_prompt[2 * c], c_prompt[2 * c + 1], c_sample[c]]))
    res = run_bass_kernel_spmd(nc, in_maps, core_ids=list(range(ncores)))
    y_prompt = np.empty(x_prompt.shape, np.float32)
    y_sample = np.empty(x_sample.shape, np.float32)
    for c in range(ncores):
        y = np.asarray(res.results[c]["yout"])
        y_prompt[2 * c] = y[0:sp]
        y_prompt[2 * c + 1] = y[sp:2 * sp]
        y_sample[c] = y[2 * sp:2 * sp + ss]
    return (y_prompt, y_sample)
```
